# Optimizing a Trainium2 kernel written in Bass

```python
import math
import jax, jax.numpy as jnp
from jax import lax
import numpy as np

D_MODEL = 1024
BATCH = 8
SEQ = 2048
DEPTH = 2

N_MIXERS = 2
N_MLA = (DEPTH + 1) // 2
N_SSM = DEPTH // 2
MLA_HEADS = 8
QK_NOPE = 128
QK_ROPE = 64
V_DIM = 128
Q_LORA = 256
KV_LORA = 128
ROPE_THETA = 10000.0
Q_BLOCK = 128
SSM_WIDTH = D_MODEL
GROUP_CH = 16
N_GROUPS = SSM_WIDTH // GROUP_CH
STATE = 64
DT_MIN = 0.001
DT_MAX = 0.1
D_FF = 4 * D_MODEL
ALPHA = (2 * DEPTH) ** 0.25
BETA = (8 * DEPTH) ** -0.25
LN_EPS = 1e-5
RMS_EPS = 1e-6

kernel_name = "hybrid_mla_s5_deepnorm_adaln"


def layer_norm(x, g, b):
    xf = x.astype(jnp.float32)
    mu = jnp.mean(xf, axis=-1, keepdims=True)
    var = jnp.mean(jnp.square(xf - mu), axis=-1, keepdims=True)
    y = (xf - mu) * lax.rsqrt(var + LN_EPS) * g.astype(jnp.float32) + b.astype(jnp.float32)
    return y.astype(x.dtype)


def rms_norm(x, g):
    xf = x.astype(jnp.float32)
    y = xf * lax.rsqrt(jnp.mean(jnp.square(xf), axis=-1, keepdims=True) + RMS_EPS) * g.astype(jnp.float32)
    return y.astype(x.dtype)


def rope_tables(positions):
    inv_freq = ROPE_THETA ** (-jnp.arange(0, QK_ROPE, 2, dtype=jnp.float32) / QK_ROPE)
    ang = positions.astype(jnp.float32)[..., None] * inv_freq
    return jnp.cos(ang), jnp.sin(ang)


def apply_rope(x, cos, sin):
    xf = x.astype(jnp.float32)
    x1, x2 = jnp.split(xf, 2, axis=-1)
    return jnp.concatenate([x1 * cos - x2 * sin, x1 * sin + x2 * cos], axis=-1).astype(x.dtype)


def mla_mixer(h, cos, sin, w_in, q_norm, w_qb, kv_norm, w_kvb, w_o):
    B, L, _ = h.shape
    z = h @ w_in
    cq = rms_norm(z[..., :Q_LORA], q_norm)
    ckv = rms_norm(z[..., Q_LORA:Q_LORA + KV_LORA], kv_norm)
    k_pe = apply_rope(z[..., Q_LORA + KV_LORA:], cos, sin)
    q = (cq @ w_qb).reshape(B, L, MLA_HEADS, QK_NOPE + QK_ROPE)
    q_nope = q[..., :QK_NOPE]
    q_pe = apply_rope(q[..., QK_NOPE:], cos[:, :, None, :], sin[:, :, None, :])
    kv = (ckv @ w_kvb).reshape(B, L, MLA_HEADS, QK_NOPE + V_DIM)
    k_nope = kv[..., :QK_NOPE]
    v = kv[..., QK_NOPE:]
    scale = 1.0 / math.sqrt(QK_NOPE + QK_ROPE)
    nb = L // Q_BLOCK
    qn_b = q_nope.reshape(B, nb, Q_BLOCK, MLA_HEADS, QK_NOPE).transpose(1, 0, 2, 3, 4)
    qp_b = q_pe.reshape(B, nb, Q_BLOCK, MLA_HEADS, QK_ROPE).transpose(1, 0, 2, 3, 4)
    k_idx = jnp.arange(L)

    def attend(args):
        qn, qp, start = args
        s = (jnp.einsum('bqhd,bkhd->bhqk', qn, k_nope)
             + jnp.einsum('bqhr,bkr->bhqk', qp, k_pe)).astype(jnp.float32) * scale
        q_idx = start + jnp.arange(Q_BLOCK)
        causal = k_idx[None, :] <= q_idx[:, None]
        s = jnp.where(causal, s, jnp.float32(-1e30))
        p = jax.nn.softmax(s, axis=-1).astype(v.dtype)
        return jnp.einsum('bhqk,bkhd->bqhd', p, v)

    o = lax.map(attend, (qn_b, qp_b, jnp.arange(nb) * Q_BLOCK))
    o = o.transpose(1, 0, 2, 3, 4).reshape(B, L, MLA_HEADS * V_DIM)
    return o @ w_o


def s5_mixer(h, w_in, log_dt, a_re, a_im, b_re, b_im, c_re, c_im, d_skip, w_glu, b_glu, w_out):
    B, L, _ = h.shape
    u = h @ w_in
    u32 = u.astype(jnp.float32).reshape(B, L, N_GROUPS, GROUP_CH)
    lr = a_re.astype(jnp.float32)
    li = a_im.astype(jnp.float32)
    dt = jnp.exp(log_dt.astype(jnp.float32))[:, None]
    mag = jnp.exp(lr * dt)
    ab_re = mag * jnp.cos(li * dt)
    ab_im = mag * jnp.sin(li * dt)
    den = lr * lr + li * li
    nr = ab_re - 1.0
    coef_re = ((nr * lr + ab_im * li) / den)[..., None]
    coef_im = ((ab_im * lr - nr * li) / den)[..., None]
    br = b_re.astype(jnp.float32)
    bi = b_im.astype(jnp.float32)
    bb_re = coef_re * br - coef_im * bi
    bb_im = coef_re * bi + coef_im * br
    bu_re = jnp.einsum('blgc,gpc->blgp', u32, bb_re)
    bu_im = jnp.einsum('blgc,gpc->blgp', u32, bb_im)
    a_re_t = jnp.broadcast_to(ab_re, (1, L, N_GROUPS, STATE))
    a_im_t = jnp.broadcast_to(ab_im, (1, L, N_GROUPS, STATE))

    def combine(e1, e2):
        a1r, a1i, b1r, b1i = e1
        a2r, a2i, b2r, b2i = e2
        return (a2r * a1r - a2i * a1i,
                a2r * a1i + a2i * a1r,
                a2r * b1r - a2i * b1i + b2r,
                a2r * b1i + a2i * b1r + b2i)

    _, _, xr, xi = lax.associative_scan(combine, (a_re_t, a_im_t, bu_re, bu_im), axis=1)
    y = (jnp.einsum('blgp,gcp->blgc', xr, c_re.astype(jnp.float32))
         - jnp.einsum('blgp,gcp->blgc', xi, c_im.astype(jnp.float32)))
    y = y + d_skip.astype(jnp.float32).reshape(N_GROUPS, GROUP_CH) * u32
    y = y.reshape(B, L, SSM_WIDTH).astype(h.dtype)
    g = jax.nn.gelu(y)
    z = g * jax.nn.sigmoid(g @ w_glu + b_glu)
    return z @ w_out


def sq_relu_mlp(h, w1, b1, w2, b2):
    a = jax.nn.relu(h @ w1 + b1)
    return (a * a) @ w2 + b2


def modulation(cs, w, b):
    m = cs @ w + b
    shift, scale, gate = jnp.split(m, 3, axis=-1)
    return shift[:, None, :], scale[:, None, :], gate[:, None, :]


def setup_inputs(seed: int = 0) -> dict:
    key = jax.random.key(seed)
    ks = iter(jax.random.split(key, 40))
    f32 = jnp.float32
    nrm = lambda shape, s: jax.random.normal(next(ks), shape, f32) * s
    D = D_MODEL
    x = jax.random.normal(next(ks), (BATCH, SEQ, D), f32)
    c = jax.random.normal(next(ks), (BATCH, D), f32)
    offs = jax.random.randint(next(ks), (BATCH, 1), 0, 1024, dtype=jnp.int32)
    positions = offs + jnp.arange(SEQ, dtype=jnp.int32)[None, :]
    mla_w_in = nrm((N_MLA, D, Q_LORA + KV_LORA + QK_ROPE), D ** -0.5)
    mla_q_norm = 1.0 + nrm((N_MLA, Q_LORA), 0.02)
    mla_w_qb = nrm((N_MLA, Q_LORA, MLA_HEADS * (QK_NOPE + QK_ROPE)), Q_LORA ** -0.5)
    mla_kv_norm = 1.0 + nrm((N_MLA, KV_LORA), 0.02)
    mla_w_kvb = nrm((N_MLA, KV_LORA, MLA_HEADS * (QK_NOPE + V_DIM)), KV_LORA ** -0.5)
    mla_w_o = nrm((N_MLA, MLA_HEADS * V_DIM, D), BETA * (MLA_HEADS * V_DIM) ** -0.5)
    ssm_w_in = nrm((N_SSM, D, SSM_WIDTH), D ** -0.5)
    ssm_log_dt = jax.random.uniform(next(ks), (N_SSM, N_GROUPS), f32,
                                    math.log(DT_MIN), math.log(DT_MAX))
    n_idx = jnp.arange(STATE, dtype=f32)
    ssm_a_re = -0.5 + nrm((N_SSM, N_GROUPS, STATE), 0.01)
    ssm_a_im = math.pi * n_idx + nrm((N_SSM, N_GROUPS, STATE), 0.01)
    ssm_b_re = nrm((N_SSM, N_GROUPS, STATE, GROUP_CH), (2 * GROUP_CH) ** -0.5)
    ssm_b_im = nrm((N_SSM, N_GROUPS, STATE, GROUP_CH), (2 * GROUP_CH) ** -0.5)
    ssm_c_re = nrm((N_SSM, N_GROUPS, GROUP_CH, STATE), (2 * STATE) ** -0.5)
    ssm_c_im = nrm((N_SSM, N_GROUPS, GROUP_CH, STATE), (2 * STATE) ** -0.5)
    ssm_d = nrm((N_SSM, SSM_WIDTH), 1.0)
    ssm_w_glu = nrm((N_SSM, SSM_WIDTH, SSM_WIDTH), SSM_WIDTH ** -0.5)
    ssm_b_glu = nrm((N_SSM, SSM_WIDTH), 0.01)
    ssm_w_out = nrm((N_SSM, SSM_WIDTH, D), BETA * SSM_WIDTH ** -0.5)
    mlp_w1 = nrm((DEPTH, D, D_FF), D ** -0.5)
    mlp_b1 = nrm((DEPTH, D_FF), 0.01)
    mlp_w2 = nrm((DEPTH, D_FF, D), BETA * D_FF ** -0.5)
    mlp_b2 = nrm((DEPTH, D), 0.01)
    mod_mix_w = nrm((DEPTH, D, 3 * D), 0.2 * D ** -0.5)
    mod_mix_b = nrm((DEPTH, 3 * D), 0.01)
    mod_ffn_w = nrm((DEPTH, D, 3 * D), 0.2 * D ** -0.5)
    mod_ffn_b = nrm((DEPTH, 3 * D), 0.01)
    ln_mix_g = 1.0 + nrm((DEPTH, D), 0.02)
    ln_mix_b = nrm((DEPTH, D), 0.01)
    ln_ffn_g = 1.0 + nrm((DEPTH, D), 0.02)
    ln_ffn_b = nrm((DEPTH, D), 0.01)
    return {"x": x, "c": c, "positions": positions,
            "mla_w_in": mla_w_in, "mla_q_norm": mla_q_norm, "mla_w_qb": mla_w_qb,
            "mla_kv_norm": mla_kv_norm, "mla_w_kvb": mla_w_kvb, "mla_w_o": mla_w_o,
            "ssm_w_in": ssm_w_in, "ssm_log_dt": ssm_log_dt, "ssm_a_re": ssm_a_re,
            "ssm_a_im": ssm_a_im, "ssm_b_re": ssm_b_re, "ssm_b_im": ssm_b_im,
            "ssm_c_re": ssm_c_re, "ssm_c_im": ssm_c_im, "ssm_d": ssm_d,
            "ssm_w_glu": ssm_w_glu, "ssm_b_glu": ssm_b_glu, "ssm_w_out": ssm_w_out,
            "mlp_w1": mlp_w1, "mlp_b1": mlp_b1, "mlp_w2": mlp_w2, "mlp_b2": mlp_b2,
            "mod_mix_w": mod_mix_w, "mod_mix_b": mod_mix_b,
            "mod_ffn_w": mod_ffn_w, "mod_ffn_b": mod_ffn_b,
            "ln_mix_g": ln_mix_g, "ln_mix_b": ln_mix_b,
            "ln_ffn_g": ln_ffn_g, "ln_ffn_b": ln_ffn_b}


def reference(x, c, positions,
              mla_w_in, mla_q_norm, mla_w_qb, mla_kv_norm, mla_w_kvb, mla_w_o,
              ssm_w_in, ssm_log_dt, ssm_a_re, ssm_a_im, ssm_b_re, ssm_b_im,
              ssm_c_re, ssm_c_im, ssm_d, ssm_w_glu, ssm_b_glu, ssm_w_out,
              mlp_w1, mlp_b1, mlp_w2, mlp_b2,
              mod_mix_w, mod_mix_b, mod_ffn_w, mod_ffn_b,
              ln_mix_g, ln_mix_b, ln_ffn_g, ln_ffn_b):
    cs = jax.nn.silu(c)
    cos, sin = rope_tables(positions)
    for i in range(DEPTH):
        shift, scale, gate = modulation(cs, mod_mix_w[i], mod_mix_b[i])
        h = x * (1.0 + scale) + shift
        j = i // N_MIXERS
        if i % N_MIXERS == 0:
            y = mla_mixer(h, cos, sin, mla_w_in[j], mla_q_norm[j], mla_w_qb[j],
                          mla_kv_norm[j], mla_w_kvb[j], mla_w_o[j])
        else:
            y = s5_mixer(h, ssm_w_in[j], ssm_log_dt[j], ssm_a_re[j], ssm_a_im[j],
                         ssm_b_re[j], ssm_b_im[j], ssm_c_re[j], ssm_c_im[j], ssm_d[j],
                         ssm_w_glu[j], ssm_b_glu[j], ssm_w_out[j])
        x = layer_norm(ALPHA * x + (1.0 + gate) * y, ln_mix_g[i], ln_mix_b[i])
        shift, scale, gate = modulation(cs, mod_ffn_w[i], mod_ffn_b[i])
        h = x * (1.0 + scale) + shift
        y = sq_relu_mlp(h, mlp_w1[i], mlp_b1[i], mlp_w2[i], mlp_b2[i])
        x = layer_norm(ALPHA * x + (1.0 + gate) * y, ln_ffn_g[i], ln_ffn_b[i])
    return x
```

```python
import contextlib
import math
import numpy as np
import concourse.bass as bass
import concourse.mybir as mybir
from concourse.bass_utils import run_bass_kernel_spmd

F32 = mybir.dt.float32
BF16 = mybir.dt.bfloat16
I32 = mybir.dt.int32
ALU = mybir.AluOpType
AF = mybir.ActivationFunctionType

D = 1024
L = 2048
NT = L // 128
NB = L // 512
DFF = 4096
ALPHA = 4 ** 0.25
LN_EPS = 1e-5
RMS_EPS = 1e-6
PI = math.pi


class Buf:
    __slots__ = ("name", "w", "r", "dsem", "dcnt")

    def __init__(self, name):
        self.name = name
        self.w = {}
        self.r = {}
        self.dsem = None
        self.dcnt = 0


class Sched:
    ENG = ("pe", "act", "dve", "pool", "sp")
    EMAP = {"pe": "tensor", "act": "scalar", "dve": "vector", "pool": "gpsimd", "sp": "sync"}

    def __init__(self, nc, stack, same_engine_sync=True):
        self.nc = nc
        self.stack = stack
        self.prog = {e: [] for e in self.ENG}
        self.sems = {}
        self.cnt = {e: 0 for e in self.ENG}
        self.waited = {e: {} for e in self.ENG}
        self.same_engine_sync = same_engine_sync
        for e in self.ENG:
            self.sems[e] = stack.enter_context(nc.semaphore("s_" + e))
        self.dsem_cnt = {}
        self.dsem_free = []
        self.swdge_sems = set()
        self.out_tokens = []

    def _get_dsem(self, fresh=False):
        if self.dsem_free and not fresh:
            return self.dsem_free.pop()
        key = "d%d" % len(self.dsem_cnt)
        self.sems[key] = self.stack.enter_context(self.nc.semaphore(key))
        self.dsem_cnt[key] = 0
        return key

    def release(self, bufs):
        for b in bufs:
            if b.dsem is not None:
                if b.dsem not in self.swdge_sems:
                    self.dsem_free.append(b.dsem)
                b.dsem = None

    def _deps(self, eng, reads, writes):
        need = {}
        for b in reads:
            for k, v in b.w.items():
                if need.get(k, 0) < v:
                    need[k] = v
        for b in writes:
            for k, v in b.w.items():
                if need.get(k, 0) < v:
                    need[k] = v
            for k, v in b.r.items():
                if need.get(k, 0) < v:
                    need[k] = v
        wd = self.waited[eng]
        for k, v in need.items():
            if k == eng and (eng == "pe" or not self.same_engine_sync):
                continue
            if wd.get(k, 0) >= v:
                continue
            wd[k] = v
            self.prog[eng].append(("wait", k, v))

    def op(self, eng, fn, reads=(), writes=(), signal=True):
        self._deps(eng, reads, writes)
        tok = self.cnt[eng] + 1
        if signal:
            self.cnt[eng] = tok
            self.prog[eng].append(("op", fn, eng, 1))
        else:
            self.prog[eng].append(("op", fn, None, 0))
        for b in reads:
            if b.r.get(eng, 0) < tok:
                b.r[eng] = tok
        for b in writes:
            if b.w.get(eng, 0) < tok:
                b.w[eng] = tok

    def dma(self, eng, out, in_, wbuf=None, rbuf=None, is_output=False):
        reads = [rbuf] if rbuf is not None else []
        writes = [wbuf] if wbuf is not None else []
        self._deps(eng, reads, writes)
        own = wbuf if wbuf is not None else rbuf
        if own.dsem is None:
            own.dsem = self._get_dsem(fresh=(eng == "pool"))
            if eng == "pool":
                self.swdge_sems.add(own.dsem)
        key = own.dsem
        self.dsem_cnt[key] += 16
        val = self.dsem_cnt[key]

        def fn(e, out=out, in_=in_):
            return e.dma_start(out=out, in_=in_)
        self.prog[eng].append(("op", fn, key, 16))
        if wbuf is not None:
            wbuf.w[key] = val
        if rbuf is not None:
            rbuf.r[key] = val
        if is_output:
            self.out_tokens.append((key, val))


    def mm(self, out, lhsT, rhs, start, stop, reads, writes, signal=None):
        if signal is None:
            signal = stop
        self.op("pe", lambda e, o=out, l=lhsT, r=rhs, s=start, t=stop: e.matmul(o, lhsT=l, rhs=r, start=s, stop=t),
                reads, writes, signal)

    def tr(self, out, in_, ident, reads, writes, signal=True):
        self.op("pe", lambda e, o=out, i=in_, d=ident: e.transpose(out=o, in_=i, identity=d), reads, writes, signal)

    def tt(self, eng, out, in0, in1, op, reads, writes):
        self.op(eng, lambda e, o=out, a=in0, b=in1, p=op: e.tensor_tensor(out=o, in0=a, in1=b, op=p), reads, writes)

    def ts(self, eng, out, in0, s1, s2, op0, op1, reads, writes):
        if s2 is None:
            self.op(eng, lambda e, o=out, a=in0, x=s1, p=op0: e.tensor_scalar(out=o, in0=a, scalar1=x, scalar2=None, op0=p), reads, writes)
        else:
            self.op(eng, lambda e, o=out, a=in0, x=s1, y=s2, p=op0, q=op1: e.tensor_scalar(out=o, in0=a, scalar1=x, scalar2=y, op0=p, op1=q), reads, writes)

    def stt(self, eng, out, in0, scalar, in1, op0, op1, reads, writes):
        self.op(eng, lambda e, o=out, a=in0, s=scalar, b=in1, p=op0, q=op1: e.scalar_tensor_tensor(out=o, in0=a, scalar=s, in1=b, op0=p, op1=q), reads, writes)

    def cp(self, eng, out, in_, reads, writes):
        if eng == "act":
            self.op(eng, lambda e, o=out, i=in_: e.copy(out=o, in_=i), reads, writes)
        else:
            self.op(eng, lambda e, o=out, i=in_: e.tensor_copy(out=o, in_=i), reads, writes)

    def actf(self, out, in_, func, reads, writes, bias=None, scale=None):
        if bias is None and scale is None:
            self.op("act", lambda e, o=out, i=in_, f=func: e.activation(out=o, in_=i, func=f), reads, writes)
        elif bias is None:
            self.op("act", lambda e, o=out, i=in_, f=func, s=scale: e.activation(out=o, in_=i, func=f, scale=s), reads, writes)
        else:
            sc = 1.0 if scale is None else scale
            self.op("act", lambda e, o=out, i=in_, f=func, b=bias, s=sc: e.activation(out=o, in_=i, func=f, bias=b, scale=s), reads, writes)

    def memset(self, eng, out, val, reads, writes):
        self.op(eng, lambda e, o=out, v=val: e.memset(o, v), reads, writes)

    def barrier(self):
        for e in self.ENG:
            wd = self.waited[e]
            for f in self.ENG:
                if f == e or self.cnt[f] == 0:
                    continue
                if wd.get(f, 0) < self.cnt[f]:
                    wd[f] = self.cnt[f]
                    self.prog[e].append(("wait", f, self.cnt[f]))
            for k, v in self.dsem_cnt.items():
                if v > 0 and wd.get(k, 0) < v:
                    wd[k] = v
                    self.prog[e].append(("wait", k, v))

    def emit(self):
        nc = self.nc
        with nc.Block() as block:
            for e in self.ENG:
                items = self.prog[e]

                def body(engine, items=items):
                    for it in items:
                        if it[0] == "wait":
                            engine.wait_ge(self.sems[it[1]], it[2])
                        else:
                            ins = it[1](engine)
                            if it[2] is not None:
                                ins.then_inc(self.sems[it[2]], it[3])
                getattr(block, self.EMAP[e])(body)
        self.prog = {e: [] for e in self.ENG}


class Ctx:
    pass


INPUT_SHAPES = {
    "x": ([L, D], F32),
    "c": ([128, 8], F32),
    "pos": ([64, L], I32),
    "invf": ([64, 1], F32),
    "mla_w_in": ([D, 512], F32),
    "mla_qn": ([128, 2], F32),
    "mla_w_qb": ([256, 8 * 320], F32),
    "mla_kvn": ([128, 1], F32),
    "mla_w_kvb": ([128, 2048], F32),
    "mla_w_o": ([1024, 1024], F32),
    "ssm_w_in": ([D, D], F32),
    "ssm_ldt": ([128, 32], F32),
    "ssm_lr": ([128, 32], F32),
    "ssm_li": ([128, 32], F32),
    "ssm_bre": ([128, 32 * 32], F32),
    "ssm_bim": ([128, 32 * 32], F32),
    "ssm_cre": ([128, 32 * 32], F32),
    "ssm_cim": ([128, 32 * 32], F32),
    "ssm_d": ([128, 8], F32),
    "ssm_drep": ([128, 32], F32),
    "ssm_w_glu": ([D, D], F32),
    "ssm_bglu": ([128, 8], F32),
    "ssm_w_out": ([D, D], F32),
    "mlp_w1": ([2, D, DFF], F32),
    "mlp_b1": ([2, 128, 32], F32),
    "mlp_w2": ([2, DFF, D], F32),
    "mlp_b2": ([2, 128, D], F32),
    "mod_w": ([4, D, 3 * D], F32),
    "mod_b": ([4, 128, 3 * D], F32),
    "ln_g": ([4, 128, D], F32),
    "ln_b": ([4, 128, D], F32),
}


def build_nc(sublayers=(0, 1, 2, 3), same_engine_sync=True):
    nc = bass.Bass("TRN2", target_bir_lowering=False)
    I = {k: nc.dram_tensor(k, list(s), d, kind="ExternalInput").ap() for k, (s, d) in INPUT_SHAPES.items()}
    out = nc.dram_tensor("out", [L, D], F32, kind="ExternalOutput").ap()

    with contextlib.ExitStack() as st:
        S = Sched(nc, st, same_engine_sync=same_engine_sync)
        g = Ctx()
        g.nc, g.S, g.I, g.out = nc, S, I, out

        g.uid = 0

        def sb(stack, name, shape, dt):
            g.uid += 1
            return stack.enter_context(nc.sbuf_tensor("%s_%d" % (name, g.uid), list(shape), dt))
        g.sb = sb

        g.X = sb(st, "X", [128, NT, D], F32)
        g.XB = [Buf("X%d" % t) for t in range(NT)]
        g.HT = sb(st, "HT", [128, 8, L], BF16)
        g.HTB = [Buf("HT%d" % t) for t in range(NT)]
        g.MOD = sb(st, "MOD", [128, 3, D], F32)
        g.MODB = Buf("MOD")
        g.ident = sb(st, "ident", [128, 128], BF16)
        g.identf = sb(st, "identf", [128, 128], F32)
        g.ones = sb(st, "ones", [128, 128], F32)
        g.onesb = sb(st, "onesb", [128, 128], BF16)
        g.CS = sb(st, "CS", [128, 8], F32)
        g.cst = sb(st, "cst", [128, 8], F32)
        g.CB = Buf("consts")
        g.CSB = Buf("CS")
        g.LNS = sb(st, "LNS", [128, 4, 16], F32)
        g.LNSB = [Buf("LNS%d" % i) for i in range(4)]
        g.lnctr = 0
        g.PS = [st.enter_context(nc.psum_tensor("psb%d" % i, [128, 512], F32)) for i in range(8)]
        g.PSB = [Buf("psb%d" % i) for i in range(8)]

        S.memset("pool", g.identf[:], 0.0, [], [g.CB])
        S.op("pool", lambda e: e.affine_select(out=g.identf[:], in_=g.identf[:], compare_op=ALU.not_equal, fill=1.0,
                                                base=0, pattern=[[-1, 128]], channel_multiplier=1),
             reads=[g.CB], writes=[g.CB])
        S.memset("pool", g.ones[:], 1.0, [], [g.CB])
        S.memset("pool", g.onesb[:], 1.0, [], [g.CB])
        for col, val in enumerate([-PI, PI, 1.0, LN_EPS, RMS_EPS, 0.0]):
            S.memset("pool", g.cst[:, col:col + 1], val, [], [g.CB])
        S.cp("dve", g.ident[:], g.identf[:], [g.CB], [g.CB])
        xin = I["x"].rearrange("(t p) d -> p t d", p=128)
        for t in range(NT):
            S.dma("sp", g.X[:, t, :], xin[:, t, :], wbuf=g.XB[t])
        S.dma("sp", g.CS[:], I["c"], wbuf=g.CSB)
        S.actf(g.CS[:], g.CS[:], AF.Silu, [g.CSB], [g.CSB])
        g.CSBC = sb(st, "CSBC", [128, 8, 128], F32)
        for k in range(8):
            S.cp("dve", g.CSBC[:, k, :], g.CS[:, k:k + 1].to_broadcast([128, 128]), [g.CSB], [g.CSB])

        for sl in sublayers:
            g.sl = sl
            prologue(g, sl)
            if sl == 0:
                mla_phase(g)
            elif sl == 2:
                s5_phase(g)
            else:
                ffn_phase(g, sl // 2, sl)

        oap = out.rearrange("(t p) d -> p t d", p=128)
        for t in range(NT):
            S.dma("sp", oap[:, t, :], g.X[:, t, :], rbuf=g.XB[t], is_output=True)
        S.barrier()
        S.emit()
    return nc


def sin_reduced(g, out, ang, shift, T, TI, M, reads, writes, tbufs, sign=1.0):
    S = g.S
    rw = (list(reads) + list(tbufs), list(tbufs))
    S.ts("dve", T, ang, shift, 1.0 / (2 * PI), ALU.add, ALU.mult, *rw)
    S.cp("dve", TI, T, *rw)
    S.cp("dve", M, TI, *rw)
    S.tt("dve", T, T, M, ALU.subtract, *rw)
    S.ts("dve", M, T, 0.5, None, ALU.is_gt, None, *rw)
    S.tt("dve", T, T, M, ALU.subtract, *rw)
    S.ts("dve", M, T, -0.5, None, ALU.is_lt, None, *rw)
    S.tt("dve", T, T, M, ALU.add, *rw)
    S.actf(out, T, AF.Sin, list(tbufs), list(writes), scale=sign * 2 * PI)


def prologue(g, sl):
    S, I = g.S, g.I
    with contextlib.ExitStack() as ph:
        WST = [g.sb(ph, "WST%d" % i, [128, 3 * D], F32) for i in range(2)]
        WSTB = [Buf("WST%d" % i) for i in range(2)]
        BROW = g.sb(ph, "BROW", [128, 3 * D], F32)
        BROWB = Buf("BROW")
        TMP = [g.sb(ph, "HTMP%d" % i, [128, D], F32) for i in range(2)]
        TMPB = [Buf("HTMP%d" % i) for i in range(2)]
        HB = [g.sb(ph, "HB%d" % i, [128, D], BF16) for i in range(2)]
        HBB = [Buf("HB%d" % i) for i in range(2)]
        local = WSTB + [BROWB] + TMPB + HBB

        wv = I["mod_w"][sl].rearrange("(k p) n -> p k n", p=128)
        S.dma("sp", BROW[:], I["mod_b"][sl], wbuf=BROWB)
        for k in range(8):
            w = WST[k % 2]
            S.dma("sp", w[:], wv[:, k, :], wbuf=WSTB[k % 2])
            for n in range(6):
                S.mm(g.PS[n][:], g.CSBC[:, k, :], w[:, n * 512:(n + 1) * 512], k == 0, k == 7, [WSTB[k % 2], g.CSB], [g.PSB[n]], signal=(n == 5 or k == 7))
        for n in range(6):
            j, hh = n // 2, n % 2
            addc = 0.0 if j == 0 else 1.0
            S.stt("dve", g.MOD[:, j, hh * 512:(hh + 1) * 512], g.PS[n][:], addc, BROW[:, n * 512:(n + 1) * 512], ALU.add, ALU.add,
                  [g.PSB[n], BROWB], [g.MODB])
        for t in range(NT):
            tm, tmb = TMP[t % 2], TMPB[t % 2]
            hb, hbb = HB[t % 2], HBB[t % 2]
            S.tt("dve", tm[:], g.X[:, t, :], g.MOD[:, 1, :], ALU.mult, [g.XB[t], g.MODB], [tmb])
            S.tt("dve", hb[:], tm[:], g.MOD[:, 0, :], ALU.add, [tmb, g.MODB], [hbb])
            pbi = 2 + (t % 2)
            pst = g.PS[pbi][:].bitcast(BF16)
            for k in range(8):
                S.tr(pst[:, k * 128:(k + 1) * 128], hb[:, k * 128:(k + 1) * 128], g.ident[:], [hbb, g.CB], [g.PSB[pbi]], signal=(k == 7))
            S.cp("act", g.HT[:, :, t * 128:(t + 1) * 128], pst.rearrange("p (k n) -> p k n", k=8), [g.PSB[pbi]], [g.HTB[t]])
        S.barrier()
        S.emit()
        S.release(local)


def load_ln(g, ph):
    g.LNG = g.sb(ph, "LNG", [128, D], F32)
    g.LNGB = Buf("LNG")
    g.LNBt = g.sb(ph, "LNBt", [128, D], F32)
    g.LNBB = Buf("LNB")
    g.S.dma("sp", g.LNG[:], g.I["ln_g"][g.sl], wbuf=g.LNGB)
    g.S.dma("sp", g.LNBt[:], g.I["ln_b"][g.sl], wbuf=g.LNBB)


def layer_norm_tile(g, t, eng2="pool"):
    S = g.S
    i = g.lnctr % 4
    g.lnctr += 1
    st_ = g.LNS[:, i, :]
    sbuf = g.LNSB[i]
    xt = g.X[:, t, :]
    for c in range(2):
        S.op("dve", lambda e, o=st_[:, c * 6:(c + 1) * 6], i_=xt[:, c * 512:(c + 1) * 512]: e.bn_stats(out=o, in_=i_),
             reads=[g.XB[t]], writes=[sbuf])
    S.op("dve", lambda e, o=st_[:, 12:14], i_=st_[:, 0:12].rearrange("p (c s) -> p c s", c=2): e.bn_aggr(out=o, in_=i_), reads=[sbuf], writes=[sbuf])
    S.actf(st_[:, 14:15], st_[:, 13:14], AF.Sqrt, [sbuf, g.CB], [sbuf], bias=g.cst[:, 3:4], scale=1.0)
    LNG, LNGB, LNBt, LNBB = g.LNG, g.LNGB, g.LNBt, g.LNBB

    def apply():
        S.op("dve", lambda e, o=st_[:, 14:15]: e.reciprocal(out=o, in_=o), reads=[sbuf], writes=[sbuf])
        S.stt("dve", xt, xt, st_[:, 12:13], LNG[:], ALU.subtract, ALU.mult, [sbuf, g.XB[t], LNGB], [g.XB[t]])
        S.stt("dve", xt, xt, st_[:, 14:15], LNBt[:], ALU.mult, ALU.add, [sbuf, g.XB[t], LNBB], [g.XB[t]])
    ln_flush(g)
    g.ln_pending = apply


def ln_flush(g):
    p = getattr(g, "ln_pending", None)
    if p is not None:
        g.ln_pending = None
        p()


def mixer_epilogue_tile(g, t, pbanks, TMP=None, TMPB=None):
    S = g.S
    for hh in range(2):
        xs = g.X[:, t, hh * 512:(hh + 1) * 512]
        S.stt("dve", xs, xs, ALPHA, g.PS[pbanks[hh]][:], ALU.mult, ALU.add, [g.XB[t], g.PSB[pbanks[hh]]], [g.XB[t]])
    layer_norm_tile(g, t)


def ffn_phase(g, li, sl):
    S, I = g.S, g.I
    with contextlib.ExitStack() as ph:
        W1 = [g.sb(ph, "W1c%d" % i, [128, 8, 512], BF16) for i in range(2)]
        W1B = [Buf("W1c%d" % i) for i in range(2)]
        W2 = [g.sb(ph, "W2c%d" % i, [128, 4, D], BF16) for i in range(2)]
        W2B = [Buf("W2c%d" % i) for i in range(2)]
        AT = [g.sb(ph, "AT%d" % i, [128, 4, L], BF16) for i in range(2)]
        ATB = [[Buf("AT%d_%d" % (i, n)) for n in range(NB)] for i in range(2)]
        RT = [g.sb(ph, "RT%d" % i, [128, 512], F32) for i in range(2)]
        RTB = [Buf("RT%d" % i) for i in range(2)]
        B1 = g.sb(ph, "B1", [128, 32], F32)
        B1B = Buf("B1")
        B2R = g.sb(ph, "B2R", [128, D], F32)
        B2B = Buf("B2R")
        load_ln(g, ph)
        local = W1B + W2B + RTB + [B1B, B2B] + [b for r in ATB for b in r]
        local_ln = True

        S.dma("sp", B1[:], I["mlp_b1"][li], wbuf=B1B)
        S.dma("sp", B2R[:], I["mlp_b2"][li], wbuf=B2B)
        S.tt("dve", B2R[:], B2R[:], g.MOD[:, 2, :], ALU.mult, [B2B, g.MODB], [B2B])
        for t in range(NT):
            S.stt("dve", g.X[:, t, :], g.X[:, t, :], ALPHA, B2R[:], ALU.mult, ALU.add, [g.XB[t], B2B], [g.XB[t]])
        w1v = I["mlp_w1"][li].rearrange("(k p) n -> p k n", p=128)
        w2v = I["mlp_w2"][li].rearrange("(j p) n -> p j n", p=128)
        rtc = 0
        for c in range(8):
            pc = c % 2
            S.dma("pool", W1[pc][:], w1v[:, :, c * 512:(c + 1) * 512], wbuf=W1B[pc])
            S.dma("pool", W2[pc][:], w2v[:, c * 4:(c + 1) * 4, :], wbuf=W2B[pc])
            for j in range(4):
                S.tt("dve", W2[pc][:, j, :], W2[pc][:, j, :], g.MOD[:, 2, :], ALU.mult, [W2B[pc], g.MODB], [W2B[pc]])
            for n in range(NB):
                for m in range(4):
                    pb = (n * 4 + m) % 2
                    for k in range(8):
                        S.mm(g.PS[pb][:], W1[pc][:, k, m * 128:(m + 1) * 128], g.HT[:, k, n * 512:(n + 1) * 512], k == 0, k == 7,
                             [W1B[pc]] + g.HTB[n * 4:(n + 1) * 4], [g.PSB[pb]])
                    rt, rtb = RT[rtc % 2], RTB[rtc % 2]
                    rtc += 1
                    col = c * 4 + m
                    S.actf(rt[:], g.PS[pb][:], AF.Relu, [g.PSB[pb], B1B], [rtb], bias=B1[:, col:col + 1], scale=1.0)
                    S.actf(AT[pc][:, m, n * 512:(n + 1) * 512], rt[:], AF.Square, [rtb], [ATB[pc][n]])
            for t in range(NT):
                pbs = (2 + 2 * (t % 2), 3 + 2 * (t % 2))
                for hh in range(2):
                    for j in range(4):
                        S.mm(g.PS[pbs[hh]][:], AT[pc][:, j, t * 128:(t + 1) * 128], W2[pc][:, j, hh * 512:(hh + 1) * 512], j == 0, j == 3,
                             [ATB[pc][t // 4], W2B[pc]], [g.PSB[pbs[hh]]])
                for hh in range(2):
                    S.tt("dve", g.X[:, t, hh * 512:(hh + 1) * 512], g.PS[pbs[hh]][:], g.X[:, t, hh * 512:(hh + 1) * 512], ALU.add,
                         [g.PSB[pbs[hh]], g.XB[t]], [g.XB[t]])
                if c == 7:
                    layer_norm_tile(g, t)
        ln_flush(g)
        S.barrier()
        S.emit()
        S.release(local + [g.LNGB, g.LNBB])


def mla_phase(g):
    S, I = g.S, g.I
    SCALE = 1.0 / math.sqrt(192.0)
    with contextlib.ExitStack() as ph:
        sb = lambda name, shape, dt: g.sb(ph, name, shape, dt)
        WIN = sb("WIN", [128, 8, 512], BF16); WINB = Buf("WIN")
        WQB = sb("WQB", [128, 2, 2560], BF16); WQBB = Buf("WQB")
        WKV = sb("WKV", [128, 2048], BF16); WKVB = Buf("WKV")
        QN = sb("QN", [128, 2], F32); QNB = Buf("QN")
        KVN = sb("KVN", [128, 1], F32); KVNB = Buf("KVN")
        CQT = sb("CQT", [128, 2, L], BF16); CQB = [Buf("CQ%d" % n) for n in range(NB)]
        CKT = sb("CKT", [128, L], BF16); CKB = [Buf("CK%d" % n) for n in range(NB)]
        KPT = sb("KPT", [64, L], BF16); KPB = [Buf("KP%d" % n) for n in range(NB)]
        C2 = sb("C2", [64, L], F32); S2 = sb("S2", [64, L], F32); TABB = Buf("ropetab")
        INVF = sb("INVF", [64, 1], F32)
        QNT = sb("QNT", [128, L], BF16); QNTB = [Buf("QNT%d" % n) for n in range(NB)]
        QPT = sb("QPT", [64, L], BF16); QPTB = [Buf("QPT%d" % n) for n in range(NB)]
        KNT = sb("KNT", [128, L], BF16); KNTB = [Buf("KNT%d" % n) for n in range(NB)]
        VT = sb("VT", [128, NT, 128], BF16); VTB = [Buf("VT%d" % n) for n in range(NB)]
        ZF = sb("ZF", [128, 2, 512], F32); ZFB = Buf("ZF")
        SQ = sb("SQ", [128, 512], F32); SQB = Buf("SQ")
        RS = sb("RS", [128, 512], F32); RSB = Buf("RS")
        R1 = SQ[0:64, :]; R1B = SQB
        R2 = RS[0:64, :]; R2B = RSB
        PT = [sb("PT%d" % i, [128, 512], BF16) for i in range(5)]; PTB = [Buf("PT%d" % i) for i in range(5)]
        MASK = sb("MASK", [128, 4, 512], BF16); MASKB = Buf("MASK")
        local = [WINB, WQBB, WKVB, QNB, KVNB, TABB, MASKB]

        S.dma("pool", WIN[:], I["mla_w_in"].rearrange("(k p) n -> p k n", p=128), wbuf=WINB)
        S.dma("pool", WQB[:], I["mla_w_qb"].rearrange("(k p) n -> p k n", p=128), wbuf=WQBB)
        S.dma("pool", WKV[:], I["mla_w_kvb"], wbuf=WKVB)
        S.dma("sp", QN[:], I["mla_qn"], wbuf=QNB)
        S.dma("sp", KVN[:], I["mla_kvn"], wbuf=KVNB)
        PI32 = sb("PI32", [64, 512], I32); PI32B = Buf("PI32")
        local.append(PI32B)
        S.dma("sp", INVF[:], I["invf"], wbuf=TABB)
        for k in range(2):
            S.ts("dve", WQB[:, k, :], WQB[:, k, :], QN[:, k:k + 1], None, ALU.mult, None, [WQBB, QNB], [WQBB])
        S.ts("dve", WKV[:, 0:1024], WKV[:, 0:1024], KVN[:, 0:1], None, ALU.mult, None, [WKVB, KVNB], [WKVB])
        S.ts("dve", WKV[:, 1024:2048], WKV[:, 1024:2048], KVN[:, 0:1], None, ALU.mult, None, [WKVB, KVNB], [WKVB])
        for j in range(4):
            S.memset("pool", SQ[:], 1.0, [SQB], [SQB])
            S.op("pool", lambda e, j=j: e.affine_select(out=SQ[:], in_=SQ[:], compare_op=ALU.is_ge, fill=0.0,
                                                        base=-128 * j, pattern=[[1, 512]], channel_multiplier=-1),
                 reads=[SQB], writes=[SQB])
            S.cp("pool", MASK[:, j, :], SQ[:], [SQB], [MASKB])

        def proj_group(pb, lhs_list, rhs_list, reads, M=128):
            nk = len(lhs_list)
            for k in range(nk):
                S.mm(g.PS[pb][0:M, :], lhs_list[k], rhs_list[k], k == 0, k == nk - 1, reads, [g.PSB[pb]])

        def rms_block(srcs_pb, dsts, dstB):
            ntile = len(srcs_pb)
            for i, pb in enumerate(srcs_pb):
                S.cp("act", ZF[:, i, :], g.PS[pb][:], [g.PSB[pb]], [ZFB])
            for i in range(ntile):
                S.actf(SQ[:], ZF[:, i, :], AF.Square, [ZFB], [SQB])
                S.mm(g.PS[6][:], g.ones[:], SQ[:], i == 0, i == ntile - 1, [SQB, g.CB], [g.PSB[6]], signal=True)
            S.actf(RS[:], g.PS[6][:], AF.Sqrt, [g.PSB[6], g.CB], [RSB], bias=g.cst[:, 4:5], scale=1.0 / (128 * ntile))
            S.op("dve", lambda e: e.reciprocal(out=RS[:], in_=RS[:]), reads=[RSB], writes=[RSB])
            for i in range(ntile):
                S.tt("dve", dsts[i], ZF[:, i, :], RS[:], ALU.mult, [ZFB, RSB], [dstB])

        def rope_block(pb_a, pb_b, n, dst, dstB):
            sl_ = slice(n * 512, (n + 1) * 512)
            S.tt("dve", R1, g.PS[pb_a][0:64, :], C2[:, sl_], ALU.mult, [g.PSB[pb_a], TABB], [R1B])
            S.tt("dve", R2, g.PS[pb_b][0:64, :], S2[:, sl_], ALU.mult, [g.PSB[pb_b], TABB], [R2B])
            S.tt("dve", dst[:, sl_], R1, R2, ALU.add, [R1B, R2B], [dstB])

        def rope_tables(n):
            sl_ = slice(n * 512, (n + 1) * 512)
            S.dma("sp", PI32[:], I["pos"][:, sl_], wbuf=PI32B)
            S.cp("dve", C2[:, sl_], PI32[:], [PI32B, TABB], [TABB])
            S.ts("dve", C2[:, sl_], C2[:, sl_], INVF[:, 0:1], None, ALU.mult, None, [TABB], [TABB])
            T_, M_ = ZF[0:64, 0, :], ZF[0:64, 1, :]
            sin_reduced(g, S2[0:32, sl_], C2[0:32, sl_], 0.0, T_[0:32], PI32[0:32, :], M_[0:32], [TABB], [TABB], [ZFB, PI32B], sign=-1.0)
            sin_reduced(g, S2[32:64, sl_], C2[32:64, sl_], 0.0, ZF[32:64, 0, :], PI32[32:64, :], ZF[32:64, 1, :], [TABB], [TABB], [ZFB, PI32B])
            sin_reduced(g, C2[:, sl_], C2[:, sl_], 0.5 * PI, T_, PI32[:], M_, [TABB], [TABB], [ZFB, PI32B])

        for n in range(NB):
            hsl = slice(n * 512, (n + 1) * 512)
            hr = g.HTB[n * 4:(n + 1) * 4]
            rhs = [g.HT[:, k, hsl] for k in range(8)]
            for i in range(2):
                proj_group(i, [WIN[:, k, i * 128:(i + 1) * 128] for k in range(8)], rhs, [WINB] + hr)
            rms_block([0, 1], [CQT[:, 0, hsl], CQT[:, 1, hsl]], CQB[n])
            proj_group(2, [WIN[:, k, 256:384] for k in range(8)], rhs, [WINB] + hr)
            rms_block([2], [CKT[:, hsl]], CKB[n])
            proj_group(3, [WIN[:, k, 384:448] for k in range(8)], rhs, [WINB] + hr, M=64)
            proj_group(4, [WIN[:, k, 448:512] for k in range(8)], rhs, [WINB] + hr, M=64)
            rope_tables(n)
            rope_block(3, 4, n, KPT, KPB[n])

        RSQ = [RS, ZF[:, 0, :]]; RSQB = [RSB, ZFB]
        OT = g.HT
        ptc = 0
        for h in range(8):
            qb = h * 320
            for n in range(NB):
                hsl = slice(n * 512, (n + 1) * 512)
                crhs = [CQT[:, k, hsl] for k in range(2)]
                proj_group(0, [WQB[:, k, qb:qb + 128] for k in range(2)], crhs, [WQBB, CQB[n]])
                S.cp("act", QNT[:, hsl], g.PS[0][:], [g.PSB[0]], [QNTB[n]])
                proj_group(3, [WQB[:, k, qb + 128:qb + 192] for k in range(2)], crhs, [WQBB, CQB[n]], M=64)
                proj_group(4, [WQB[:, k, qb + 192:qb + 256] for k in range(2)], crhs, [WQBB, CQB[n]], M=64)
                rope_block(3, 4, n, QPT, QPTB[n])
                proj_group(1, [WKV[:, h * 128:(h + 1) * 128]], [CKT[:, hsl]], [WKVB, CKB[n]])
                S.cp("act", KNT[:, hsl], g.PS[1][:], [g.PSB[1]], [KNTB[n]])
                for tt in range(4):
                    t = n * 4 + tt
                    S.mm(g.PS[2][:, tt * 128:(tt + 1) * 128], CKT[:, t * 128:(t + 1) * 128], WKV[:, 1024 + h * 128:1024 + (h + 1) * 128], True, True,
                         [WKVB, CKB[n]], [g.PSB[2]], signal=(tt == 3))
                S.cp("act", VT[:, n * 4:(n + 1) * 4, :], g.PS[2][:].rearrange("p (t d) -> p t d", t=4), [g.PSB[2]], [VTB[n]])
            iters = [(qi, kt) for qi in range(NB) for kt in range(4 * (qi + 1))]
            SBK = [3, 4, 0, 1]
            LOOK = 3

            def issue_scores(idx):
                qi, kt = iters[idx]
                spb = SBK[idx % 4]
                ksl = slice(kt * 128, (kt + 1) * 128)
                qsl = slice(qi * 512, (qi + 1) * 512)
                kn = kt // 4
                S.mm(g.PS[spb][:], KNT[:, ksl], QNT[:, qsl], True, False, [KNTB[kn], QNTB[qi]], [g.PSB[spb]], signal=False)
                S.mm(g.PS[spb][:], KPT[:, ksl], QPT[:, qsl], False, True, [KPB[kn], QPTB[qi]], [g.PSB[spb]], signal=True)

            for i_ in range(min(LOOK, len(iters))):
                issue_scores(i_)
            for idx, (qi, kt) in enumerate(iters):
                if idx + LOOK < len(iters):
                    issue_scores(idx + LOOK)
                spb = SBK[idx % 4]
                qsl = slice(qi * 512, (qi + 1) * 512)
                nkt = 4 * (qi + 1)
                kn = kt // 4
                ob = 5 + (qi % 2)
                pt, ptb = PT[ptc % 5], PTB[ptc % 5]
                ptc += 1
                S.actf(pt[:], g.PS[spb][:], AF.Exp, [g.PSB[spb], g.CB], [ptb], bias=g.cst[:, 5:6], scale=SCALE)
                if kt >= 4 * qi:
                    S.tt("dve", pt[:], pt[:], MASK[:, kt - 4 * qi, :], ALU.mult, [ptb, MASKB], [ptb])
                sbk = 7 if qi % 2 == 0 else 2
                S.mm(g.PS[ob][:], VT[:, kt, :], pt[:], kt == 0, kt == nkt - 1, [VTB[kn], ptb], [g.PSB[ob]], signal=False)
                S.mm(g.PS[sbk][:], g.onesb[:], pt[:], kt == 0, kt == nkt - 1, [ptb, g.CB], [g.PSB[sbk]], signal=True)
                if kt == nkt - 1:
                    rs, rsb = RSQ[qi % 2], RSQB[qi % 2]
                    S.cp("act", rs[:], g.PS[sbk][:], [g.PSB[sbk]], [rsb])
                    S.op("dve", lambda e, rs=rs: e.reciprocal(out=rs[:], in_=rs[:]), reads=[rsb], writes=[rsb])
                    S.tt("dve", OT[:, h, qsl], g.PS[ob][:], rs[:], ALU.mult, [g.PSB[ob], rsb], g.HTB[qi * 4:(qi + 1) * 4])
        S.barrier()
        S.emit()
        S.release(local)

    with contextlib.ExitStack() as ph:
        WO = g.sb(ph, "WO", [128, 8, D], BF16); WOB = Buf("WO")
        load_ln(g, ph)
        S.dma("pool", WO[:], I["mla_w_o"].rearrange("(k p) n -> p k n", p=128), wbuf=WOB)
        for k in range(8):
            S.tt("dve", WO[:, k, :], WO[:, k, :], g.MOD[:, 2, :], ALU.mult, [WOB, g.MODB], [WOB])
        for t in range(NT):
            pbs = (2 * (t % 2), 2 * (t % 2) + 1)
            for hh in range(2):
                for k in range(8):
                    S.mm(g.PS[pbs[hh]][:], g.HT[:, k, t * 128:(t + 1) * 128], WO[:, k, hh * 512:(hh + 1) * 512], k == 0, k == 7,
                         [g.HTB[t], WOB], [g.PSB[pbs[hh]]])
            mixer_epilogue_tile(g, t, pbs)
        ln_flush(g)
        S.barrier()
        S.emit()
        S.release([WOB, g.LNGB, g.LNBB])


def s5_phase(g):
    S, I = g.S, g.I
    NSTG = 11
    with contextlib.ExitStack() as s5:
        UT = g.sb(s5, "UT", [128, 8, L], BF16)
        UTB = [Buf("UT%d" % n) for n in range(NB)]
        DCOL = g.sb(s5, "DCOL", [128, 8], F32); BGLU = g.sb(s5, "BGLU", [128, 8], F32); SMB = Buf("s5small")
        DREP = g.sb(s5, "DREP", [128, 32], F32)
        ab = s5.enter_context(contextlib.ExitStack())
        BBR = g.sb(ab, "BBR", [128, 32, 32], F32); BBI = g.sb(ab, "BBI", [128, 32, 32], F32); BBB = Buf("BB")
        CRE = g.sb(ab, "CRE", [128, 32, 32], BF16); CIM = g.sb(ab, "CIM", [128, 32, 32], BF16); CCB = Buf("CC")
        PR = g.sb(ab, "PR", [128, NSTG, 32], F32); PIm = g.sb(ab, "PIm", [128, NSTG, 32], F32); NPI = g.sb(ab, "NPI", [128, NSTG, 32], F32)
        IAT = g.sb(ab, "IAT", [128, 4, 32], F32)
        ER = g.sb(ab, "ER", [128, 32, 16], F32); EI = g.sb(ab, "EI", [128, 32, 16], F32)
        FR = g.sb(ab, "FR", [128, 32, 16], F32); FI = g.sb(ab, "FI", [128, 32, 16], F32)
        POWB = Buf("POW")

        with contextlib.ExitStack() as ph:
            sb = lambda name, shape, dt: g.sb(ph, name, shape, dt)
            WIN = sb("SWIN", [128, 8, D], BF16); WINB = Buf("SWIN")
            BRE = sb("BRE", [128, 32, 32], F32); BIM = sb("BIM", [128, 32, 32], F32); BRB = Buf("BRE")
            PRM = sb("PRM", [128, 16, 32], F32); PB_ = Buf("PRM")
            TB = sb("TB", [128, 2, 32], F32); TBB = Buf("TB")
            S.dma("pool", WIN[:], I["ssm_w_in"].rearrange("(k p) n -> p k n", p=128), wbuf=WINB)
            S.dma("pool", CRE[:], I["ssm_cre"].rearrange("p (s c) -> p s c", c=32), wbuf=CCB)
            S.dma("pool", CIM[:], I["ssm_cim"].rearrange("p (s c) -> p s c", c=32), wbuf=CCB)
            S.dma("sp", BRE[:], I["ssm_bre"].rearrange("p (s c) -> p s c", c=32), wbuf=BRB)
            S.dma("sp", BIM[:], I["ssm_bim"].rearrange("p (s c) -> p s c", c=32), wbuf=BRB)
            S.dma("sp", PRM[:, 0, :], I["ssm_ldt"], wbuf=PB_)
            S.dma("sp", PRM[:, 1, :], I["ssm_lr"], wbuf=PB_)
            S.dma("sp", PRM[:, 2, :], I["ssm_li"], wbuf=PB_)
            S.dma("sp", DCOL[:], I["ssm_d"], wbuf=SMB)
            S.dma("sp", BGLU[:], I["ssm_bglu"], wbuf=SMB)
            S.dma("sp", DREP[:], I["ssm_drep"], wbuf=SMB)
            P = lambda i: PRM[:, i, :]
            rw = ([PB_], [PB_])
            S.actf(P(3), P(0), AF.Exp, *rw)
            S.tt("dve", P(4), P(1), P(3), ALU.mult, *rw)
            S.tt("dve", P(5), P(2), P(3), ALU.mult, *rw)
            S.actf(P(6), P(4), AF.Exp, *rw)
            PINT = sb("PINT", [128, 32], I32)
            sin_reduced(g, P(7), P(5), 0.5 * PI, P(13), PINT[:], P(14), [PB_], [PB_], [PB_])
            sin_reduced(g, P(8), P(5), 0.0, P(13), PINT[:], P(14), [PB_], [PB_], [PB_])
            S.tt("dve", PR[:, 0, :], P(6), P(7), ALU.mult, [PB_], [POWB])
            S.tt("dve", PIm[:, 0, :], P(6), P(8), ALU.mult, [PB_], [POWB])
            S.ts("dve", P(9), PR[:, 0, :], -1.0, None, ALU.add, None, [PB_, POWB], [PB_])
            S.tt("dve", P(13), P(1), P(1), ALU.mult, *rw)
            S.tt("dve", P(14), P(2), P(2), ALU.mult, *rw)
            S.tt("dve", P(10), P(13), P(14), ALU.add, *rw)
            S.op("dve", lambda e: e.reciprocal(out=PRM[:, 10, :], in_=PRM[:, 10, :]), reads=[PB_], writes=[PB_])
            S.tt("dve", P(13), P(9), P(1), ALU.mult, *rw)
            S.tt("dve", P(14), PIm[:, 0, :], P(2), ALU.mult, [PB_, POWB], [PB_])
            S.tt("dve", P(13), P(13), P(14), ALU.add, *rw)
            S.tt("dve", P(11), P(13), P(10), ALU.mult, *rw)
            S.tt("dve", P(13), PIm[:, 0, :], P(1), ALU.mult, [PB_, POWB], [PB_])
            S.tt("dve", P(14), P(9), P(2), ALU.mult, *rw)
            S.tt("dve", P(13), P(13), P(14), ALU.subtract, *rw)
            S.tt("dve", P(12), P(13), P(10), ALU.mult, *rw)
            for k in range(NSTG):
                if k > 0:
                    S.tt("dve", P(13), PR[:, k - 1, :], PR[:, k - 1, :], ALU.mult, [POWB, PB_], [PB_])
                    S.tt("dve", P(14), PIm[:, k - 1, :], PIm[:, k - 1, :], ALU.mult, [POWB, PB_], [PB_])
                    S.tt("dve", PR[:, k, :], P(13), P(14), ALU.subtract, [PB_, POWB], [POWB])
                    S.tt("dve", P(13), PR[:, k - 1, :], PIm[:, k - 1, :], ALU.mult, [POWB, PB_], [PB_])
                    S.ts("dve", PIm[:, k, :], P(13), 2.0, None, ALU.mult, None, [PB_, POWB], [POWB])
                S.ts("dve", NPI[:, k, :], PIm[:, k, :], -1.0, None, ALU.mult, None, [POWB], [POWB])
            for s in range(32):
                cr, ci = PRM[:, 11, s:s + 1], PRM[:, 12, s:s + 1]
                S.ts("dve", TB[:, 0, :], BIM[:, s, :], ci, None, ALU.mult, None, [BRB, PB_, TBB], [TBB])
                S.ts("dve", TB[:, 1, :], BRE[:, s, :], ci, None, ALU.mult, None, [BRB, PB_, TBB], [TBB])
                S.stt("dve", TB[:, 0, :], BRE[:, s, :], cr, TB[:, 0, :], ALU.mult, ALU.subtract, [BRB, PB_, TBB], [TBB])
                S.stt("dve", BBI[:, s, :], BIM[:, s, :], cr, TB[:, 1, :], ALU.mult, ALU.add, [BRB, PB_, TBB], [BBB])
                S.cp("dve", BBR[:, s, :], TB[:, 0, :], [TBB], [BBB])
            S.tt("dve", P(13), PR[:, 4, :], PR[:, 4, :], ALU.mult, [POWB, PB_], [PB_])
            S.tt("dve", P(14), PIm[:, 4, :], PIm[:, 4, :], ALU.mult, [POWB, PB_], [PB_])
            S.tt("dve", P(13), P(13), P(14), ALU.add, *rw)
            S.op("dve", lambda e: e.reciprocal(out=PRM[:, 13, :], in_=PRM[:, 13, :]), reads=[PB_], writes=[PB_])
            S.tt("dve", IAT[:, 0, :], PR[:, 4, :], P(13), ALU.mult, [POWB, PB_], [POWB])
            S.tt("dve", IAT[:, 1, :], NPI[:, 4, :], P(13), ALU.mult, [POWB, PB_], [POWB])
            S.ts("dve", IAT[:, 2, :], IAT[:, 0, :], -1.0, None, ALU.mult, None, [POWB], [POWB])
            S.ts("dve", IAT[:, 3, :], IAT[:, 1, :], -1.0, None, ALU.mult, None, [POWB], [POWB])
            TT = sb("TT", [128, 2, 256], F32); TTB = Buf("TT")

            def cmul_bc(dr, di, sr, si, k, w):
                pr = PR[:, k, :].unsqueeze(2).to_broadcast([128, 32, w])
                pi_ = PIm[:, k, :].unsqueeze(2).to_broadcast([128, 32, w])
                t1 = TT[:, 0, 0:32 * w].rearrange("p (s w) -> p s w", w=w)
                t2 = TT[:, 1, 0:32 * w].rearrange("p (s w) -> p s w", w=w)
                rwt = ([POWB, TTB], [TTB])
                S.tt("dve", t1, sr, pr, ALU.mult, *rwt)
                S.tt("dve", t2, si, pi_, ALU.mult, *rwt)
                S.tt("dve", dr, t1, t2, ALU.subtract, [TTB, POWB], [POWB])
                S.tt("dve", t1, sr, pi_, ALU.mult, *rwt)
                S.tt("dve", t2, si, pr, ALU.mult, *rwt)
                S.tt("dve", di, t1, t2, ALU.add, [TTB, POWB], [POWB])
            S.memset("pool", ER[:, :, 15:16], 1.0, [POWB], [POWB])
            S.memset("pool", EI[:, :, 15:16], 0.0, [POWB], [POWB])
            S.cp("dve", ER[:, :, 14:15], PR[:, 0, :].unsqueeze(2), [POWB], [POWB])
            S.cp("dve", EI[:, :, 14:15], PIm[:, 0, :].unsqueeze(2), [POWB], [POWB])
            cmul_bc(ER[:, :, 12:14], EI[:, :, 12:14], ER[:, :, 14:16], EI[:, :, 14:16], 1, 2)
            cmul_bc(ER[:, :, 8:12], EI[:, :, 8:12], ER[:, :, 12:16], EI[:, :, 12:16], 2, 4)
            cmul_bc(ER[:, :, 0:8], EI[:, :, 0:8], ER[:, :, 8:16], EI[:, :, 8:16], 3, 8)
            S.cp("dve", FR[:, :, 0:1], PR[:, 0, :].unsqueeze(2), [POWB], [POWB])
            S.cp("dve", FI[:, :, 0:1], PIm[:, 0, :].unsqueeze(2), [POWB], [POWB])
            S.cp("dve", FR[:, :, 1:2], PR[:, 1, :].unsqueeze(2), [POWB], [POWB])
            S.cp("dve", FI[:, :, 1:2], PIm[:, 1, :].unsqueeze(2), [POWB], [POWB])
            cmul_bc(FR[:, :, 2:4], FI[:, :, 2:4], FR[:, :, 0:2], FI[:, :, 0:2], 1, 2)
            cmul_bc(FR[:, :, 4:8], FI[:, :, 4:8], FR[:, :, 0:4], FI[:, :, 0:4], 2, 4)
            cmul_bc(FR[:, :, 8:16], FI[:, :, 8:16], FR[:, :, 0:8], FI[:, :, 0:8], 3, 8)
            for n in range(NB):
                hsl = slice(n * 512, (n + 1) * 512)
                for m in range(8):
                    pb = m % 2
                    for k in range(8):
                        S.mm(g.PS[pb][:], WIN[:, k, m * 128:(m + 1) * 128], g.HT[:, k, hsl], k == 0, k == 7,
                             [WINB] + g.HTB[n * 4:(n + 1) * 4], [g.PSB[pb]])
                    S.cp("act", UT[:, m, hsl], g.PS[pb][:], [g.PSB[pb]], [UTB[n]])
            S.barrier()
            S.emit()
            S.release([WINB, BRB, PB_, CCB, SMB])

        with contextlib.ExitStack() as ph:
            sb = lambda name, shape, dt: g.sb(ph, name, shape, dt)
            NCH = L // 16
            SEL = sb("SEL", [128, 16, 128], BF16); SELT = sb("SELT", [128, 16, 128], BF16); SELB = Buf("SEL")
            TRI = sb("TRI", [128, 128], BF16); MSKB = Buf("TRI")
            CPW = [sb("CPW%d" % i, [128, 512], F32) for i in range(2)]; CPWB = Buf("CPW")
            BPb = [sb("BPb%d" % i, [128, 512], BF16) for i in range(2)]; BPbB = Buf("BPb")
            CW2 = [sb("CW2%d" % i, [128, 512], BF16) for i in range(2)]; CW2B = Buf("CW2")
            CPb = [sb("CPb%d" % i, [128, 512], BF16) for i in range(2)]; CPbB = Buf("CPb")
            M2 = sb("M2", [128, 8, 128], BF16); M2B = Buf("M2")
            M1 = sb("M1", [128, 4, 512], BF16); M1B = Buf("M1")
            DG = sb("DG", [128, 128], BF16); DGB = Buf("DG")
            UG = sb("UG", [128, 512], BF16); UGB = Buf("UG")
            GG = sb("GG", [128, 4, 512], BF16); GGB = [Buf("GG%d" % j) for j in range(4)]
            SA = [sb("SA%d" % i, [128, NCH], F32) for i in range(2)]
            SBf = [sb("SBf%d" % i, [128, NCH], F32) for i in range(2)]
            SAB = [Buf("SA0"), Buf("SA1")]; SBB = [Buf("SB0"), Buf("SB1")]
            XS = [sb("XS%d" % i, [128, NCH], BF16) for i in range(2)]; XSB = Buf("XS")
            GA = sb("GA", [128, 512], F32); GAB = Buf("GA")
            GS = sb("GS", [128, 512], F32); GSB = Buf("GS")
            S.memset("pool", SEL[:], 0.0, [], [SELB])
            S.memset("pool", SELT[:], 0.0, [], [SELB])
            for j in range(4):
                for il in range(4):
                    S.cp("dve", SEL[32 * j:32 * j + 32, j * 4 + il, 32 * il:32 * il + 32], g.ident[32 * j:32 * j + 32, 32 * j:32 * j + 32], [g.CB, SELB], [SELB])
                    S.cp("dve", SELT[32 * il:32 * il + 32, j * 4 + il, 32 * j:32 * j + 32], g.ident[32 * il:32 * il + 32, 32 * il:32 * il + 32], [g.CB, SELB], [SELB])
            S.memset("pool", GA[:, 0:128], 1.0, [GAB], [GAB])
            S.op("pool", lambda e: e.affine_select(out=GA[:, 0:128].rearrange("p (j c) -> p j c", c=32), in_=GA[:, 0:128].rearrange("p (j c) -> p j c", c=32),
                                                   compare_op=ALU.is_ge, fill=0.0, base=31, pattern=[[32, 4], [0, 32]], channel_multiplier=-1),
                 reads=[GAB], writes=[GAB])
            S.cp("pool", TRI[:], GA[:, 0:128], [GAB], [MSKB])
            for i in range(2):
                S.memset("pool", XS[i][:, 0:1], 0.0, [], [XSB])

            def cmul(dst, src, lo_d, lo_s, width, k, s, reads, writes):
                pr, pi_, npi = PR[:, k, s:s + 1], PIm[:, k, s:s + 1], NPI[:, k, s:s + 1]
                d0, d1 = dst[0][:, lo_d:lo_d + width], dst[1][:, lo_d:lo_d + width]
                s0, s1 = src[0][:, lo_s:lo_s + width], src[1][:, lo_s:lo_s + width]
                S.ts("dve", d0, s0, pr, None, ALU.mult, None, reads, writes)
                S.stt("dve", d0, s1, npi, d0, ALU.mult, ALU.add, reads, writes)
                S.ts("dve", d1, s1, pr, None, ALU.mult, None, reads, writes)
                S.stt("dve", d1, s0, pi_, d1, ALU.mult, ALU.add, reads, writes)

            gT = g.HT
            pending = []
            for ct in range(8):
                for j in range(4):
                    s = 4 * ct + j
                    for il in range(4):
                        S.mm(g.PS[2][:], SEL[:, j * 4 + il, :], UT[:, ct, il::4], il == 0, il == 3, [SELB] + UTB, [g.PSB[2]])
                    S.cp("act", UG[:], g.PS[2][:], [g.PSB[2]], [UGB])
                    v3 = lambda ap: ap.rearrange("p (i c) -> p i c", c=32)
                    bc_i = lambda tab: tab[:, s, :].unsqueeze(2).to_broadcast([128, 16, 32])
                    bc_c = lambda tab: tab[:, s, :].unsqueeze(1).to_broadcast([128, 16, 32])
                    ga3, gs3 = v3(GA[:]), v3(GS[:])
                    rwa = ([POWB, BBB, CCB, GAB], [GAB])
                    rws = ([POWB, BBB, CCB, GSB], [GSB])
                    S.tt("dve", ga3, bc_i(ER), bc_c(BBR), ALU.mult, *rwa)
                    S.tt("dve", gs3, bc_i(EI), bc_c(BBI), ALU.mult, *rws)
                    S.tt("dve", BPb[0][:], GA[:], GS[:], ALU.subtract, [GAB, GSB], [BPbB])
                    S.tt("dve", ga3, bc_i(ER), bc_c(BBI), ALU.mult, *rwa)
                    S.tt("dve", gs3, bc_i(EI), bc_c(BBR), ALU.mult, *rws)
                    S.tt("dve", BPb[1][:], GA[:], GS[:], ALU.add, [GAB, GSB], [BPbB])
                    pst = g.PS[6][:].bitcast(BF16)
                    for r in range(2):
                        for kt in range(4):
                            S.tr(pst[:, (r * 4 + kt) * 128:(r * 4 + kt + 1) * 128], BPb[r][:, kt * 128:(kt + 1) * 128], g.ident[:], [BPbB, g.CB], [g.PSB[6]],
                                 signal=(r == 1 and kt == 3))
                    S.cp("act", M2[:], pst.rearrange("p (k n) -> p k n", k=8), [g.PSB[6]], [M2B])
                    for r in range(2):
                        for kt in range(4):
                            S.mm(g.PS[3][:, r * NCH:(r + 1) * NCH], M2[:, r * 4 + kt, :], UG[:, kt::4], kt == 0, kt == 3, [M2B, UGB], [g.PSB[3]])
                    S.cp("act", SA[0][:], g.PS[3][:, 0:NCH], [g.PSB[3]], [SAB[0]])
                    S.cp("act", SA[1][:], g.PS[3][:, NCH:2 * NCH], [g.PSB[3]], [SAB[1]])
                    S.tt("dve", ga3, bc_i(FR), bc_c(CRE), ALU.mult, *rwa)
                    S.tt("dve", gs3, bc_i(FI), bc_c(CIM), ALU.mult, *rws)
                    S.tt("dve", CPW[0][:], GA[:], GS[:], ALU.subtract, [GAB, GSB], [CPWB])
                    S.tt("dve", ga3, bc_i(FR), bc_c(CIM), ALU.mult, *rwa)
                    S.tt("dve", gs3, bc_i(FI), bc_c(CRE), ALU.mult, *rws)
                    S.tt("dve", CPW[1][:], GA[:], GS[:], ALU.add, [GAB, GSB], [CPWB])
                    S.cp("act", CPb[0][:], CPW[0][:], [CPWB], [CPbB])
                    S.actf(CPb[1][:], CPW[1][:], AF.Copy, [CPWB], [CPbB], scale=-1.0)
                    iar, iai, niar, niai = IAT[:, 0, s:s + 1], IAT[:, 1, s:s + 1], IAT[:, 2, s:s + 1], IAT[:, 3, s:s + 1]
                    rw2 = ([CPWB, POWB, GAB], [GAB])
                    S.actf(GA[:], CPW[1][:], AF.Copy, [CPWB, POWB, GAB], [GAB], scale=niai)
                    S.actf(GS[:], CPW[1][:], AF.Copy, [CPWB, POWB, GSB], [GSB], scale=niar)
                    S.stt("dve", CW2[0][:], CPW[0][:], iar, GA[:], ALU.mult, ALU.add, [CPWB, POWB, GAB], [CW2B])
                    S.stt("dve", CW2[1][:], CPW[0][:], niai, GS[:], ALU.mult, ALU.add, [CPWB, POWB, GSB], [CW2B])
                    S.actf(DG[:], g.identf[:], AF.Copy, [g.CB, SMB], [DGB], scale=DREP[:, s:s + 1])
                    if pending:
                        pending.pop()()
                    m1_evac = []
                    for kt in range(4):
                        pb = kt % 2
                        c0 = 128 * kt
                        S.mm(g.PS[pb][:, c0:512], BPb[0][:, c0:c0 + 128], CW2[0][:, c0:512], True, False, [BPbB, CW2B], [g.PSB[pb]], signal=False)
                        S.mm(g.PS[pb][:, c0:512], BPb[1][:, c0:c0 + 128], CW2[1][:, c0:512], False, True, [BPbB, CW2B], [g.PSB[pb]], signal=True)
                        if kt < 2:
                            S.tt("dve", M1[:, kt, c0:c0 + 128], g.PS[pb][:, c0:c0 + 128], TRI[:], ALU.mult, [g.PSB[pb], MSKB], [M1B])
                            S.cp("act", M1[:, kt, c0 + 128:512], g.PS[pb][:, c0 + 128:512], [g.PSB[pb]], [M1B])
                        else:
                            def ev(kt=kt, pb=pb, c0=c0):
                                S.tt("dve", M1[:, kt, c0:c0 + 128], g.PS[pb][:, c0:c0 + 128], TRI[:], ALU.mult, [g.PSB[pb], MSKB], [M1B])
                                if kt < 3:
                                    S.cp("act", M1[:, kt, c0 + 128:512], g.PS[pb][:, c0 + 128:512], [g.PSB[pb]], [M1B])
                            m1_evac.append(ev)
                    src, dst, srcB, dstB = SA, SBf, SAB, SBB
                    for k in range(7):
                        sh = 1 << k
                        pr, pi_, npi = PR[:, k + 4, s:s + 1], PIm[:, k + 4, s:s + 1], NPI[:, k + 4, s:s + 1]
                        S.stt("dve", dst[0][:, sh:], src[0][:, :NCH - sh], pr, src[0][:, sh:], ALU.mult, ALU.add, [srcB[0], POWB], [dstB[0]])
                        S.stt("dve", dst[1][:, sh:], src[1][:, :NCH - sh], pr, src[1][:, sh:], ALU.mult, ALU.add, [srcB[1], POWB], [dstB[1]])
                        S.stt("dve", dst[0][:, sh:], src[1][:, :NCH - sh], npi, dst[0][:, sh:], ALU.mult, ALU.add, [srcB[1], dstB[0], POWB], [dstB[0]])
                        S.stt("dve", dst[1][:, sh:], src[0][:, :NCH - sh], pi_, dst[1][:, sh:], ALU.mult, ALU.add, [srcB[0], dstB[1], POWB], [dstB[1]])
                        S.cp("dve", dst[0][:, :sh], src[0][:, :sh], [srcB[0]], [dstB[0]])
                        S.cp("dve", dst[1][:, :sh], src[1][:, :sh], [srcB[1]], [dstB[1]])
                        if m1_evac:
                            m1_evac.pop(0)()
                        src, dst, srcB, dstB = dst, src, dstB, srcB
                    S.cp("act", XS[0][:, 1:NCH], src[0][:, 0:NCH - 1], [srcB[0]], [XSB])
                    S.cp("act", XS[1][:, 1:NCH], src[1][:, 0:NCH - 1], [srcB[1]], [XSB])
                    yb = 4 + (s % 2)
                    for jt in range(4):
                        osl = g.PS[yb][:, jt * NCH:(jt + 1) * NCH]
                        first = True
                        for kt in range(jt + 1):
                            S.mm(osl, M1[:, kt, jt * 128:(jt + 1) * 128], UG[:, kt::4], first, False, [M1B, UGB], [g.PSB[yb]], signal=False)
                            first = False
                        S.mm(osl, DG[:], UG[:, jt::4], False, False, [DGB, UGB], [g.PSB[yb]], signal=False)
                        S.mm(osl, CPb[0][:, jt * 128:(jt + 1) * 128], XS[0][:], False, False, [CPbB, XSB], [g.PSB[yb]], signal=False)
                        S.mm(osl, CPb[1][:, jt * 128:(jt + 1) * 128], XS[1][:], False, True, [CPbB, XSB], [g.PSB[yb]], signal=True)

                    def gelu(yb=yb, j=j):
                        ybuf = g.PSB[yb]
                        yp = g.PS[yb][:]
                        S.actf(GA[:], yp, AF.Square, [ybuf], [GAB])
                        S.actf(GA[:], GA[:], AF.Identity, [GAB, g.CB], [GAB], bias=g.cst[:, 2:3], scale=0.044715)
                        S.tt("dve", GA[:], GA[:], yp, ALU.mult, [GAB, ybuf], [GAB])
                        S.actf(GS[:], GA[:], AF.Sigmoid, [GAB], [GSB], scale=1.5957691216057308)
                        S.tt("dve", GG[:, j, :].rearrange("p (n jt) -> p jt n", jt=4), GS[:].rearrange("p (jt n) -> p jt n", jt=4),
                             yp.rearrange("p (jt n) -> p jt n", jt=4), ALU.mult, [GSB, ybuf], [GGB[j]])
                    pending.append(gelu)
                if pending:
                    pending.pop()()
                for jl in range(4):
                    pb = jl % 2
                    for j in range(4):
                        S.mm(g.PS[pb][:], SELT[:, j * 4 + jl, :], GG[:, j, :], j == 0, j == 3, [SELB, GGB[j]], [g.PSB[pb]])
                    S.cp("act", gT[:, ct, jl::4], g.PS[pb][:], [g.PSB[pb]], g.HTB)
            S.barrier()
            S.emit()
            S.release([BBB, POWB])
        ab.close()

        with contextlib.ExitStack() as ph:
            sb = lambda name, shape, dt: g.sb(ph, name, shape, dt)
            WG = sb("WG", [128, 8, D], BF16); WGB = Buf("WG")
            WOU = sb("WOU", [128, 8, D], BF16); WOUB = Buf("WOU")
            load_ln(g, ph)
            SG = [sb("SG%d" % i, [128, 512], F32) for i in range(2)]; SGB = [Buf("SG%d" % i) for i in range(2)]
            S.dma("pool", WG[:], I["ssm_w_glu"].rearrange("(k p) n -> p k n", p=128), wbuf=WGB)
            S.dma("pool", WOU[:], I["ssm_w_out"].rearrange("(k p) n -> p k n", p=128), wbuf=WOUB)
            for k in range(8):
                S.tt("dve", WOU[:, k, :], WOU[:, k, :], g.MOD[:, 2, :], ALU.mult, [WOUB, g.MODB], [WOUB])
            gT = g.HT
            ZT = UT
            ZTB = [Buf("ZT%d" % n) for n in range(NB)]
            c_ = 0
            for n in range(NB):
                hsl = slice(n * 512, (n + 1) * 512)
                for m in range(8):
                    pb = c_ % 2
                    sg, sgb = SG[c_ % 2], SGB[c_ % 2]
                    c_ += 1
                    for k in range(8):
                        S.mm(g.PS[pb][:], WG[:, k, m * 128:(m + 1) * 128], gT[:, k, hsl], k == 0, k == 7,
                             [WGB] + g.HTB[n * 4:(n + 1) * 4], [g.PSB[pb]])
                    S.actf(sg[:], g.PS[pb][:], AF.Sigmoid, [g.PSB[pb], SMB], [sgb], bias=BGLU[:, m:m + 1], scale=1.0)
                    S.tt("dve", ZT[:, m, hsl], gT[:, m, hsl], sg[:], ALU.mult, [sgb] + g.HTB[n * 4:(n + 1) * 4] + [UTB[n]], [ZTB[n], UTB[n]])
            for t in range(NT):
                pbs = (2 + 2 * (t % 2), 3 + 2 * (t % 2))
                for hh in range(2):
                    for k in range(8):
                        S.mm(g.PS[pbs[hh]][:], ZT[:, k, t * 128:(t + 1) * 128], WOU[:, k, hh * 512:(hh + 1) * 512], k == 0, k == 7,
                             [ZTB[t // 4], WOUB], [g.PSB[pbs[hh]]])
                mixer_epilogue_tile(g, t, pbs)
            ln_flush(g)
            S.barrier()
            S.emit()
            S.release([WGB, WOUB, g.LNGB, g.LNBB])


def _prep_inputs(inp, b):
    f = np.float32
    m = {}
    m["x"] = np.ascontiguousarray(inp["x"][b], dtype=f)
    m["c"] = np.ascontiguousarray(inp["c"][b].reshape(8, 128).T, dtype=f)
    m["pos"] = np.ascontiguousarray(np.broadcast_to(inp["positions"][b][None, :], (64, L)), dtype=np.int32)
    p = np.arange(64) % 32
    m["invf"] = (10000.0 ** (-(2.0 * p) / 64.0)).astype(f).reshape(64, 1)
    w_in = inp["mla_w_in"][0]
    kpe = w_in[:, 384:448]
    m["mla_w_in"] = np.ascontiguousarray(np.concatenate([w_in, kpe[:, 32:], kpe[:, :32]], axis=1), dtype=f)
    m["mla_qn"] = np.ascontiguousarray(inp["mla_q_norm"][0].reshape(2, 128).T, dtype=f)
    wq = inp["mla_w_qb"][0].reshape(256, 8, 192)
    wq_ext = np.zeros((256, 8, 320), f)
    wq_ext[:, :, 0:192] = wq
    wq_ext[:, :, 192:224] = wq[:, :, 160:192]
    wq_ext[:, :, 224:256] = wq[:, :, 128:160]
    m["mla_w_qb"] = wq_ext.reshape(256, 2560)
    m["mla_kvn"] = np.ascontiguousarray(inp["mla_kv_norm"][0].reshape(128, 1), dtype=f)
    wkv = inp["mla_w_kvb"][0].reshape(128, 8, 256)
    m["mla_w_kvb"] = np.ascontiguousarray(np.concatenate([wkv[:, :, :128].reshape(128, 1024), wkv[:, :, 128:].reshape(128, 1024)], axis=1), dtype=f)
    m["mla_w_o"] = np.ascontiguousarray(inp["mla_w_o"][0], dtype=f)
    m["ssm_w_in"] = np.ascontiguousarray(inp["ssm_w_in"][0], dtype=f)

    def sp(a):
        return np.ascontiguousarray(a.reshape(32, 2, 64).transpose(1, 2, 0).reshape(128, 32), dtype=f)
    m["ssm_ldt"] = sp(np.repeat(inp["ssm_log_dt"][0][:, None], 64, axis=1))
    m["ssm_lr"] = sp(inp["ssm_a_re"][0])
    m["ssm_li"] = sp(inp["ssm_a_im"][0])

    def bd(a):
        o = np.zeros((2, 64, 32, 2, 16), f)
        a4 = a.reshape(32, 2, 64, 16)
        for gl in range(2):
            o[gl, :, :, gl, :] = a4[:, gl].transpose(1, 0, 2)
        return o.reshape(128, 32 * 32)
    m["ssm_bre"] = bd(inp["ssm_b_re"][0])
    m["ssm_bim"] = bd(inp["ssm_b_im"][0])
    m["ssm_cre"] = bd(inp["ssm_c_re"][0].transpose(0, 2, 1))
    m["ssm_cim"] = bd(inp["ssm_c_im"][0].transpose(0, 2, 1))
    m["ssm_d"] = np.ascontiguousarray(inp["ssm_d"][0].reshape(8, 128).T, dtype=f)
    m["ssm_drep"] = np.ascontiguousarray(np.tile(inp["ssm_d"][0].reshape(32, 32).T, (4, 1)), dtype=f)
    m["ssm_w_glu"] = np.ascontiguousarray(inp["ssm_w_glu"][0], dtype=f)
    m["ssm_bglu"] = np.ascontiguousarray(inp["ssm_b_glu"][0].reshape(8, 128).T, dtype=f)
    m["ssm_w_out"] = np.ascontiguousarray(inp["ssm_w_out"][0], dtype=f)
    m["mlp_w1"] = np.ascontiguousarray(inp["mlp_w1"], dtype=f)
    m["mlp_b1"] = np.ascontiguousarray(inp["mlp_b1"].reshape(2, 32, 128).transpose(0, 2, 1), dtype=f)
    m["mlp_w2"] = np.ascontiguousarray(inp["mlp_w2"], dtype=f)
    m["mlp_b2"] = np.ascontiguousarray(np.broadcast_to(inp["mlp_b2"][:, None, :], (2, 128, D)), dtype=f)
    mw = np.stack([inp["mod_mix_w"][0], inp["mod_ffn_w"][0], inp["mod_mix_w"][1], inp["mod_ffn_w"][1]])
    mb = np.stack([inp["mod_mix_b"][0], inp["mod_ffn_b"][0], inp["mod_mix_b"][1], inp["mod_ffn_b"][1]])
    m["mod_w"] = np.ascontiguousarray(mw, dtype=f)
    m["mod_b"] = np.ascontiguousarray(np.broadcast_to(mb[:, None, :], (4, 128, 3 * D)), dtype=f)
    lg = np.stack([inp["ln_mix_g"][0], inp["ln_ffn_g"][0], inp["ln_mix_g"][1], inp["ln_ffn_g"][1]])
    lb = np.stack([inp["ln_mix_b"][0], inp["ln_ffn_b"][0], inp["ln_mix_b"][1], inp["ln_ffn_b"][1]])
    m["ln_g"] = np.ascontiguousarray(np.broadcast_to(lg[:, None, :], (4, 128, D)), dtype=f)
    m["ln_b"] = np.ascontiguousarray(np.broadcast_to(lb[:, None, :], (4, 128, D)), dtype=f)
    return m


_NC_CACHE = {}


def run(inp, sublayers=(0, 1, 2, 3), trace=False, same_engine_sync=True):
    key = (tuple(sublayers), same_engine_sync)
    if key not in _NC_CACHE:
        _NC_CACHE[key] = build_nc(sublayers, same_engine_sync)
    nc = _NC_CACHE[key]
    shared = None
    in_maps = []
    for b in range(8):
        m = _prep_inputs(inp, b)
        if shared is None:
            shared = m
        else:
            for k in m:
                if k not in ("x", "c", "pos"):
                    m[k] = shared[k]
        in_maps.append(m)
    res = run_bass_kernel_spmd(nc, in_maps, core_ids=list(range(8)), trace=trace)
    outp = np.stack([np.asarray(r["out"], dtype=np.float32) for r in res.results], axis=0)
    return outp, res


def kernel(**inputs):
    inp = {k: np.asarray(v) for k, v in inputs.items()}
    outp, _ = run(inp)
    return outp
```

```python
import contextlib
import math
import numpy as np
import concourse.bass as bass
import concourse.mybir as mybir
from concourse.bass_utils import run_bass_kernel_spmd

F32 = mybir.dt.float32
BF16 = mybir.dt.bfloat16
I32 = mybir.dt.int32
ALU = mybir.AluOpType
AF = mybir.ActivationFunctionType

D = 1024
L = 2048
NT = L // 128
NB = L // 512
DFF = 4096
ALPHA = 4 ** 0.25
LN_EPS = 1e-5
RMS_EPS = 1e-6
PI = math.pi


class Buf:
    __slots__ = ("name", "w", "r", "dsem", "dcnt")

    def __init__(self, name):
        self.name = name
        self.w = {}
        self.r = {}
        self.dsem = None
        self.dcnt = 0


class Sched:
    ENG = ("pe", "act", "dve", "pool", "sp")
    EMAP = {"pe": "tensor", "act": "scalar", "dve": "vector", "pool": "gpsimd", "sp": "sync"}

    def __init__(self, nc, stack, same_engine_sync=True):
        self.nc = nc
        self.stack = stack
        self.prog = {e: [] for e in self.ENG}
        self.sems = {}
        self.cnt = {e: 0 for e in self.ENG}
        self.waited = {e: {} for e in self.ENG}
        self.same_engine_sync = same_engine_sync
        for e in self.ENG:
            self.sems[e] = stack.enter_context(nc.semaphore("s_" + e))
        self.dsem_cnt = {}
        self.dsem_free = []
        self.swdge_sems = set()
        self.out_tokens = []

    def _get_dsem(self, fresh=False):
        if self.dsem_free and not fresh:
            return self.dsem_free.pop()
        key = "d%d" % len(self.dsem_cnt)
        self.sems[key] = self.stack.enter_context(self.nc.semaphore(key))
        self.dsem_cnt[key] = 0
        return key

    def release(self, bufs):
        for b in bufs:
            if b.dsem is not None:
                if b.dsem not in self.swdge_sems:
                    self.dsem_free.append(b.dsem)
                b.dsem = None

    def _deps(self, eng, reads, writes):
        need = {}
        for b in reads:
            for k, v in b.w.items():
                if need.get(k, 0) < v:
                    need[k] = v
        for b in writes:
            for k, v in b.w.items():
                if need.get(k, 0) < v:
                    need[k] = v
            for k, v in b.r.items():
                if need.get(k, 0) < v:
                    need[k] = v
        wd = self.waited[eng]
        for k, v in need.items():
            if k == eng and (eng == "pe" or not self.same_engine_sync):
                continue
            if wd.get(k, 0) >= v:
                continue
            wd[k] = v
            self.prog[eng].append(("wait", k, v))

    def op(self, eng, fn, reads=(), writes=(), signal=True):
        self._deps(eng, reads, writes)
        tok = self.cnt[eng] + 1
        if signal:
            self.cnt[eng] = tok
            self.prog[eng].append(("op", fn, eng, 1))
        else:
            self.prog[eng].append(("op", fn, None, 0))
        for b in reads:
            if b.r.get(eng, 0) < tok:
                b.r[eng] = tok
        for b in writes:
            if b.w.get(eng, 0) < tok:
                b.w[eng] = tok

    def dma(self, eng, out, in_, wbuf=None, rbuf=None, is_output=False):
        reads = [rbuf] if rbuf is not None else []
        writes = [wbuf] if wbuf is not None else []
        self._deps(eng, reads, writes)
        own = wbuf if wbuf is not None else rbuf
        if own.dsem is None:
            own.dsem = self._get_dsem(fresh=(eng == "pool"))
            if eng == "pool":
                self.swdge_sems.add(own.dsem)
        key = own.dsem
        self.dsem_cnt[key] += 16
        val = self.dsem_cnt[key]

        def fn(e, out=out, in_=in_):
            return e.dma_start(out=out, in_=in_)
        self.prog[eng].append(("op", fn, key, 16))
        if wbuf is not None:
            wbuf.w[key] = val
        if rbuf is not None:
            rbuf.r[key] = val
        if is_output:
            self.out_tokens.append((key, val))


    def mm(self, out, lhsT, rhs, start, stop, reads, writes, signal=None):
        if signal is None:
            signal = stop
        self.op("pe", lambda e, o=out, l=lhsT, r=rhs, s=start, t=stop: e.matmul(o, lhsT=l, rhs=r, start=s, stop=t),
                reads, writes, signal)

    def tr(self, out, in_, ident, reads, writes, signal=True):
        self.op("pe", lambda e, o=out, i=in_, d=ident: e.transpose(out=o, in_=i, identity=d), reads, writes, signal)

    def tt(self, eng, out, in0, in1, op, reads, writes):
        self.op(eng, lambda e, o=out, a=in0, b=in1, p=op: e.tensor_tensor(out=o, in0=a, in1=b, op=p), reads, writes)

    def ts(self, eng, out, in0, s1, s2, op0, op1, reads, writes):
        if s2 is None:
            self.op(eng, lambda e, o=out, a=in0, x=s1, p=op0: e.tensor_scalar(out=o, in0=a, scalar1=x, scalar2=None, op0=p), reads, writes)
        else:
            self.op(eng, lambda e, o=out, a=in0, x=s1, y=s2, p=op0, q=op1: e.tensor_scalar(out=o, in0=a, scalar1=x, scalar2=y, op0=p, op1=q), reads, writes)

    def stt(self, eng, out, in0, scalar, in1, op0, op1, reads, writes):
        self.op(eng, lambda e, o=out, a=in0, s=scalar, b=in1, p=op0, q=op1: e.scalar_tensor_tensor(out=o, in0=a, scalar=s, in1=b, op0=p, op1=q), reads, writes)

    def cp(self, eng, out, in_, reads, writes):
        if eng == "act":
            self.op(eng, lambda e, o=out, i=in_: e.copy(out=o, in_=i), reads, writes)
        else:
            self.op(eng, lambda e, o=out, i=in_: e.tensor_copy(out=o, in_=i), reads, writes)

    def actf(self, out, in_, func, reads, writes, bias=None, scale=None):
        if bias is None and scale is None:
            self.op("act", lambda e, o=out, i=in_, f=func: e.activation(out=o, in_=i, func=f), reads, writes)
        elif bias is None:
            self.op("act", lambda e, o=out, i=in_, f=func, s=scale: e.activation(out=o, in_=i, func=f, scale=s), reads, writes)
        else:
            sc = 1.0 if scale is None else scale
            self.op("act", lambda e, o=out, i=in_, f=func, b=bias, s=sc: e.activation(out=o, in_=i, func=f, bias=b, scale=s), reads, writes)

    def memset(self, eng, out, val, reads, writes):
        self.op(eng, lambda e, o=out, v=val: e.memset(o, v), reads, writes)

    def barrier(self):
        for e in self.ENG:
            wd = self.waited[e]
            for f in self.ENG:
                if f == e or self.cnt[f] == 0:
                    continue
                if wd.get(f, 0) < self.cnt[f]:
                    wd[f] = self.cnt[f]
                    self.prog[e].append(("wait", f, self.cnt[f]))
            for k, v in self.dsem_cnt.items():
                if v > 0 and wd.get(k, 0) < v:
                    wd[k] = v
                    self.prog[e].append(("wait", k, v))

    def emit(self):
        nc = self.nc
        with nc.Block() as block:
            for e in self.ENG:
                items = self.prog[e]

                def body(engine, items=items):
                    for it in items:
                        if it[0] == "wait":
                            engine.wait_ge(self.sems[it[1]], it[2])
                        else:
                            ins = it[1](engine)
                            if it[2] is not None:
                                ins.then_inc(self.sems[it[2]], it[3])
                getattr(block, self.EMAP[e])(body)
        self.prog = {e: [] for e in self.ENG}


class Ctx:
    pass


INPUT_SHAPES = {
    "x": ([L, D], F32),
    "c": ([128, 8], F32),
    "pos": ([64, L], I32),
    "invf": ([64, 1], F32),
    "mla_w_in": ([D, 512], F32),
    "mla_qn": ([128, 2], F32),
    "mla_w_qb": ([256, 8 * 320], F32),
    "mla_kvn": ([128, 1], F32),
    "mla_w_kvb": ([128, 2048], F32),
    "mla_w_o": ([1024, 1024], F32),
    "ssm_w_in": ([D, D], F32),
    "ssm_ldt": ([128, 32], F32),
    "ssm_lr": ([128, 32], F32),
    "ssm_li": ([128, 32], F32),
    "ssm_bre": ([128, 32 * 32], F32),
    "ssm_bim": ([128, 32 * 32], F32),
    "ssm_cre": ([128, 32 * 32], F32),
    "ssm_cim": ([128, 32 * 32], F32),
    "ssm_d": ([128, 8], F32),
    "ssm_drep": ([128, 32], F32),
    "ssm_w_glu": ([D, D], F32),
    "ssm_bglu": ([128, 8], F32),
    "ssm_w_out": ([D, D], F32),
    "mlp_w1": ([2, D, DFF], F32),
    "mlp_b1": ([2, 128, 32], F32),
    "mlp_w2": ([2, DFF, D], F32),
    "mlp_b2": ([2, 128, D], F32),
    "mod_w": ([4, D, 3 * D], F32),
    "mod_b": ([4, 128, 3 * D], F32),
    "ln_g": ([4, 128, D], F32),
    "ln_b": ([4, 128, D], F32),
}


def build_nc(sublayers=(0, 1, 2, 3), same_engine_sync=True):
    nc = bass.Bass("TRN2", target_bir_lowering=False)
    I = {k: nc.dram_tensor(k, list(s), d, kind="ExternalInput").ap() for k, (s, d) in INPUT_SHAPES.items()}
    out = nc.dram_tensor("out", [L, D], F32, kind="ExternalOutput").ap()

    with contextlib.ExitStack() as st:
        S = Sched(nc, st, same_engine_sync=same_engine_sync)
        g = Ctx()
        g.nc, g.S, g.I, g.out = nc, S, I, out

        g.uid = 0

        def sb(stack, name, shape, dt):
            g.uid += 1
            return stack.enter_context(nc.sbuf_tensor("%s_%d" % (name, g.uid), list(shape), dt))
        g.sb = sb

        g.X = sb(st, "X", [128, NT, D], F32)
        g.XB = [Buf("X%d" % t) for t in range(NT)]
        g.HT = sb(st, "HT", [128, 8, L], BF16)
        g.HTB = [Buf("HT%d" % t) for t in range(NT)]
        g.MOD = sb(st, "MOD", [128, 3, D], F32)
        g.MODB = Buf("MOD")
        g.ident = sb(st, "ident", [128, 128], BF16)
        g.identf = sb(st, "identf", [128, 128], F32)
        g.ones = sb(st, "ones", [128, 128], F32)
        g.onesb = sb(st, "onesb", [128, 128], BF16)
        g.CS = sb(st, "CS", [128, 8], F32)
        g.cst = sb(st, "cst", [128, 8], F32)
        g.CB = Buf("consts")
        g.CSB = Buf("CS")
        g.LNS = sb(st, "LNS", [128, 4, 16], F32)
        g.LNSB = [Buf("LNS%d" % i) for i in range(4)]
        g.lnctr = 0
        g.PS = [st.enter_context(nc.psum_tensor("psb%d" % i, [128, 512], F32)) for i in range(8)]
        g.PSB = [Buf("psb%d" % i) for i in range(8)]

        S.memset("pool", g.identf[:], 0.0, [], [g.CB])
        S.op("pool", lambda e: e.affine_select(out=g.identf[:], in_=g.identf[:], compare_op=ALU.not_equal, fill=1.0,
                                                base=0, pattern=[[-1, 128]], channel_multiplier=1),
             reads=[g.CB], writes=[g.CB])
        S.memset("pool", g.ones[:], 1.0, [], [g.CB])
        S.memset("pool", g.onesb[:], 1.0, [], [g.CB])
        for col, val in enumerate([-PI, PI, 1.0, LN_EPS, RMS_EPS, 0.0]):
            S.memset("pool", g.cst[:, col:col + 1], val, [], [g.CB])
        S.cp("dve", g.ident[:], g.identf[:], [g.CB], [g.CB])
        xin = I["x"].rearrange("(t p) d -> p t d", p=128)
        for t in range(NT):
            S.dma("sp", g.X[:, t, :], xin[:, t, :], wbuf=g.XB[t])
        S.dma("sp", g.CS[:], I["c"], wbuf=g.CSB)
        S.actf(g.CS[:], g.CS[:], AF.Silu, [g.CSB], [g.CSB])
        g.CSBC = sb(st, "CSBC", [128, 8, 128], F32)
        for k in range(8):
            S.cp("dve", g.CSBC[:, k, :], g.CS[:, k:k + 1].to_broadcast([128, 128]), [g.CSB], [g.CSB])

        for sl in sublayers:
            g.sl = sl
            prologue(g, sl)
            if sl == 0:
                mla_phase(g)
            elif sl == 2:
                s5_phase(g)
            else:
                ffn_phase(g, sl // 2, sl)

        oap = out.rearrange("(t p) d -> p t d", p=128)
        for t in range(NT):
            S.dma("sp", oap[:, t, :], g.X[:, t, :], rbuf=g.XB[t], is_output=True)
        S.barrier()
        S.emit()
    return nc


def sin_reduced(g, out, ang, shift, T, TI, M, reads, writes, tbufs, sign=1.0):
    S = g.S
    rw = (list(reads) + list(tbufs), list(tbufs))
    S.ts("dve", T, ang, shift, 1.0 / (2 * PI), ALU.add, ALU.mult, *rw)
    S.cp("dve", TI, T, *rw)
    S.cp("dve", M, TI, *rw)
    S.tt("dve", T, T, M, ALU.subtract, *rw)
    S.ts("dve", M, T, 0.5, None, ALU.is_gt, None, *rw)
    S.tt("dve", T, T, M, ALU.subtract, *rw)
    S.ts("dve", M, T, -0.5, None, ALU.is_lt, None, *rw)
    S.tt("dve", T, T, M, ALU.add, *rw)
    S.actf(out, T, AF.Sin, list(tbufs), list(writes), scale=sign * 2 * PI)


def prologue(g, sl):
    S, I = g.S, g.I
    with contextlib.ExitStack() as ph:
        WST = [g.sb(ph, "WST%d" % i, [128, 3 * D], F32) for i in range(2)]
        WSTB = [Buf("WST%d" % i) for i in range(2)]
        BROW = g.sb(ph, "BROW", [128, 3 * D], F32)
        BROWB = Buf("BROW")
        TMP = [g.sb(ph, "HTMP%d" % i, [128, D], F32) for i in range(2)]
        TMPB = [Buf("HTMP%d" % i) for i in range(2)]
        HB = [g.sb(ph, "HB%d" % i, [128, D], BF16) for i in range(2)]
        HBB = [Buf("HB%d" % i) for i in range(2)]
        local = WSTB + [BROWB] + TMPB + HBB

        wv = I["mod_w"][sl].rearrange("(k p) n -> p k n", p=128)
        S.dma("sp", BROW[:], I["mod_b"][sl], wbuf=BROWB)
        for k in range(8):
            w = WST[k % 2]
            S.dma("sp", w[:], wv[:, k, :], wbuf=WSTB[k % 2])
            for n in range(6):
                S.mm(g.PS[n][:], g.CSBC[:, k, :], w[:, n * 512:(n + 1) * 512], k == 0, k == 7, [WSTB[k % 2], g.CSB], [g.PSB[n]], signal=(n == 5 or k == 7))
        for n in range(6):
            j, hh = n // 2, n % 2
            addc = 0.0 if j == 0 else 1.0
            S.stt("dve", g.MOD[:, j, hh * 512:(hh + 1) * 512], g.PS[n][:], addc, BROW[:, n * 512:(n + 1) * 512], ALU.add, ALU.add,
                  [g.PSB[n], BROWB], [g.MODB])
        for t in range(NT):
            tm, tmb = TMP[t % 2], TMPB[t % 2]
            hb, hbb = HB[t % 2], HBB[t % 2]
            S.tt("dve", tm[:], g.X[:, t, :], g.MOD[:, 1, :], ALU.mult, [g.XB[t], g.MODB], [tmb])
            S.tt("dve", hb[:], tm[:], g.MOD[:, 0, :], ALU.add, [tmb, g.MODB], [hbb])
            pbi = 2 + (t % 2)
            pst = g.PS[pbi][:].bitcast(BF16)
            for k in range(8):
                S.tr(pst[:, k * 128:(k + 1) * 128], hb[:, k * 128:(k + 1) * 128], g.ident[:], [hbb, g.CB], [g.PSB[pbi]], signal=(k == 7))
            S.cp("act", g.HT[:, :, t * 128:(t + 1) * 128], pst.rearrange("p (k n) -> p k n", k=8), [g.PSB[pbi]], [g.HTB[t]])
        S.barrier()
        S.emit()
        S.release(local)


def load_ln(g, ph):
    g.LNG = g.sb(ph, "LNG", [128, D], F32)
    g.LNGB = Buf("LNG")
    g.LNBt = g.sb(ph, "LNBt", [128, D], F32)
    g.LNBB = Buf("LNB")
    g.S.dma("sp", g.LNG[:], g.I["ln_g"][g.sl], wbuf=g.LNGB)
    g.S.dma("sp", g.LNBt[:], g.I["ln_b"][g.sl], wbuf=g.LNBB)


def layer_norm_tile(g, t, eng2="pool"):
    S = g.S
    i = g.lnctr % 4
    g.lnctr += 1
    st_ = g.LNS[:, i, :]
    sbuf = g.LNSB[i]
    xt = g.X[:, t, :]
    for c in range(2):
        S.op("dve", lambda e, o=st_[:, c * 6:(c + 1) * 6], i_=xt[:, c * 512:(c + 1) * 512]: e.bn_stats(out=o, in_=i_),
             reads=[g.XB[t]], writes=[sbuf])
    S.op("dve", lambda e, o=st_[:, 12:14], i_=st_[:, 0:12].rearrange("p (c s) -> p c s", c=2): e.bn_aggr(out=o, in_=i_), reads=[sbuf], writes=[sbuf])
    S.actf(st_[:, 14:15], st_[:, 13:14], AF.Sqrt, [sbuf, g.CB], [sbuf], bias=g.cst[:, 3:4], scale=1.0)
    LNG, LNGB, LNBt, LNBB = g.LNG, g.LNGB, g.LNBt, g.LNBB

    def apply():
        S.op("dve", lambda e, o=st_[:, 14:15]: e.reciprocal(out=o, in_=o), reads=[sbuf], writes=[sbuf])
        S.stt("dve", xt, xt, st_[:, 12:13], LNG[:], ALU.subtract, ALU.mult, [sbuf, g.XB[t], LNGB], [g.XB[t]])
        S.stt("dve", xt, xt, st_[:, 14:15], LNBt[:], ALU.mult, ALU.add, [sbuf, g.XB[t], LNBB], [g.XB[t]])
    ln_flush(g)
    g.ln_pending = apply


def ln_flush(g):
    p = getattr(g, "ln_pending", None)
    if p is not None:
        g.ln_pending = None
        p()


def mixer_epilogue_tile(g, t, pbanks, TMP=None, TMPB=None):
    S = g.S
    for hh in range(2):
        xs = g.X[:, t, hh * 512:(hh + 1) * 512]
        S.stt("dve", xs, xs, ALPHA, g.PS[pbanks[hh]][:], ALU.mult, ALU.add, [g.XB[t], g.PSB[pbanks[hh]]], [g.XB[t]])
    layer_norm_tile(g, t)


def ffn_phase(g, li, sl):
    S, I = g.S, g.I
    with contextlib.ExitStack() as ph:
        W1 = [g.sb(ph, "W1c%d" % i, [128, 8, 512], BF16) for i in range(2)]
        W1B = [Buf("W1c%d" % i) for i in range(2)]
        W2 = [g.sb(ph, "W2c%d" % i, [128, 4, D], BF16) for i in range(2)]
        W2B = [Buf("W2c%d" % i) for i in range(2)]
        AT = [g.sb(ph, "AT%d" % i, [128, 4, L], BF16) for i in range(2)]
        ATB = [[Buf("AT%d_%d" % (i, n)) for n in range(NB)] for i in range(2)]
        RT = [g.sb(ph, "RT%d" % i, [128, 512], F32) for i in range(2)]
        RTB = [Buf("RT%d" % i) for i in range(2)]
        B1 = g.sb(ph, "B1", [128, 32], F32)
        B1B = Buf("B1")
        B2R = g.sb(ph, "B2R", [128, D], F32)
        B2B = Buf("B2R")
        load_ln(g, ph)
        local = W1B + W2B + RTB + [B1B, B2B] + [b for r in ATB for b in r]
        local_ln = True

        S.dma("sp", B1[:], I["mlp_b1"][li], wbuf=B1B)
        S.dma("sp", B2R[:], I["mlp_b2"][li], wbuf=B2B)
        S.tt("dve", B2R[:], B2R[:], g.MOD[:, 2, :], ALU.mult, [B2B, g.MODB], [B2B])
        for t in range(NT):
            S.stt("dve", g.X[:, t, :], g.X[:, t, :], ALPHA, B2R[:], ALU.mult, ALU.add, [g.XB[t], B2B], [g.XB[t]])
        w1v = I["mlp_w1"][li].rearrange("(k p) n -> p k n", p=128)
        w2v = I["mlp_w2"][li].rearrange("(j p) n -> p j n", p=128)
        rtc = 0
        for c in range(8):
            pc = c % 2
            S.dma("pool", W1[pc][:], w1v[:, :, c * 512:(c + 1) * 512], wbuf=W1B[pc])
            S.dma("pool", W2[pc][:], w2v[:, c * 4:(c + 1) * 4, :], wbuf=W2B[pc])
            for j in range(4):
                S.tt("dve", W2[pc][:, j, :], W2[pc][:, j, :], g.MOD[:, 2, :], ALU.mult, [W2B[pc], g.MODB], [W2B[pc]])
            for n in range(NB):
                for m in range(4):
                    pb = (n * 4 + m) % 2
                    for k in range(8):
                        S.mm(g.PS[pb][:], W1[pc][:, k, m * 128:(m + 1) * 128], g.HT[:, k, n * 512:(n + 1) * 512], k == 0, k == 7,
                             [W1B[pc]] + g.HTB[n * 4:(n + 1) * 4], [g.PSB[pb]])
                    rt, rtb = RT[rtc % 2], RTB[rtc % 2]
                    rtc += 1
                    col = c * 4 + m
                    S.actf(rt[:], g.PS[pb][:], AF.Relu, [g.PSB[pb], B1B], [rtb], bias=B1[:, col:col + 1], scale=1.0)
                    S.actf(AT[pc][:, m, n * 512:(n + 1) * 512], rt[:], AF.Square, [rtb], [ATB[pc][n]])
            for t in range(NT):
                pbs = (2 + 2 * (t % 2), 3 + 2 * (t % 2))
                for hh in range(2):
                    for j in range(4):
                        S.mm(g.PS[pbs[hh]][:], AT[pc][:, j, t * 128:(t + 1) * 128], W2[pc][:, j, hh * 512:(hh + 1) * 512], j == 0, j == 3,
                             [ATB[pc][t // 4], W2B[pc]], [g.PSB[pbs[hh]]])
                for hh in range(2):
                    S.tt("dve", g.X[:, t, hh * 512:(hh + 1) * 512], g.PS[pbs[hh]][:], g.X[:, t, hh * 512:(hh + 1) * 512], ALU.add,
                         [g.PSB[pbs[hh]], g.XB[t]], [g.XB[t]])
                if c == 7:
                    layer_norm_tile(g, t)
        ln_flush(g)
        S.barrier()
        S.emit()
        S.release(local + [g.LNGB, g.LNBB])


def mla_phase(g):
    S, I = g.S, g.I
    SCALE = 1.0 / math.sqrt(192.0)
    with contextlib.ExitStack() as ph:
        sb = lambda name, shape, dt: g.sb(ph, name, shape, dt)
        WIN = sb("WIN", [128, 8, 512], BF16); WINB = Buf("WIN")
        WQB = sb("WQB", [128, 2, 2560], BF16); WQBB = Buf("WQB")
        WKV = sb("WKV", [128, 2048], BF16); WKVB = Buf("WKV")
        QN = sb("QN", [128, 2], F32); QNB = Buf("QN")
        KVN = sb("KVN", [128, 1], F32); KVNB = Buf("KVN")
        CQT = sb("CQT", [128, 2, L], BF16); CQB = [Buf("CQ%d" % n) for n in range(NB)]
        CKT = sb("CKT", [128, L], BF16); CKB = [Buf("CK%d" % n) for n in range(NB)]
        KPT = sb("KPT", [64, L], BF16); KPB = [Buf("KP%d" % n) for n in range(NB)]
        C2 = sb("C2", [64, L], F32); S2 = sb("S2", [64, L], F32); TABB = Buf("ropetab")
        INVF = sb("INVF", [64, 1], F32)
        QNT = sb("QNT", [128, L], BF16); QNTB = [Buf("QNT%d" % n) for n in range(NB)]
        QPT = sb("QPT", [64, L], BF16); QPTB = [Buf("QPT%d" % n) for n in range(NB)]
        KNT = sb("KNT", [128, L], BF16); KNTB = [Buf("KNT%d" % n) for n in range(NB)]
        VT = sb("VT", [128, NT, 128], BF16); VTB = [Buf("VT%d" % n) for n in range(NB)]
        ZF = sb("ZF", [128, 2, 512], F32); ZFB = Buf("ZF")
        SQ = sb("SQ", [128, 512], F32); SQB = Buf("SQ")
        RS = sb("RS", [128, 512], F32); RSB = Buf("RS")
        R1 = SQ[0:64, :]; R1B = SQB
        R2 = RS[0:64, :]; R2B = RSB
        PT = [sb("PT%d" % i, [128, 512], BF16) for i in range(5)]; PTB = [Buf("PT%d" % i) for i in range(5)]
        MASK = sb("MASK", [128, 4, 512], BF16); MASKB = Buf("MASK")
        local = [WINB, WQBB, WKVB, QNB, KVNB, TABB, MASKB]

        S.dma("pool", WIN[:], I["mla_w_in"].rearrange("(k p) n -> p k n", p=128), wbuf=WINB)
        S.dma("pool", WQB[:], I["mla_w_qb"].rearrange("(k p) n -> p k n", p=128), wbuf=WQBB)
        S.dma("pool", WKV[:], I["mla_w_kvb"], wbuf=WKVB)
        S.dma("sp", QN[:], I["mla_qn"], wbuf=QNB)
        S.dma("sp", KVN[:], I["mla_kvn"], wbuf=KVNB)
        PI32 = sb("PI32", [64, 512], I32); PI32B = Buf("PI32")
        local.append(PI32B)
        S.dma("sp", INVF[:], I["invf"], wbuf=TABB)
        for k in range(2):
            S.ts("dve", WQB[:, k, :], WQB[:, k, :], QN[:, k:k + 1], None, ALU.mult, None, [WQBB, QNB], [WQBB])
        S.ts("dve", WKV[:, 0:1024], WKV[:, 0:1024], KVN[:, 0:1], None, ALU.mult, None, [WKVB, KVNB], [WKVB])
        S.ts("dve", WKV[:, 1024:2048], WKV[:, 1024:2048], KVN[:, 0:1], None, ALU.mult, None, [WKVB, KVNB], [WKVB])
        for j in range(4):
            S.memset("pool", SQ[:], 1.0, [SQB], [SQB])
            S.op("pool", lambda e, j=j: e.affine_select(out=SQ[:], in_=SQ[:], compare_op=ALU.is_ge, fill=0.0,
                                                        base=-128 * j, pattern=[[1, 512]], channel_multiplier=-1),
                 reads=[SQB], writes=[SQB])
            S.cp("pool", MASK[:, j, :], SQ[:], [SQB], [MASKB])

        def proj_group(pb, lhs_list, rhs_list, reads, M=128):
            nk = len(lhs_list)
            for k in range(nk):
                S.mm(g.PS[pb][0:M, :], lhs_list[k], rhs_list[k], k == 0, k == nk - 1, reads, [g.PSB[pb]])

        def rms_block(srcs_pb, dsts, dstB):
            ntile = len(srcs_pb)
            for i, pb in enumerate(srcs_pb):
                S.cp("act", ZF[:, i, :], g.PS[pb][:], [g.PSB[pb]], [ZFB])
            for i in range(ntile):
                S.actf(SQ[:], ZF[:, i, :], AF.Square, [ZFB], [SQB])
                S.mm(g.PS[6][:], g.ones[:], SQ[:], i == 0, i == ntile - 1, [SQB, g.CB], [g.PSB[6]], signal=True)
            S.actf(RS[:], g.PS[6][:], AF.Sqrt, [g.PSB[6], g.CB], [RSB], bias=g.cst[:, 4:5], scale=1.0 / (128 * ntile))
            S.op("dve", lambda e: e.reciprocal(out=RS[:], in_=RS[:]), reads=[RSB], writes=[RSB])
            for i in range(ntile):
                S.tt("dve", dsts[i], ZF[:, i, :], RS[:], ALU.mult, [ZFB, RSB], [dstB])

        def rope_block(pb_a, pb_b, n, dst, dstB):
            sl_ = slice(n * 512, (n + 1) * 512)
            S.tt("dve", R1, g.PS[pb_a][0:64, :], C2[:, sl_], ALU.mult, [g.PSB[pb_a], TABB], [R1B])
            S.tt("dve", R2, g.PS[pb_b][0:64, :], S2[:, sl_], ALU.mult, [g.PSB[pb_b], TABB], [R2B])
            S.tt("dve", dst[:, sl_], R1, R2, ALU.add, [R1B, R2B], [dstB])

        def rope_tables(n):
            sl_ = slice(n * 512, (n + 1) * 512)
            S.dma("sp", PI32[:], I["pos"][:, sl_], wbuf=PI32B)
            S.cp("dve", C2[:, sl_], PI32[:], [PI32B, TABB], [TABB])
            S.ts("dve", C2[:, sl_], C2[:, sl_], INVF[:, 0:1], None, ALU.mult, None, [TABB], [TABB])
            T_, M_ = ZF[0:64, 0, :], ZF[0:64, 1, :]
            sin_reduced(g, S2[0:32, sl_], C2[0:32, sl_], 0.0, T_[0:32], PI32[0:32, :], M_[0:32], [TABB], [TABB], [ZFB, PI32B], sign=-1.0)
            sin_reduced(g, S2[32:64, sl_], C2[32:64, sl_], 0.0, ZF[32:64, 0, :], PI32[32:64, :], ZF[32:64, 1, :], [TABB], [TABB], [ZFB, PI32B])
            sin_reduced(g, C2[:, sl_], C2[:, sl_], 0.5 * PI, T_, PI32[:], M_, [TABB], [TABB], [ZFB, PI32B])

        for n in range(NB):
            hsl = slice(n * 512, (n + 1) * 512)
            hr = g.HTB[n * 4:(n + 1) * 4]
            rhs = [g.HT[:, k, hsl] for k in range(8)]
            for i in range(2):
                proj_group(i, [WIN[:, k, i * 128:(i + 1) * 128] for k in range(8)], rhs, [WINB] + hr)
            rms_block([0, 1], [CQT[:, 0, hsl], CQT[:, 1, hsl]], CQB[n])
            proj_group(2, [WIN[:, k, 256:384] for k in range(8)], rhs, [WINB] + hr)
            rms_block([2], [CKT[:, hsl]], CKB[n])
            proj_group(3, [WIN[:, k, 384:448] for k in range(8)], rhs, [WINB] + hr, M=64)
            proj_group(4, [WIN[:, k, 448:512] for k in range(8)], rhs, [WINB] + hr, M=64)
            rope_tables(n)
            rope_block(3, 4, n, KPT, KPB[n])

        RSQ = [RS, ZF[:, 0, :]]; RSQB = [RSB, ZFB]
        OT = g.HT
        ptc = 0
        for h in range(8):
            qb = h * 320
            for n in range(NB):
                hsl = slice(n * 512, (n + 1) * 512)
                crhs = [CQT[:, k, hsl] for k in range(2)]
                proj_group(0, [WQB[:, k, qb:qb + 128] for k in range(2)], crhs, [WQBB, CQB[n]])
                S.cp("act", QNT[:, hsl], g.PS[0][:], [g.PSB[0]], [QNTB[n]])
                proj_group(3, [WQB[:, k, qb + 128:qb + 192] for k in range(2)], crhs, [WQBB, CQB[n]], M=64)
                proj_group(4, [WQB[:, k, qb + 192:qb + 256] for k in range(2)], crhs, [WQBB, CQB[n]], M=64)
                rope_block(3, 4, n, QPT, QPTB[n])
                proj_group(1, [WKV[:, h * 128:(h + 1) * 128]], [CKT[:, hsl]], [WKVB, CKB[n]])
                S.cp("act", KNT[:, hsl], g.PS[1][:], [g.PSB[1]], [KNTB[n]])
                for tt in range(4):
                    t = n * 4 + tt
                    S.mm(g.PS[2][:, tt * 128:(tt + 1) * 128], CKT[:, t * 128:(t + 1) * 128], WKV[:, 1024 + h * 128:1024 + (h + 1) * 128], True, True,
                         [WKVB, CKB[n]], [g.PSB[2]], signal=(tt == 3))
                S.cp("act", VT[:, n * 4:(n + 1) * 4, :], g.PS[2][:].rearrange("p (t d) -> p t d", t=4), [g.PSB[2]], [VTB[n]])
            iters = [(qi, kt) for qi in range(NB) for kt in range(4 * (qi + 1))]
            SBK = [3, 4, 0, 1]
            LOOK = 3

            def issue_scores(idx):
                qi, kt = iters[idx]
                spb = SBK[idx % 4]
                ksl = slice(kt * 128, (kt + 1) * 128)
                qsl = slice(qi * 512, (qi + 1) * 512)
                kn = kt // 4
                S.mm(g.PS[spb][:], KNT[:, ksl], QNT[:, qsl], True, False, [KNTB[kn], QNTB[qi]], [g.PSB[spb]], signal=False)
                S.mm(g.PS[spb][:], KPT[:, ksl], QPT[:, qsl], False, True, [KPB[kn], QPTB[qi]], [g.PSB[spb]], signal=True)

            for i_ in range(min(LOOK, len(iters))):
                issue_scores(i_)
            for idx, (qi, kt) in enumerate(iters):
                if idx + LOOK < len(iters):
                    issue_scores(idx + LOOK)
                spb = SBK[idx % 4]
                qsl = slice(qi * 512, (qi + 1) * 512)
                nkt = 4 * (qi + 1)
                kn = kt // 4
                ob = 5 + (qi % 2)
                pt, ptb = PT[ptc % 5], PTB[ptc % 5]
                ptc += 1
                S.actf(pt[:], g.PS[spb][:], AF.Exp, [g.PSB[spb], g.CB], [ptb], bias=g.cst[:, 5:6], scale=SCALE)
                if kt >= 4 * qi:
                    S.tt("dve", pt[:], pt[:], MASK[:, kt - 4 * qi, :], ALU.mult, [ptb, MASKB], [ptb])
                sbk = 7 if qi % 2 == 0 else 2
                S.mm(g.PS[ob][:], VT[:, kt, :], pt[:], kt == 0, kt == nkt - 1, [VTB[kn], ptb], [g.PSB[ob]], signal=False)
                S.mm(g.PS[sbk][:], g.onesb[:], pt[:], kt == 0, kt == nkt - 1, [ptb, g.CB], [g.PSB[sbk]], signal=True)
                if kt == nkt - 1:
                    rs, rsb = RSQ[qi % 2], RSQB[qi % 2]
                    S.cp("act", rs[:], g.PS[sbk][:], [g.PSB[sbk]], [rsb])
                    S.op("dve", lambda e, rs=rs: e.reciprocal(out=rs[:], in_=rs[:]), reads=[rsb], writes=[rsb])
                    S.tt("dve", OT[:, h, qsl], g.PS[ob][:], rs[:], ALU.mult, [g.PSB[ob], rsb], g.HTB[qi * 4:(qi + 1) * 4])
        S.barrier()
        S.emit()
        S.release(local)

    with contextlib.ExitStack() as ph:
        WO = g.sb(ph, "WO", [128, 8, D], BF16); WOB = Buf("WO")
        load_ln(g, ph)
        S.dma("pool", WO[:], I["mla_w_o"].rearrange("(k p) n -> p k n", p=128), wbuf=WOB)
        for k in range(8):
            S.tt("dve", WO[:, k, :], WO[:, k, :], g.MOD[:, 2, :], ALU.mult, [WOB, g.MODB], [WOB])
        for t in range(NT):
            pbs = (2 * (t % 2), 2 * (t % 2) + 1)
            for hh in range(2):
                for k in range(8):
                    S.mm(g.PS[pbs[hh]][:], g.HT[:, k, t * 128:(t + 1) * 128], WO[:, k, hh * 512:(hh + 1) * 512], k == 0, k == 7,
                         [g.HTB[t], WOB], [g.PSB[pbs[hh]]])
            mixer_epilogue_tile(g, t, pbs)
        ln_flush(g)
        S.barrier()
        S.emit()
        S.release([WOB, g.LNGB, g.LNBB])


def s5_phase(g):
    S, I = g.S, g.I
    NSTG = 11
    with contextlib.ExitStack() as s5:
        UT = g.sb(s5, "UT", [128, 8, L], BF16)
        UTB = [Buf("UT%d" % n) for n in range(NB)]
        DCOL = g.sb(s5, "DCOL", [128, 8], F32); BGLU = g.sb(s5, "BGLU", [128, 8], F32); SMB = Buf("s5small")
        DREP = g.sb(s5, "DREP", [128, 32], F32)
        ab = s5.enter_context(contextlib.ExitStack())
        BBR = g.sb(ab, "BBR", [128, 32, 32], F32); BBI = g.sb(ab, "BBI", [128, 32, 32], F32); BBB = Buf("BB")
        CRE = g.sb(ab, "CRE", [128, 32, 32], BF16); CIM = g.sb(ab, "CIM", [128, 32, 32], BF16); CCB = Buf("CC")
        PR = g.sb(ab, "PR", [128, NSTG, 32], F32); PIm = g.sb(ab, "PIm", [128, NSTG, 32], F32); NPI = g.sb(ab, "NPI", [128, NSTG, 32], F32)
        IAT = g.sb(ab, "IAT", [128, 4, 32], F32)
        ER = g.sb(ab, "ER", [128, 32, 16], F32); EI = g.sb(ab, "EI", [128, 32, 16], F32)
        FR = g.sb(ab, "FR", [128, 32, 16], F32); FI = g.sb(ab, "FI", [128, 32, 16], F32)
        POWB = Buf("POW")

        with contextlib.ExitStack() as ph:
            sb = lambda name, shape, dt: g.sb(ph, name, shape, dt)
            WIN = sb("SWIN", [128, 8, D], BF16); WINB = Buf("SWIN")
            BRE = sb("BRE", [128, 32, 32], F32); BIM = sb("BIM", [128, 32, 32], F32); BRB = Buf("BRE")
            PRM = sb("PRM", [128, 16, 32], F32); PB_ = Buf("PRM")
            TB = sb("TB", [128, 2, 32], F32); TBB = Buf("TB")
            S.dma("pool", WIN[:], I["ssm_w_in"].rearrange("(k p) n -> p k n", p=128), wbuf=WINB)
            S.dma("pool", CRE[:], I["ssm_cre"].rearrange("p (s c) -> p s c", c=32), wbuf=CCB)
            S.dma("pool", CIM[:], I["ssm_cim"].rearrange("p (s c) -> p s c", c=32), wbuf=CCB)
            S.dma("sp", BRE[:], I["ssm_bre"].rearrange("p (s c) -> p s c", c=32), wbuf=BRB)
            S.dma("sp", BIM[:], I["ssm_bim"].rearrange("p (s c) -> p s c", c=32), wbuf=BRB)
            S.dma("sp", PRM[:, 0, :], I["ssm_ldt"], wbuf=PB_)
            S.dma("sp", PRM[:, 1, :], I["ssm_lr"], wbuf=PB_)
            S.dma("sp", PRM[:, 2, :], I["ssm_li"], wbuf=PB_)
            S.dma("sp", DCOL[:], I["ssm_d"], wbuf=SMB)
            S.dma("sp", BGLU[:], I["ssm_bglu"], wbuf=SMB)
            S.dma("sp", DREP[:], I["ssm_drep"], wbuf=SMB)
            P = lambda i: PRM[:, i, :]
            rw = ([PB_], [PB_])
            S.actf(P(3), P(0), AF.Exp, *rw)
            S.tt("dve", P(4), P(1), P(3), ALU.mult, *rw)
            S.tt("dve", P(5), P(2), P(3), ALU.mult, *rw)
            S.actf(P(6), P(4), AF.Exp, *rw)
            def sin_ladder(out, ang, shift):
                S.ts("dve", P(13), ang, shift + 2 * PI, None, ALU.add, None, *rw)
                for mult in (16, 8, 4, 2):
                    S.ts("dve", P(14), P(13), mult * PI, None, ALU.is_ge, None, *rw)
                    S.stt("dve", P(13), P(14), -mult * PI, P(13), ALU.mult, ALU.add, *rw)
                S.actf(out, P(13), AF.Sin, [PB_, g.CB], [PB_], bias=g.cst[:, 1:2], scale=-1.0)
            sin_ladder(P(7), P(5), 0.5 * PI)
            sin_ladder(P(8), P(5), 0.0)
            S.tt("dve", PR[:, 0, :], P(6), P(7), ALU.mult, [PB_], [POWB])
            S.tt("dve", PIm[:, 0, :], P(6), P(8), ALU.mult, [PB_], [POWB])
            S.ts("dve", P(9), PR[:, 0, :], -1.0, None, ALU.add, None, [PB_, POWB], [PB_])
            S.tt("dve", P(13), P(1), P(1), ALU.mult, *rw)
            S.tt("dve", P(14), P(2), P(2), ALU.mult, *rw)
            S.tt("dve", P(10), P(13), P(14), ALU.add, *rw)
            S.op("dve", lambda e: e.reciprocal(out=PRM[:, 10, :], in_=PRM[:, 10, :]), reads=[PB_], writes=[PB_])
            S.tt("dve", P(13), P(9), P(1), ALU.mult, *rw)
            S.tt("dve", P(14), PIm[:, 0, :], P(2), ALU.mult, [PB_, POWB], [PB_])
            S.tt("dve", P(13), P(13), P(14), ALU.add, *rw)
            S.tt("dve", P(11), P(13), P(10), ALU.mult, *rw)
            S.tt("dve", P(13), PIm[:, 0, :], P(1), ALU.mult, [PB_, POWB], [PB_])
            S.tt("dve", P(14), P(9), P(2), ALU.mult, *rw)
            S.tt("dve", P(13), P(13), P(14), ALU.subtract, *rw)
            S.tt("dve", P(12), P(13), P(10), ALU.mult, *rw)
            for k in range(NSTG):
                if k > 0:
                    S.tt("dve", P(13), PR[:, k - 1, :], PR[:, k - 1, :], ALU.mult, [POWB, PB_], [PB_])
                    S.tt("dve", P(14), PIm[:, k - 1, :], PIm[:, k - 1, :], ALU.mult, [POWB, PB_], [PB_])
                    S.tt("dve", PR[:, k, :], P(13), P(14), ALU.subtract, [PB_, POWB], [POWB])
                    S.tt("dve", P(13), PR[:, k - 1, :], PIm[:, k - 1, :], ALU.mult, [POWB, PB_], [PB_])
                    S.ts("dve", PIm[:, k, :], P(13), 2.0, None, ALU.mult, None, [PB_, POWB], [POWB])
                S.ts("dve", NPI[:, k, :], PIm[:, k, :], -1.0, None, ALU.mult, None, [POWB], [POWB])
            for s in range(32):
                cr, ci = PRM[:, 11, s:s + 1], PRM[:, 12, s:s + 1]
                S.ts("dve", TB[:, 0, :], BIM[:, s, :], ci, None, ALU.mult, None, [BRB, PB_, TBB], [TBB])
                S.ts("dve", TB[:, 1, :], BRE[:, s, :], ci, None, ALU.mult, None, [BRB, PB_, TBB], [TBB])
                S.stt("dve", TB[:, 0, :], BRE[:, s, :], cr, TB[:, 0, :], ALU.mult, ALU.subtract, [BRB, PB_, TBB], [TBB])
                S.stt("dve", BBI[:, s, :], BIM[:, s, :], cr, TB[:, 1, :], ALU.mult, ALU.add, [BRB, PB_, TBB], [BBB])
                S.cp("dve", BBR[:, s, :], TB[:, 0, :], [TBB], [BBB])
            S.tt("dve", P(13), PR[:, 4, :], PR[:, 4, :], ALU.mult, [POWB, PB_], [PB_])
            S.tt("dve", P(14), PIm[:, 4, :], PIm[:, 4, :], ALU.mult, [POWB, PB_], [PB_])
            S.tt("dve", P(13), P(13), P(14), ALU.add, *rw)
            S.op("dve", lambda e: e.reciprocal(out=PRM[:, 13, :], in_=PRM[:, 13, :]), reads=[PB_], writes=[PB_])
            S.tt("dve", IAT[:, 0, :], PR[:, 4, :], P(13), ALU.mult, [POWB, PB_], [POWB])
            S.tt("dve", IAT[:, 1, :], NPI[:, 4, :], P(13), ALU.mult, [POWB, PB_], [POWB])
            S.ts("dve", IAT[:, 2, :], IAT[:, 0, :], -1.0, None, ALU.mult, None, [POWB], [POWB])
            S.ts("dve", IAT[:, 3, :], IAT[:, 1, :], -1.0, None, ALU.mult, None, [POWB], [POWB])
            TT = sb("TT", [128, 2, 256], F32); TTB = Buf("TT")

            def cmul_bc(dr, di, sr, si, k, w):
                pr = PR[:, k, :].unsqueeze(2).to_broadcast([128, 32, w])
                pi_ = PIm[:, k, :].unsqueeze(2).to_broadcast([128, 32, w])
                t1 = TT[:, 0, 0:32 * w].rearrange("p (s w) -> p s w", w=w)
                t2 = TT[:, 1, 0:32 * w].rearrange("p (s w) -> p s w", w=w)
                rwt = ([POWB, TTB], [TTB])
                S.tt("dve", t1, sr, pr, ALU.mult, *rwt)
                S.tt("dve", t2, si, pi_, ALU.mult, *rwt)
                S.tt("dve", dr, t1, t2, ALU.subtract, [TTB, POWB], [POWB])
                S.tt("dve", t1, sr, pi_, ALU.mult, *rwt)
                S.tt("dve", t2, si, pr, ALU.mult, *rwt)
                S.tt("dve", di, t1, t2, ALU.add, [TTB, POWB], [POWB])
            S.memset("pool", ER[:, :, 15:16], 1.0, [POWB], [POWB])
            S.memset("pool", EI[:, :, 15:16], 0.0, [POWB], [POWB])
            S.cp("dve", ER[:, :, 14:15], PR[:, 0, :].unsqueeze(2), [POWB], [POWB])
            S.cp("dve", EI[:, :, 14:15], PIm[:, 0, :].unsqueeze(2), [POWB], [POWB])
            cmul_bc(ER[:, :, 12:14], EI[:, :, 12:14], ER[:, :, 14:16], EI[:, :, 14:16], 1, 2)
            cmul_bc(ER[:, :, 8:12], EI[:, :, 8:12], ER[:, :, 12:16], EI[:, :, 12:16], 2, 4)
            cmul_bc(ER[:, :, 0:8], EI[:, :, 0:8], ER[:, :, 8:16], EI[:, :, 8:16], 3, 8)
            S.cp("dve", FR[:, :, 0:1], PR[:, 0, :].unsqueeze(2), [POWB], [POWB])
            S.cp("dve", FI[:, :, 0:1], PIm[:, 0, :].unsqueeze(2), [POWB], [POWB])
            S.cp("dve", FR[:, :, 1:2], PR[:, 1, :].unsqueeze(2), [POWB], [POWB])
            S.cp("dve", FI[:, :, 1:2], PIm[:, 1, :].unsqueeze(2), [POWB], [POWB])
            cmul_bc(FR[:, :, 2:4], FI[:, :, 2:4], FR[:, :, 0:2], FI[:, :, 0:2], 1, 2)
            cmul_bc(FR[:, :, 4:8], FI[:, :, 4:8], FR[:, :, 0:4], FI[:, :, 0:4], 2, 4)
            cmul_bc(FR[:, :, 8:16], FI[:, :, 8:16], FR[:, :, 0:8], FI[:, :, 0:8], 3, 8)
            for n in range(NB):
                hsl = slice(n * 512, (n + 1) * 512)
                for m in range(8):
                    pb = m % 2
                    for k in range(8):
                        S.mm(g.PS[pb][:], WIN[:, k, m * 128:(m + 1) * 128], g.HT[:, k, hsl], k == 0, k == 7,
                             [WINB] + g.HTB[n * 4:(n + 1) * 4], [g.PSB[pb]])
                    S.cp("act", UT[:, m, hsl], g.PS[pb][:], [g.PSB[pb]], [UTB[n]])
            S.barrier()
            S.emit()
            S.release([WINB, BRB, PB_, CCB, SMB])

        with contextlib.ExitStack() as ph:
            sb = lambda name, shape, dt: g.sb(ph, name, shape, dt)
            NCH = L // 16
            SEL = sb("SEL", [128, 16, 128], BF16); SELT = sb("SELT", [128, 16, 128], BF16); SELB = Buf("SEL")
            TRI = sb("TRI", [128, 128], BF16); MSKB = Buf("TRI")
            CPW = [sb("CPW%d" % i, [128, 512], F32) for i in range(2)]; CPWB = Buf("CPW")
            BPb = [sb("BPb%d" % i, [128, 512], BF16) for i in range(2)]; BPbB = Buf("BPb")
            CW2 = [sb("CW2%d" % i, [128, 512], BF16) for i in range(2)]; CW2B = Buf("CW2")
            CPb = [sb("CPb%d" % i, [128, 512], BF16) for i in range(2)]; CPbB = Buf("CPb")
            M2 = sb("M2", [128, 8, 128], BF16); M2B = Buf("M2")
            M1 = sb("M1", [128, 4, 512], BF16); M1B = Buf("M1")
            DG = sb("DG", [128, 128], BF16); DGB = Buf("DG")
            UG = sb("UG", [128, 512], BF16); UGB = Buf("UG")
            GG = sb("GG", [128, 4, 512], BF16); GGB = [Buf("GG%d" % j) for j in range(4)]
            SA = [sb("SA%d" % i, [128, NCH], F32) for i in range(2)]
            SBf = [sb("SBf%d" % i, [128, NCH], F32) for i in range(2)]
            SAB = [Buf("SA0"), Buf("SA1")]; SBB = [Buf("SB0"), Buf("SB1")]
            XS = [sb("XS%d" % i, [128, NCH], BF16) for i in range(2)]; XSB = Buf("XS")
            GA = sb("GA", [128, 512], F32); GAB = Buf("GA")
            GS = sb("GS", [128, 512], F32); GSB = Buf("GS")
            S.memset("pool", SEL[:], 0.0, [], [SELB])
            S.memset("pool", SELT[:], 0.0, [], [SELB])
            for j in range(4):
                for il in range(4):
                    S.cp("dve", SEL[32 * j:32 * j + 32, j * 4 + il, 32 * il:32 * il + 32], g.ident[32 * j:32 * j + 32, 32 * j:32 * j + 32], [g.CB, SELB], [SELB])
                    S.cp("dve", SELT[32 * il:32 * il + 32, j * 4 + il, 32 * j:32 * j + 32], g.ident[32 * il:32 * il + 32, 32 * il:32 * il + 32], [g.CB, SELB], [SELB])
            S.memset("pool", GA[:, 0:128], 1.0, [GAB], [GAB])
            S.op("pool", lambda e: e.affine_select(out=GA[:, 0:128].rearrange("p (j c) -> p j c", c=32), in_=GA[:, 0:128].rearrange("p (j c) -> p j c", c=32),
                                                   compare_op=ALU.is_ge, fill=0.0, base=31, pattern=[[32, 4], [0, 32]], channel_multiplier=-1),
                 reads=[GAB], writes=[GAB])
            S.cp("pool", TRI[:], GA[:, 0:128], [GAB], [MSKB])
            for i in range(2):
                S.memset("pool", XS[i][:, 0:1], 0.0, [], [XSB])

            def cmul(dst, src, lo_d, lo_s, width, k, s, reads, writes):
                pr, pi_, npi = PR[:, k, s:s + 1], PIm[:, k, s:s + 1], NPI[:, k, s:s + 1]
                d0, d1 = dst[0][:, lo_d:lo_d + width], dst[1][:, lo_d:lo_d + width]
                s0, s1 = src[0][:, lo_s:lo_s + width], src[1][:, lo_s:lo_s + width]
                S.ts("dve", d0, s0, pr, None, ALU.mult, None, reads, writes)
                S.stt("dve", d0, s1, npi, d0, ALU.mult, ALU.add, reads, writes)
                S.ts("dve", d1, s1, pr, None, ALU.mult, None, reads, writes)
                S.stt("dve", d1, s0, pi_, d1, ALU.mult, ALU.add, reads, writes)

            gT = g.HT
            pending = []
            for ct in range(8):
                for j in range(4):
                    s = 4 * ct + j
                    for il in range(4):
                        S.mm(g.PS[2][:], SEL[:, j * 4 + il, :], UT[:, ct, il::4], il == 0, il == 3, [SELB] + UTB, [g.PSB[2]])
                    S.cp("act", UG[:], g.PS[2][:], [g.PSB[2]], [UGB])
                    v3 = lambda ap: ap.rearrange("p (i c) -> p i c", c=32)
                    bc_i = lambda tab: tab[:, s, :].unsqueeze(2).to_broadcast([128, 16, 32])
                    bc_c = lambda tab: tab[:, s, :].unsqueeze(1).to_broadcast([128, 16, 32])
                    ga3, gs3 = v3(GA[:]), v3(GS[:])
                    rwa = ([POWB, BBB, CCB, GAB], [GAB])
                    rws = ([POWB, BBB, CCB, GSB], [GSB])
                    S.tt("dve", ga3, bc_i(ER), bc_c(BBR), ALU.mult, *rwa)
                    S.tt("dve", gs3, bc_i(EI), bc_c(BBI), ALU.mult, *rws)
                    S.tt("dve", BPb[0][:], GA[:], GS[:], ALU.subtract, [GAB, GSB], [BPbB])
                    S.tt("dve", ga3, bc_i(ER), bc_c(BBI), ALU.mult, *rwa)
                    S.tt("dve", gs3, bc_i(EI), bc_c(BBR), ALU.mult, *rws)
                    S.tt("dve", BPb[1][:], GA[:], GS[:], ALU.add, [GAB, GSB], [BPbB])
                    pst = g.PS[6][:].bitcast(BF16)
                    for r in range(2):
                        for kt in range(4):
                            S.tr(pst[:, (r * 4 + kt) * 128:(r * 4 + kt + 1) * 128], BPb[r][:, kt * 128:(kt + 1) * 128], g.ident[:], [BPbB, g.CB], [g.PSB[6]],
                                 signal=(r == 1 and kt == 3))
                    S.cp("act", M2[:], pst.rearrange("p (k n) -> p k n", k=8), [g.PSB[6]], [M2B])
                    for r in range(2):
                        for kt in range(4):
                            S.mm(g.PS[3][:, r * NCH:(r + 1) * NCH], M2[:, r * 4 + kt, :], UG[:, kt::4], kt == 0, kt == 3, [M2B, UGB], [g.PSB[3]])
                    S.cp("act", SA[0][:], g.PS[3][:, 0:NCH], [g.PSB[3]], [SAB[0]])
                    S.cp("act", SA[1][:], g.PS[3][:, NCH:2 * NCH], [g.PSB[3]], [SAB[1]])
                    S.tt("dve", ga3, bc_i(FR), bc_c(CRE), ALU.mult, *rwa)
                    S.tt("dve", gs3, bc_i(FI), bc_c(CIM), ALU.mult, *rws)
                    S.tt("dve", CPW[0][:], GA[:], GS[:], ALU.subtract, [GAB, GSB], [CPWB])
                    S.tt("dve", ga3, bc_i(FR), bc_c(CIM), ALU.mult, *rwa)
                    S.tt("dve", gs3, bc_i(FI), bc_c(CRE), ALU.mult, *rws)
                    S.tt("dve", CPW[1][:], GA[:], GS[:], ALU.add, [GAB, GSB], [CPWB])
                    S.cp("act", CPb[0][:], CPW[0][:], [CPWB], [CPbB])
                    S.actf(CPb[1][:], CPW[1][:], AF.Copy, [CPWB], [CPbB], scale=-1.0)
                    iar, iai, niar, niai = IAT[:, 0, s:s + 1], IAT[:, 1, s:s + 1], IAT[:, 2, s:s + 1], IAT[:, 3, s:s + 1]
                    rw2 = ([CPWB, POWB, GAB], [GAB])
                    S.ts("dve", GA[:], CPW[1][:], niai, None, ALU.mult, None, *rw2)
                    S.stt("dve", CW2[0][:], CPW[0][:], iar, GA[:], ALU.mult, ALU.add, [CPWB, POWB, GAB], [CW2B])
                    S.ts("dve", GA[:], CPW[1][:], niar, None, ALU.mult, None, *rw2)
                    S.stt("dve", CW2[1][:], CPW[0][:], niai, GA[:], ALU.mult, ALU.add, [CPWB, POWB, GAB], [CW2B])
                    S.ts("dve", DG[:], g.identf[:], DREP[:, s:s + 1], None, ALU.mult, None, [g.CB, SMB], [DGB])
                    if pending:
                        pending.pop()()
                    m1_evac = []
                    for kt in range(4):
                        pb = kt % 2
                        c0 = 128 * kt
                        S.mm(g.PS[pb][:, c0:512], BPb[0][:, c0:c0 + 128], CW2[0][:, c0:512], True, False, [BPbB, CW2B], [g.PSB[pb]], signal=False)
                        S.mm(g.PS[pb][:, c0:512], BPb[1][:, c0:c0 + 128], CW2[1][:, c0:512], False, True, [BPbB, CW2B], [g.PSB[pb]], signal=True)
                        if kt < 2:
                            S.tt("dve", M1[:, kt, c0:c0 + 128], g.PS[pb][:, c0:c0 + 128], TRI[:], ALU.mult, [g.PSB[pb], MSKB], [M1B])
                            S.cp("act", M1[:, kt, c0 + 128:512], g.PS[pb][:, c0 + 128:512], [g.PSB[pb]], [M1B])
                        else:
                            def ev(kt=kt, pb=pb, c0=c0):
                                S.tt("dve", M1[:, kt, c0:c0 + 128], g.PS[pb][:, c0:c0 + 128], TRI[:], ALU.mult, [g.PSB[pb], MSKB], [M1B])
                                if kt < 3:
                                    S.cp("act", M1[:, kt, c0 + 128:512], g.PS[pb][:, c0 + 128:512], [g.PSB[pb]], [M1B])
                            m1_evac.append(ev)
                    src, dst, srcB, dstB = SA, SBf, SAB, SBB
                    for k in range(7):
                        sh = 1 << k
                        pr, pi_, npi = PR[:, k + 4, s:s + 1], PIm[:, k + 4, s:s + 1], NPI[:, k + 4, s:s + 1]
                        S.stt("dve", dst[0][:, sh:], src[0][:, :NCH - sh], pr, src[0][:, sh:], ALU.mult, ALU.add, [srcB[0], POWB], [dstB[0]])
                        S.stt("dve", dst[1][:, sh:], src[1][:, :NCH - sh], pr, src[1][:, sh:], ALU.mult, ALU.add, [srcB[1], POWB], [dstB[1]])
                        S.stt("dve", dst[0][:, sh:], src[1][:, :NCH - sh], npi, dst[0][:, sh:], ALU.mult, ALU.add, [srcB[1], dstB[0], POWB], [dstB[0]])
                        S.stt("dve", dst[1][:, sh:], src[0][:, :NCH - sh], pi_, dst[1][:, sh:], ALU.mult, ALU.add, [srcB[0], dstB[1], POWB], [dstB[1]])
                        S.cp("dve", dst[0][:, :sh], src[0][:, :sh], [srcB[0]], [dstB[0]])
                        S.cp("dve", dst[1][:, :sh], src[1][:, :sh], [srcB[1]], [dstB[1]])
                        if m1_evac:
                            m1_evac.pop(0)()
                        src, dst, srcB, dstB = dst, src, dstB, srcB
                    S.cp("act", XS[0][:, 1:NCH], src[0][:, 0:NCH - 1], [srcB[0]], [XSB])
                    S.cp("act", XS[1][:, 1:NCH], src[1][:, 0:NCH - 1], [srcB[1]], [XSB])
                    yb = 4 + (s % 2)
                    for jt in range(4):
                        osl = g.PS[yb][:, jt * NCH:(jt + 1) * NCH]
                        first = True
                        for kt in range(jt + 1):
                            S.mm(osl, M1[:, kt, jt * 128:(jt + 1) * 128], UG[:, kt::4], first, False, [M1B, UGB], [g.PSB[yb]], signal=False)
                            first = False
                        S.mm(osl, DG[:], UG[:, jt::4], False, False, [DGB, UGB], [g.PSB[yb]], signal=False)
                        S.mm(osl, CPb[0][:, jt * 128:(jt + 1) * 128], XS[0][:], False, False, [CPbB, XSB], [g.PSB[yb]], signal=False)
                        S.mm(osl, CPb[1][:, jt * 128:(jt + 1) * 128], XS[1][:], False, True, [CPbB, XSB], [g.PSB[yb]], signal=True)

                    def gelu(yb=yb, j=j):
                        ybuf = g.PSB[yb]
                        yp = g.PS[yb][:]
                        S.actf(GA[:], yp, AF.Square, [ybuf], [GAB])
                        S.ts("dve", GA[:], GA[:], 0.044715, 1.0, ALU.mult, ALU.add, [GAB], [GAB])
                        S.tt("dve", GA[:], GA[:], yp, ALU.mult, [GAB, ybuf], [GAB])
                        S.actf(GS[:], GA[:], AF.Sigmoid, [GAB], [GSB], scale=1.5957691216057308)
                        S.tt("dve", GG[:, j, :].rearrange("p (n jt) -> p jt n", jt=4), GS[:].rearrange("p (jt n) -> p jt n", jt=4),
                             yp.rearrange("p (jt n) -> p jt n", jt=4), ALU.mult, [GSB, ybuf], [GGB[j]])
                    pending.append(gelu)
                if pending:
                    pending.pop()()
                for jl in range(4):
                    pb = jl % 2
                    for j in range(4):
                        S.mm(g.PS[pb][:], SELT[:, j * 4 + jl, :], GG[:, j, :], j == 0, j == 3, [SELB, GGB[j]], [g.PSB[pb]])
                    S.cp("act", gT[:, ct, jl::4], g.PS[pb][:], [g.PSB[pb]], g.HTB)
            S.barrier()
            S.emit()
            S.release([BBB, POWB])
        ab.close()

        with contextlib.ExitStack() as ph:
            sb = lambda name, shape, dt: g.sb(ph, name, shape, dt)
            WG = sb("WG", [128, 8, D], BF16); WGB = Buf("WG")
            WOU = sb("WOU", [128, 8, D], BF16); WOUB = Buf("WOU")
            load_ln(g, ph)
            SG = [sb("SG%d" % i, [128, 512], F32) for i in range(2)]; SGB = [Buf("SG%d" % i) for i in range(2)]
            S.dma("pool", WG[:], I["ssm_w_glu"].rearrange("(k p) n -> p k n", p=128), wbuf=WGB)
            S.dma("pool", WOU[:], I["ssm_w_out"].rearrange("(k p) n -> p k n", p=128), wbuf=WOUB)
            for k in range(8):
                S.tt("dve", WOU[:, k, :], WOU[:, k, :], g.MOD[:, 2, :], ALU.mult, [WOUB, g.MODB], [WOUB])
            gT = g.HT
            ZT = UT
            ZTB = [Buf("ZT%d" % n) for n in range(NB)]
            c_ = 0
            for n in range(NB):
                hsl = slice(n * 512, (n + 1) * 512)
                for m in range(8):
                    pb = c_ % 2
                    sg, sgb = SG[c_ % 2], SGB[c_ % 2]
                    c_ += 1
                    for k in range(8):
                        S.mm(g.PS[pb][:], WG[:, k, m * 128:(m + 1) * 128], gT[:, k, hsl], k == 0, k == 7,
                             [WGB] + g.HTB[n * 4:(n + 1) * 4], [g.PSB[pb]])
                    S.actf(sg[:], g.PS[pb][:], AF.Sigmoid, [g.PSB[pb], SMB], [sgb], bias=BGLU[:, m:m + 1], scale=1.0)
                    S.tt("dve", ZT[:, m, hsl], gT[:, m, hsl], sg[:], ALU.mult, [sgb] + g.HTB[n * 4:(n + 1) * 4] + [UTB[n]], [ZTB[n], UTB[n]])
            for t in range(NT):
                pbs = (2 + 2 * (t % 2), 3 + 2 * (t % 2))
                for hh in range(2):
                    for k in range(8):
                        S.mm(g.PS[pbs[hh]][:], ZT[:, k, t * 128:(t + 1) * 128], WOU[:, k, hh * 512:(hh + 1) * 512], k == 0, k == 7,
                             [ZTB[t // 4], WOUB], [g.PSB[pbs[hh]]])
                mixer_epilogue_tile(g, t, pbs)
            ln_flush(g)
            S.barrier()
            S.emit()
            S.release([WGB, WOUB, g.LNGB, g.LNBB])


def _prep_inputs(inp, b):
    f = np.float32
    m = {}
    m["x"] = np.ascontiguousarray(inp["x"][b], dtype=f)
    m["c"] = np.ascontiguousarray(inp["c"][b].reshape(8, 128).T, dtype=f)
    m["pos"] = np.ascontiguousarray(np.broadcast_to(inp["positions"][b][None, :], (64, L)), dtype=np.int32)
    p = np.arange(64) % 32
    m["invf"] = (10000.0 ** (-(2.0 * p) / 64.0)).astype(f).reshape(64, 1)
    w_in = inp["mla_w_in"][0]
    kpe = w_in[:, 384:448]
    m["mla_w_in"] = np.ascontiguousarray(np.concatenate([w_in, kpe[:, 32:], kpe[:, :32]], axis=1), dtype=f)
    m["mla_qn"] = np.ascontiguousarray(inp["mla_q_norm"][0].reshape(2, 128).T, dtype=f)
    wq = inp["mla_w_qb"][0].reshape(256, 8, 192)
    wq_ext = np.zeros((256, 8, 320), f)
    wq_ext[:, :, 0:192] = wq
    wq_ext[:, :, 192:224] = wq[:, :, 160:192]
    wq_ext[:, :, 224:256] = wq[:, :, 128:160]
    m["mla_w_qb"] = wq_ext.reshape(256, 2560)
    m["mla_kvn"] = np.ascontiguousarray(inp["mla_kv_norm"][0].reshape(128, 1), dtype=f)
    wkv = inp["mla_w_kvb"][0].reshape(128, 8, 256)
    m["mla_w_kvb"] = np.ascontiguousarray(np.concatenate([wkv[:, :, :128].reshape(128, 1024), wkv[:, :, 128:].reshape(128, 1024)], axis=1), dtype=f)
    m["mla_w_o"] = np.ascontiguousarray(inp["mla_w_o"][0], dtype=f)
    m["ssm_w_in"] = np.ascontiguousarray(inp["ssm_w_in"][0], dtype=f)

    def sp(a):
        return np.ascontiguousarray(a.reshape(32, 2, 64).transpose(1, 2, 0).reshape(128, 32), dtype=f)
    m["ssm_ldt"] = sp(np.repeat(inp["ssm_log_dt"][0][:, None], 64, axis=1))
    m["ssm_lr"] = sp(inp["ssm_a_re"][0])
    m["ssm_li"] = sp(inp["ssm_a_im"][0])

    def bd(a):
        o = np.zeros((2, 64, 32, 2, 16), f)
        a4 = a.reshape(32, 2, 64, 16)
        for gl in range(2):
            o[gl, :, :, gl, :] = a4[:, gl].transpose(1, 0, 2)
        return o.reshape(128, 32 * 32)
    m["ssm_bre"] = bd(inp["ssm_b_re"][0])
    m["ssm_bim"] = bd(inp["ssm_b_im"][0])
    m["ssm_cre"] = bd(inp["ssm_c_re"][0].transpose(0, 2, 1))
    m["ssm_cim"] = bd(inp["ssm_c_im"][0].transpose(0, 2, 1))
    m["ssm_d"] = np.ascontiguousarray(inp["ssm_d"][0].reshape(8, 128).T, dtype=f)
    m["ssm_drep"] = np.ascontiguousarray(np.tile(inp["ssm_d"][0].reshape(32, 32).T, (4, 1)), dtype=f)
    m["ssm_w_glu"] = np.ascontiguousarray(inp["ssm_w_glu"][0], dtype=f)
    m["ssm_bglu"] = np.ascontiguousarray(inp["ssm_b_glu"][0].reshape(8, 128).T, dtype=f)
    m["ssm_w_out"] = np.ascontiguousarray(inp["ssm_w_out"][0], dtype=f)
    m["mlp_w1"] = np.ascontiguousarray(inp["mlp_w1"], dtype=f)
    m["mlp_b1"] = np.ascontiguousarray(inp["mlp_b1"].reshape(2, 32, 128).transpose(0, 2, 1), dtype=f)
    m["mlp_w2"] = np.ascontiguousarray(inp["mlp_w2"], dtype=f)
    m["mlp_b2"] = np.ascontiguousarray(np.broadcast_to(inp["mlp_b2"][:, None, :], (2, 128, D)), dtype=f)
    mw = np.stack([inp["mod_mix_w"][0], inp["mod_ffn_w"][0], inp["mod_mix_w"][1], inp["mod_ffn_w"][1]])
    mb = np.stack([inp["mod_mix_b"][0], inp["mod_ffn_b"][0], inp["mod_mix_b"][1], inp["mod_ffn_b"][1]])
    m["mod_w"] = np.ascontiguousarray(mw, dtype=f)
    m["mod_b"] = np.ascontiguousarray(np.broadcast_to(mb[:, None, :], (4, 128, 3 * D)), dtype=f)
    lg = np.stack([inp["ln_mix_g"][0], inp["ln_ffn_g"][0], inp["ln_mix_g"][1], inp["ln_ffn_g"][1]])
    lb = np.stack([inp["ln_mix_b"][0], inp["ln_ffn_b"][0], inp["ln_mix_b"][1], inp["ln_ffn_b"][1]])
    m["ln_g"] = np.ascontiguousarray(np.broadcast_to(lg[:, None, :], (4, 128, D)), dtype=f)
    m["ln_b"] = np.ascontiguousarray(np.broadcast_to(lb[:, None, :], (4, 128, D)), dtype=f)
    return m


_NC_CACHE = {}


def run(inp, sublayers=(0, 1, 2, 3), trace=False, same_engine_sync=True):
    key = (tuple(sublayers), same_engine_sync)
    if key not in _NC_CACHE:
        _NC_CACHE[key] = build_nc(sublayers, same_engine_sync)
    nc = _NC_CACHE[key]
    shared = None
    in_maps = []
    for b in range(8):
        m = _prep_inputs(inp, b)
        if shared is None:
            shared = m
        else:
            for k in m:
                if k not in ("x", "c", "pos"):
                    m[k] = shared[k]
        in_maps.append(m)
    res = run_bass_kernel_spmd(nc, in_maps, core_ids=list(range(8)), trace=trace)
    outp = np.stack([np.asarray(r["out"], dtype=np.float32) for r in res.results], axis=0)
    return outp, res


def kernel(**inputs):
    inp = {k: np.asarray(v) for k, v in inputs.items()}
    outp, _ = run(inp)
    return outp
```

```python
import contextlib
import math
import numpy as np
import concourse.bass as bass
import concourse.mybir as mybir
from concourse.bass_utils import run_bass_kernel_spmd

F32 = mybir.dt.float32
BF16 = mybir.dt.bfloat16
I32 = mybir.dt.int32
ALU = mybir.AluOpType
AF = mybir.ActivationFunctionType

D = 1024
L = 2048
NT = L // 128
NB = L // 512
DFF = 4096
ALPHA = 4 ** 0.25
LN_EPS = 1e-5
RMS_EPS = 1e-6
PI = math.pi


class Buf:
    __slots__ = ("name", "w", "r", "dsem", "dcnt")

    def __init__(self, name):
        self.name = name
        self.w = {}
        self.r = {}
        self.dsem = None
        self.dcnt = 0


class Sched:
    ENG = ("pe", "act", "dve", "pool", "sp")
    EMAP = {"pe": "tensor", "act": "scalar", "dve": "vector", "pool": "gpsimd", "sp": "sync"}

    def __init__(self, nc, stack, same_engine_sync=True):
        self.nc = nc
        self.stack = stack
        self.prog = {e: [] for e in self.ENG}
        self.sems = {}
        self.cnt = {e: 0 for e in self.ENG}
        self.waited = {e: {} for e in self.ENG}
        self.same_engine_sync = same_engine_sync
        for e in self.ENG:
            self.sems[e] = stack.enter_context(nc.semaphore("s_" + e))
        self.dsem_cnt = {}
        self.dsem_free = []
        self.swdge_sems = set()
        self.out_tokens = []

    def _get_dsem(self, fresh=False):
        if self.dsem_free and not fresh:
            return self.dsem_free.pop()
        key = "d%d" % len(self.dsem_cnt)
        self.sems[key] = self.stack.enter_context(self.nc.semaphore(key))
        self.dsem_cnt[key] = 0
        return key

    def release(self, bufs):
        for b in bufs:
            if b.dsem is not None:
                if b.dsem not in self.swdge_sems:
                    self.dsem_free.append(b.dsem)
                b.dsem = None

    def _deps(self, eng, reads, writes):
        need = {}
        for b in reads:
            for k, v in b.w.items():
                if need.get(k, 0) < v:
                    need[k] = v
        for b in writes:
            for k, v in b.w.items():
                if need.get(k, 0) < v:
                    need[k] = v
            for k, v in b.r.items():
                if need.get(k, 0) < v:
                    need[k] = v
        wd = self.waited[eng]
        for k, v in need.items():
            if k == eng and (eng == "pe" or not self.same_engine_sync):
                continue
            if wd.get(k, 0) >= v:
                continue
            wd[k] = v
            self.prog[eng].append(("wait", k, v))

    def op(self, eng, fn, reads=(), writes=(), signal=True):
        self._deps(eng, reads, writes)
        tok = self.cnt[eng] + 1
        if signal:
            self.cnt[eng] = tok
            self.prog[eng].append(("op", fn, eng, 1))
        else:
            self.prog[eng].append(("op", fn, None, 0))
        for b in reads:
            if b.r.get(eng, 0) < tok:
                b.r[eng] = tok
        for b in writes:
            if b.w.get(eng, 0) < tok:
                b.w[eng] = tok

    def dma(self, eng, out, in_, wbuf=None, rbuf=None, is_output=False):
        reads = [rbuf] if rbuf is not None else []
        writes = [wbuf] if wbuf is not None else []
        self._deps(eng, reads, writes)
        own = wbuf if wbuf is not None else rbuf
        if own.dsem is None:
            own.dsem = self._get_dsem(fresh=(eng == "pool"))
            if eng == "pool":
                self.swdge_sems.add(own.dsem)
        key = own.dsem
        self.dsem_cnt[key] += 16
        val = self.dsem_cnt[key]

        def fn(e, out=out, in_=in_):
            return e.dma_start(out=out, in_=in_)
        self.prog[eng].append(("op", fn, key, 16))
        if wbuf is not None:
            wbuf.w[key] = val
        if rbuf is not None:
            rbuf.r[key] = val
        if is_output:
            self.out_tokens.append((key, val))


    def mm(self, out, lhsT, rhs, start, stop, reads, writes, signal=None):
        if signal is None:
            signal = stop
        self.op("pe", lambda e, o=out, l=lhsT, r=rhs, s=start, t=stop: e.matmul(o, lhsT=l, rhs=r, start=s, stop=t),
                reads, writes, signal)

    def tr(self, out, in_, ident, reads, writes, signal=True):
        self.op("pe", lambda e, o=out, i=in_, d=ident: e.transpose(out=o, in_=i, identity=d), reads, writes, signal)

    def tt(self, eng, out, in0, in1, op, reads, writes):
        self.op(eng, lambda e, o=out, a=in0, b=in1, p=op: e.tensor_tensor(out=o, in0=a, in1=b, op=p), reads, writes)

    def ts(self, eng, out, in0, s1, s2, op0, op1, reads, writes):
        if s2 is None:
            self.op(eng, lambda e, o=out, a=in0, x=s1, p=op0: e.tensor_scalar(out=o, in0=a, scalar1=x, scalar2=None, op0=p), reads, writes)
        else:
            self.op(eng, lambda e, o=out, a=in0, x=s1, y=s2, p=op0, q=op1: e.tensor_scalar(out=o, in0=a, scalar1=x, scalar2=y, op0=p, op1=q), reads, writes)

    def stt(self, eng, out, in0, scalar, in1, op0, op1, reads, writes):
        self.op(eng, lambda e, o=out, a=in0, s=scalar, b=in1, p=op0, q=op1: e.scalar_tensor_tensor(out=o, in0=a, scalar=s, in1=b, op0=p, op1=q), reads, writes)

    def cp(self, eng, out, in_, reads, writes):
        if eng == "act":
            self.op(eng, lambda e, o=out, i=in_: e.copy(out=o, in_=i), reads, writes)
        else:
            self.op(eng, lambda e, o=out, i=in_: e.tensor_copy(out=o, in_=i), reads, writes)

    def actf(self, out, in_, func, reads, writes, bias=None, scale=None):
        if bias is None and scale is None:
            self.op("act", lambda e, o=out, i=in_, f=func: e.activation(out=o, in_=i, func=f), reads, writes)
        elif bias is None:
            self.op("act", lambda e, o=out, i=in_, f=func, s=scale: e.activation(out=o, in_=i, func=f, scale=s), reads, writes)
        else:
            sc = 1.0 if scale is None else scale
            self.op("act", lambda e, o=out, i=in_, f=func, b=bias, s=sc: e.activation(out=o, in_=i, func=f, bias=b, scale=s), reads, writes)

    def memset(self, eng, out, val, reads, writes):
        self.op(eng, lambda e, o=out, v=val: e.memset(o, v), reads, writes)

    def barrier(self):
        for e in self.ENG:
            wd = self.waited[e]
            for f in self.ENG:
                if f == e or self.cnt[f] == 0:
                    continue
                if wd.get(f, 0) < self.cnt[f]:
                    wd[f] = self.cnt[f]
                    self.prog[e].append(("wait", f, self.cnt[f]))
            for k, v in self.dsem_cnt.items():
                if v > 0 and wd.get(k, 0) < v:
                    wd[k] = v
                    self.prog[e].append(("wait", k, v))

    def emit(self):
        nc = self.nc
        with nc.Block() as block:
            for e in self.ENG:
                items = self.prog[e]

                def body(engine, items=items):
                    for it in items:
                        if it[0] == "wait":
                            engine.wait_ge(self.sems[it[1]], it[2])
                        else:
                            ins = it[1](engine)
                            if it[2] is not None:
                                ins.then_inc(self.sems[it[2]], it[3])
                getattr(block, self.EMAP[e])(body)
        self.prog = {e: [] for e in self.ENG}


class Ctx:
    pass


INPUT_SHAPES = {
    "x": ([L, D], F32),
    "c": ([128, 8], F32),
    "pos": ([64, L], I32),
    "invf": ([64, 1], F32),
    "mla_w_in": ([D, 512], F32),
    "mla_qn": ([128, 2], F32),
    "mla_w_qb": ([256, 8 * 320], F32),
    "mla_kvn": ([128, 1], F32),
    "mla_w_kvb": ([128, 2048], F32),
    "mla_w_o": ([1024, 1024], F32),
    "ssm_w_in": ([D, D], F32),
    "ssm_ldt": ([128, 32], F32),
    "ssm_lr": ([128, 32], F32),
    "ssm_li": ([128, 32], F32),
    "ssm_bre": ([128, 32 * 32], F32),
    "ssm_bim": ([128, 32 * 32], F32),
    "ssm_cre": ([128, 32 * 32], F32),
    "ssm_cim": ([128, 32 * 32], F32),
    "ssm_d": ([128, 8], F32),
    "ssm_drep": ([128, 32], F32),
    "ssm_w_glu": ([D, D], F32),
    "ssm_bglu": ([128, 8], F32),
    "ssm_w_out": ([D, D], F32),
    "mlp_w1": ([2, D, DFF], F32),
    "mlp_b1": ([2, 128, 32], F32),
    "mlp_w2": ([2, DFF, D], F32),
    "mlp_b2": ([2, 128, D], F32),
    "mod_w": ([4, D, 3 * D], F32),
    "mod_b": ([4, 128, 3 * D], F32),
    "ln_g": ([4, 128, D], F32),
    "ln_b": ([4, 128, D], F32),
}


def build_nc(sublayers=(0, 1, 2, 3), same_engine_sync=True):
    nc = bass.Bass("TRN2", target_bir_lowering=False)
    I = {k: nc.dram_tensor(k, list(s), d, kind="ExternalInput").ap() for k, (s, d) in INPUT_SHAPES.items()}
    out = nc.dram_tensor("out", [L, D], F32, kind="ExternalOutput").ap()

    with contextlib.ExitStack() as st:
        S = Sched(nc, st, same_engine_sync=same_engine_sync)
        g = Ctx()
        g.nc, g.S, g.I, g.out = nc, S, I, out

        g.uid = 0

        def sb(stack, name, shape, dt):
            g.uid += 1
            return stack.enter_context(nc.sbuf_tensor("%s_%d" % (name, g.uid), list(shape), dt))
        g.sb = sb

        g.X = sb(st, "X", [128, NT, D], F32)
        g.XB = [Buf("X%d" % t) for t in range(NT)]
        g.HT = sb(st, "HT", [128, 8, L], BF16)
        g.HTB = [Buf("HT%d" % t) for t in range(NT)]
        g.MOD = sb(st, "MOD", [128, 3, D], F32)
        g.MODB = Buf("MOD")
        g.ident = sb(st, "ident", [128, 128], BF16)
        g.identf = sb(st, "identf", [128, 128], F32)
        g.ones = sb(st, "ones", [128, 128], F32)
        g.onesb = sb(st, "onesb", [128, 128], BF16)
        g.CS = sb(st, "CS", [128, 8], F32)
        g.cst = sb(st, "cst", [128, 8], F32)
        g.CB = Buf("consts")
        g.CSB = Buf("CS")
        g.LNS = sb(st, "LNS", [128, 4, 16], F32)
        g.LNSB = [Buf("LNS%d" % i) for i in range(4)]
        g.lnctr = 0
        g.PS = [st.enter_context(nc.psum_tensor("psb%d" % i, [128, 512], F32)) for i in range(8)]
        g.PSB = [Buf("psb%d" % i) for i in range(8)]

        S.memset("pool", g.identf[:], 0.0, [], [g.CB])
        S.op("pool", lambda e: e.affine_select(out=g.identf[:], in_=g.identf[:], compare_op=ALU.not_equal, fill=1.0,
                                                base=0, pattern=[[-1, 128]], channel_multiplier=1),
             reads=[g.CB], writes=[g.CB])
        S.memset("pool", g.ones[:], 1.0, [], [g.CB])
        S.memset("pool", g.onesb[:], 1.0, [], [g.CB])
        for col, val in enumerate([-PI, PI, 1.0, LN_EPS, RMS_EPS, 0.0]):
            S.memset("pool", g.cst[:, col:col + 1], val, [], [g.CB])
        S.cp("dve", g.ident[:], g.identf[:], [g.CB], [g.CB])
        xin = I["x"].rearrange("(t p) d -> p t d", p=128)
        for t in range(NT):
            S.dma("sp", g.X[:, t, :], xin[:, t, :], wbuf=g.XB[t])
        S.dma("sp", g.CS[:], I["c"], wbuf=g.CSB)
        S.actf(g.CS[:], g.CS[:], AF.Silu, [g.CSB], [g.CSB])
        g.CSBC = sb(st, "CSBC", [128, 8, 128], F32)
        for k in range(8):
            S.cp("dve", g.CSBC[:, k, :], g.CS[:, k:k + 1].to_broadcast([128, 128]), [g.CSB], [g.CSB])

        for sl in sublayers:
            g.sl = sl
            prologue(g, sl)
            if sl == 0:
                mla_phase(g)
            elif sl == 2:
                s5_phase(g)
            else:
                ffn_phase(g, sl // 2, sl)

        oap = out.rearrange("(t p) d -> p t d", p=128)
        for t in range(NT):
            S.dma("sp", oap[:, t, :], g.X[:, t, :], rbuf=g.XB[t], is_output=True)
        S.barrier()
        S.emit()
    return nc


def sin_reduced(g, out, ang, shift, T, TI, M, reads, writes, tbufs, sign=1.0):
    S = g.S
    rw = (list(reads) + list(tbufs), list(tbufs))
    S.ts("dve", T, ang, shift, 1.0 / (2 * PI), ALU.add, ALU.mult, *rw)
    S.cp("dve", TI, T, *rw)
    S.cp("dve", M, TI, *rw)
    S.tt("dve", T, T, M, ALU.subtract, *rw)
    S.ts("dve", M, T, 0.5, None, ALU.is_gt, None, *rw)
    S.tt("dve", T, T, M, ALU.subtract, *rw)
    S.ts("dve", M, T, -0.5, None, ALU.is_lt, None, *rw)
    S.tt("dve", T, T, M, ALU.add, *rw)
    S.actf(out, T, AF.Sin, list(tbufs), list(writes), scale=sign * 2 * PI)


def prologue(g, sl):
    S, I = g.S, g.I
    with contextlib.ExitStack() as ph:
        WST = [g.sb(ph, "WST%d" % i, [128, 3 * D], F32) for i in range(2)]
        WSTB = [Buf("WST%d" % i) for i in range(2)]
        BROW = g.sb(ph, "BROW", [128, 3 * D], F32)
        BROWB = Buf("BROW")
        TMP = [g.sb(ph, "HTMP%d" % i, [128, D], F32) for i in range(2)]
        TMPB = [Buf("HTMP%d" % i) for i in range(2)]
        HB = [g.sb(ph, "HB%d" % i, [128, D], BF16) for i in range(2)]
        HBB = [Buf("HB%d" % i) for i in range(2)]
        local = WSTB + [BROWB] + TMPB + HBB

        wv = I["mod_w"][sl].rearrange("(k p) n -> p k n", p=128)
        S.dma("sp", BROW[:], I["mod_b"][sl], wbuf=BROWB)
        for k in range(8):
            w = WST[k % 2]
            S.dma("sp", w[:], wv[:, k, :], wbuf=WSTB[k % 2])
            for n in range(6):
                S.mm(g.PS[n][:], g.CSBC[:, k, :], w[:, n * 512:(n + 1) * 512], k == 0, k == 7, [WSTB[k % 2], g.CSB], [g.PSB[n]], signal=(n == 5 or k == 7))
        for n in range(6):
            j, hh = n // 2, n % 2
            addc = 0.0 if j == 0 else 1.0
            S.stt("dve", g.MOD[:, j, hh * 512:(hh + 1) * 512], g.PS[n][:], addc, BROW[:, n * 512:(n + 1) * 512], ALU.add, ALU.add,
                  [g.PSB[n], BROWB], [g.MODB])
        for t in range(NT):
            tm, tmb = TMP[t % 2], TMPB[t % 2]
            hb, hbb = HB[t % 2], HBB[t % 2]
            S.tt("dve", tm[:], g.X[:, t, :], g.MOD[:, 1, :], ALU.mult, [g.XB[t], g.MODB], [tmb])
            S.tt("dve", hb[:], tm[:], g.MOD[:, 0, :], ALU.add, [tmb, g.MODB], [hbb])
            pbi = 2 + (t % 2)
            pst = g.PS[pbi][:].bitcast(BF16)
            for k in range(8):
                S.tr(pst[:, k * 128:(k + 1) * 128], hb[:, k * 128:(k + 1) * 128], g.ident[:], [hbb, g.CB], [g.PSB[pbi]], signal=(k == 7))
            S.cp("act", g.HT[:, :, t * 128:(t + 1) * 128], pst.rearrange("p (k n) -> p k n", k=8), [g.PSB[pbi]], [g.HTB[t]])
        S.barrier()
        S.emit()
        S.release(local)


def load_ln(g, ph):
    g.LNG = g.sb(ph, "LNG", [128, D], F32)
    g.LNGB = Buf("LNG")
    g.LNBt = g.sb(ph, "LNBt", [128, D], F32)
    g.LNBB = Buf("LNB")
    g.S.dma("sp", g.LNG[:], g.I["ln_g"][g.sl], wbuf=g.LNGB)
    g.S.dma("sp", g.LNBt[:], g.I["ln_b"][g.sl], wbuf=g.LNBB)


def layer_norm_tile(g, t, eng2="pool"):
    S = g.S
    i = g.lnctr % 4
    g.lnctr += 1
    st_ = g.LNS[:, i, :]
    sbuf = g.LNSB[i]
    xt = g.X[:, t, :]
    for c in range(2):
        S.op("dve", lambda e, o=st_[:, c * 6:(c + 1) * 6], i_=xt[:, c * 512:(c + 1) * 512]: e.bn_stats(out=o, in_=i_),
             reads=[g.XB[t]], writes=[sbuf])
    S.op("dve", lambda e, o=st_[:, 12:14], i_=st_[:, 0:12].rearrange("p (c s) -> p c s", c=2): e.bn_aggr(out=o, in_=i_), reads=[sbuf], writes=[sbuf])
    S.actf(st_[:, 14:15], st_[:, 13:14], AF.Sqrt, [sbuf, g.CB], [sbuf], bias=g.cst[:, 3:4], scale=1.0)
    LNG, LNGB, LNBt, LNBB = g.LNG, g.LNGB, g.LNBt, g.LNBB

    def apply():
        S.op("dve", lambda e, o=st_[:, 14:15]: e.reciprocal(out=o, in_=o), reads=[sbuf], writes=[sbuf])
        S.stt("dve", xt, xt, st_[:, 12:13], LNG[:], ALU.subtract, ALU.mult, [sbuf, g.XB[t], LNGB], [g.XB[t]])
        S.stt("dve", xt, xt, st_[:, 14:15], LNBt[:], ALU.mult, ALU.add, [sbuf, g.XB[t], LNBB], [g.XB[t]])
    ln_flush(g)
    g.ln_pending = apply


def ln_flush(g):
    p = getattr(g, "ln_pending", None)
    if p is not None:
        g.ln_pending = None
        p()


def mixer_epilogue_tile(g, t, pbanks, TMP=None, TMPB=None):
    S = g.S
    for hh in range(2):
        xs = g.X[:, t, hh * 512:(hh + 1) * 512]
        S.stt("dve", xs, xs, ALPHA, g.PS[pbanks[hh]][:], ALU.mult, ALU.add, [g.XB[t], g.PSB[pbanks[hh]]], [g.XB[t]])
    layer_norm_tile(g, t)


def ffn_phase(g, li, sl):
    S, I = g.S, g.I
    with contextlib.ExitStack() as ph:
        W1 = [g.sb(ph, "W1c%d" % i, [128, 8, 512], BF16) for i in range(2)]
        W1B = [Buf("W1c%d" % i) for i in range(2)]
        W2 = [g.sb(ph, "W2c%d" % i, [128, 4, D], BF16) for i in range(2)]
        W2B = [Buf("W2c%d" % i) for i in range(2)]
        AT = [g.sb(ph, "AT%d" % i, [128, 4, L], BF16) for i in range(2)]
        ATB = [[Buf("AT%d_%d" % (i, n)) for n in range(NB)] for i in range(2)]
        RT = [g.sb(ph, "RT%d" % i, [128, 512], F32) for i in range(2)]
        RTB = [Buf("RT%d" % i) for i in range(2)]
        B1 = g.sb(ph, "B1", [128, 32], F32)
        B1B = Buf("B1")
        B2R = g.sb(ph, "B2R", [128, D], F32)
        B2B = Buf("B2R")
        load_ln(g, ph)
        local = W1B + W2B + RTB + [B1B, B2B] + [b for r in ATB for b in r]
        local_ln = True

        S.dma("sp", B1[:], I["mlp_b1"][li], wbuf=B1B)
        S.dma("sp", B2R[:], I["mlp_b2"][li], wbuf=B2B)
        S.tt("dve", B2R[:], B2R[:], g.MOD[:, 2, :], ALU.mult, [B2B, g.MODB], [B2B])
        for t in range(NT):
            S.stt("dve", g.X[:, t, :], g.X[:, t, :], ALPHA, B2R[:], ALU.mult, ALU.add, [g.XB[t], B2B], [g.XB[t]])
        w1v = I["mlp_w1"][li].rearrange("(k p) n -> p k n", p=128)
        w2v = I["mlp_w2"][li].rearrange("(j p) n -> p j n", p=128)
        rtc = 0
        for c in range(8):
            pc = c % 2
            S.dma("pool", W1[pc][:], w1v[:, :, c * 512:(c + 1) * 512], wbuf=W1B[pc])
            S.dma("pool", W2[pc][:], w2v[:, c * 4:(c + 1) * 4, :], wbuf=W2B[pc])
            for j in range(4):
                S.tt("dve", W2[pc][:, j, :], W2[pc][:, j, :], g.MOD[:, 2, :], ALU.mult, [W2B[pc], g.MODB], [W2B[pc]])
            for n in range(NB):
                for m in range(4):
                    pb = (n * 4 + m) % 2
                    for k in range(8):
                        S.mm(g.PS[pb][:], W1[pc][:, k, m * 128:(m + 1) * 128], g.HT[:, k, n * 512:(n + 1) * 512], k == 0, k == 7,
                             [W1B[pc]] + g.HTB[n * 4:(n + 1) * 4], [g.PSB[pb]])
                    rt, rtb = RT[rtc % 2], RTB[rtc % 2]
                    rtc += 1
                    col = c * 4 + m
                    S.actf(rt[:], g.PS[pb][:], AF.Relu, [g.PSB[pb], B1B], [rtb], bias=B1[:, col:col + 1], scale=1.0)
                    S.actf(AT[pc][:, m, n * 512:(n + 1) * 512], rt[:], AF.Square, [rtb], [ATB[pc][n]])
            for t in range(NT):
                pbs = (2 + 2 * (t % 2), 3 + 2 * (t % 2))
                for hh in range(2):
                    for j in range(4):
                        S.mm(g.PS[pbs[hh]][:], AT[pc][:, j, t * 128:(t + 1) * 128], W2[pc][:, j, hh * 512:(hh + 1) * 512], j == 0, j == 3,
                             [ATB[pc][t // 4], W2B[pc]], [g.PSB[pbs[hh]]])
                for hh in range(2):
                    S.tt("dve", g.X[:, t, hh * 512:(hh + 1) * 512], g.PS[pbs[hh]][:], g.X[:, t, hh * 512:(hh + 1) * 512], ALU.add,
                         [g.PSB[pbs[hh]], g.XB[t]], [g.XB[t]])
                if c == 7:
                    layer_norm_tile(g, t)
        ln_flush(g)
        S.barrier()
        S.emit()
        S.release(local + [g.LNGB, g.LNBB])


def mla_phase(g):
    S, I = g.S, g.I
    SCALE = 1.0 / math.sqrt(192.0)
    with contextlib.ExitStack() as ph:
        sb = lambda name, shape, dt: g.sb(ph, name, shape, dt)
        WIN = sb("WIN", [128, 8, 512], BF16); WINB = Buf("WIN")
        WQB = sb("WQB", [128, 2, 2560], BF16); WQBB = Buf("WQB")
        WKV = sb("WKV", [128, 2048], BF16); WKVB = Buf("WKV")
        QN = sb("QN", [128, 2], F32); QNB = Buf("QN")
        KVN = sb("KVN", [128, 1], F32); KVNB = Buf("KVN")
        CQT = sb("CQT", [128, 2, L], BF16); CQB = [Buf("CQ%d" % n) for n in range(NB)]
        CKT = sb("CKT", [128, L], BF16); CKB = [Buf("CK%d" % n) for n in range(NB)]
        KPT = sb("KPT", [64, L], BF16); KPB = [Buf("KP%d" % n) for n in range(NB)]
        C2 = sb("C2", [64, L], F32); S2 = sb("S2", [64, L], F32); TABB = Buf("ropetab")
        INVF = sb("INVF", [64, 1], F32)
        QNT = sb("QNT", [128, L], BF16); QNTB = [Buf("QNT%d" % n) for n in range(NB)]
        QPT = sb("QPT", [64, L], BF16); QPTB = [Buf("QPT%d" % n) for n in range(NB)]
        KNT = sb("KNT", [128, L], BF16); KNTB = [Buf("KNT%d" % n) for n in range(NB)]
        VT = sb("VT", [128, NT, 128], BF16); VTB = [Buf("VT%d" % n) for n in range(NB)]
        ZF = sb("ZF", [128, 2, 512], F32); ZFB = Buf("ZF")
        SQ = sb("SQ", [128, 512], F32); SQB = Buf("SQ")
        RS = sb("RS", [128, 512], F32); RSB = Buf("RS")
        R1 = SQ[0:64, :]; R1B = SQB
        R2 = RS[0:64, :]; R2B = RSB
        PT = [sb("PT%d" % i, [128, 512], BF16) for i in range(5)]; PTB = [Buf("PT%d" % i) for i in range(5)]
        MASK = sb("MASK", [128, 4, 512], BF16); MASKB = Buf("MASK")
        local = [WINB, WQBB, WKVB, QNB, KVNB, TABB, MASKB]

        S.dma("pool", WIN[:], I["mla_w_in"].rearrange("(k p) n -> p k n", p=128), wbuf=WINB)
        S.dma("pool", WQB[:], I["mla_w_qb"].rearrange("(k p) n -> p k n", p=128), wbuf=WQBB)
        S.dma("pool", WKV[:], I["mla_w_kvb"], wbuf=WKVB)
        S.dma("sp", QN[:], I["mla_qn"], wbuf=QNB)
        S.dma("sp", KVN[:], I["mla_kvn"], wbuf=KVNB)
        PI32 = sb("PI32", [64, 512], I32); PI32B = Buf("PI32")
        local.append(PI32B)
        S.dma("sp", INVF[:], I["invf"], wbuf=TABB)
        for k in range(2):
            S.ts("dve", WQB[:, k, :], WQB[:, k, :], QN[:, k:k + 1], None, ALU.mult, None, [WQBB, QNB], [WQBB])
        S.ts("dve", WKV[:, 0:1024], WKV[:, 0:1024], KVN[:, 0:1], None, ALU.mult, None, [WKVB, KVNB], [WKVB])
        S.ts("dve", WKV[:, 1024:2048], WKV[:, 1024:2048], KVN[:, 0:1], None, ALU.mult, None, [WKVB, KVNB], [WKVB])
        for j in range(4):
            S.memset("pool", SQ[:], 1.0, [SQB], [SQB])
            S.op("pool", lambda e, j=j: e.affine_select(out=SQ[:], in_=SQ[:], compare_op=ALU.is_ge, fill=0.0,
                                                        base=-128 * j, pattern=[[1, 512]], channel_multiplier=-1),
                 reads=[SQB], writes=[SQB])
            S.cp("pool", MASK[:, j, :], SQ[:], [SQB], [MASKB])

        def proj_group(pb, lhs_list, rhs_list, reads, M=128):
            nk = len(lhs_list)
            for k in range(nk):
                S.mm(g.PS[pb][0:M, :], lhs_list[k], rhs_list[k], k == 0, k == nk - 1, reads, [g.PSB[pb]])

        def rms_block(srcs_pb, dsts, dstB):
            ntile = len(srcs_pb)
            for i, pb in enumerate(srcs_pb):
                S.cp("act", ZF[:, i, :], g.PS[pb][:], [g.PSB[pb]], [ZFB])
            for i in range(ntile):
                S.actf(SQ[:], ZF[:, i, :], AF.Square, [ZFB], [SQB])
                S.mm(g.PS[6][:], g.ones[:], SQ[:], i == 0, i == ntile - 1, [SQB, g.CB], [g.PSB[6]], signal=True)
            S.actf(RS[:], g.PS[6][:], AF.Sqrt, [g.PSB[6], g.CB], [RSB], bias=g.cst[:, 4:5], scale=1.0 / (128 * ntile))
            S.op("dve", lambda e: e.reciprocal(out=RS[:], in_=RS[:]), reads=[RSB], writes=[RSB])
            for i in range(ntile):
                S.tt("dve", dsts[i], ZF[:, i, :], RS[:], ALU.mult, [ZFB, RSB], [dstB])

        def rope_block(pb_a, pb_b, n, dst, dstB):
            sl_ = slice(n * 512, (n + 1) * 512)
            S.tt("dve", R1, g.PS[pb_a][0:64, :], C2[:, sl_], ALU.mult, [g.PSB[pb_a], TABB], [R1B])
            S.tt("dve", R2, g.PS[pb_b][0:64, :], S2[:, sl_], ALU.mult, [g.PSB[pb_b], TABB], [R2B])
            S.tt("dve", dst[:, sl_], R1, R2, ALU.add, [R1B, R2B], [dstB])

        def rope_tables(n):
            sl_ = slice(n * 512, (n + 1) * 512)
            S.dma("sp", PI32[:], I["pos"][:, sl_], wbuf=PI32B)
            S.cp("dve", C2[:, sl_], PI32[:], [PI32B, TABB], [TABB])
            S.ts("dve", C2[:, sl_], C2[:, sl_], INVF[:, 0:1], None, ALU.mult, None, [TABB], [TABB])
            T_, M_ = ZF[0:64, 0, :], ZF[0:64, 1, :]
            sin_reduced(g, S2[0:32, sl_], C2[0:32, sl_], 0.0, T_[0:32], PI32[0:32, :], M_[0:32], [TABB], [TABB], [ZFB, PI32B], sign=-1.0)
            sin_reduced(g, S2[32:64, sl_], C2[32:64, sl_], 0.0, ZF[32:64, 0, :], PI32[32:64, :], ZF[32:64, 1, :], [TABB], [TABB], [ZFB, PI32B])
            sin_reduced(g, C2[:, sl_], C2[:, sl_], 0.5 * PI, T_, PI32[:], M_, [TABB], [TABB], [ZFB, PI32B])

        for n in range(NB):
            hsl = slice(n * 512, (n + 1) * 512)
            hr = g.HTB[n * 4:(n + 1) * 4]
            rhs = [g.HT[:, k, hsl] for k in range(8)]
            for i in range(2):
                proj_group(i, [WIN[:, k, i * 128:(i + 1) * 128] for k in range(8)], rhs, [WINB] + hr)
            rms_block([0, 1], [CQT[:, 0, hsl], CQT[:, 1, hsl]], CQB[n])
            proj_group(2, [WIN[:, k, 256:384] for k in range(8)], rhs, [WINB] + hr)
            rms_block([2], [CKT[:, hsl]], CKB[n])
            proj_group(3, [WIN[:, k, 384:448] for k in range(8)], rhs, [WINB] + hr, M=64)
            proj_group(4, [WIN[:, k, 448:512] for k in range(8)], rhs, [WINB] + hr, M=64)
            rope_tables(n)
            rope_block(3, 4, n, KPT, KPB[n])

        RSQ = [RS, ZF[:, 0, :]]; RSQB = [RSB, ZFB]
        OT = g.HT
        ptc = 0
        for h in range(8):
            qb = h * 320
            for n in range(NB):
                hsl = slice(n * 512, (n + 1) * 512)
                crhs = [CQT[:, k, hsl] for k in range(2)]
                proj_group(0, [WQB[:, k, qb:qb + 128] for k in range(2)], crhs, [WQBB, CQB[n]])
                S.cp("act", QNT[:, hsl], g.PS[0][:], [g.PSB[0]], [QNTB[n]])
                proj_group(3, [WQB[:, k, qb + 128:qb + 192] for k in range(2)], crhs, [WQBB, CQB[n]], M=64)
                proj_group(4, [WQB[:, k, qb + 192:qb + 256] for k in range(2)], crhs, [WQBB, CQB[n]], M=64)
                rope_block(3, 4, n, QPT, QPTB[n])
                proj_group(1, [WKV[:, h * 128:(h + 1) * 128]], [CKT[:, hsl]], [WKVB, CKB[n]])
                S.cp("act", KNT[:, hsl], g.PS[1][:], [g.PSB[1]], [KNTB[n]])
                for tt in range(4):
                    t = n * 4 + tt
                    S.mm(g.PS[2][:, tt * 128:(tt + 1) * 128], CKT[:, t * 128:(t + 1) * 128], WKV[:, 1024 + h * 128:1024 + (h + 1) * 128], True, True,
                         [WKVB, CKB[n]], [g.PSB[2]], signal=(tt == 3))
                S.cp("act", VT[:, n * 4:(n + 1) * 4, :], g.PS[2][:].rearrange("p (t d) -> p t d", t=4), [g.PSB[2]], [VTB[n]])
            iters = [(qi, kt) for qi in range(NB) for kt in range(4 * (qi + 1))]
            SBK = [3, 4, 0, 1]
            LOOK = 3

            def issue_scores(idx):
                qi, kt = iters[idx]
                spb = SBK[idx % 4]
                ksl = slice(kt * 128, (kt + 1) * 128)
                qsl = slice(qi * 512, (qi + 1) * 512)
                kn = kt // 4
                S.mm(g.PS[spb][:], KNT[:, ksl], QNT[:, qsl], True, False, [KNTB[kn], QNTB[qi]], [g.PSB[spb]], signal=False)
                S.mm(g.PS[spb][:], KPT[:, ksl], QPT[:, qsl], False, True, [KPB[kn], QPTB[qi]], [g.PSB[spb]], signal=True)

            for i_ in range(min(LOOK, len(iters))):
                issue_scores(i_)
            for idx, (qi, kt) in enumerate(iters):
                if idx + LOOK < len(iters):
                    issue_scores(idx + LOOK)
                spb = SBK[idx % 4]
                qsl = slice(qi * 512, (qi + 1) * 512)
                nkt = 4 * (qi + 1)
                kn = kt // 4
                ob = 5 + (qi % 2)
                pt, ptb = PT[ptc % 5], PTB[ptc % 5]
                ptc += 1
                S.actf(pt[:], g.PS[spb][:], AF.Exp, [g.PSB[spb], g.CB], [ptb], bias=g.cst[:, 5:6], scale=SCALE)
                if kt >= 4 * qi:
                    S.tt("dve", pt[:], pt[:], MASK[:, kt - 4 * qi, :], ALU.mult, [ptb, MASKB], [ptb])
                sbk = 7 if qi % 2 == 0 else 2
                S.mm(g.PS[ob][:], VT[:, kt, :], pt[:], kt == 0, kt == nkt - 1, [VTB[kn], ptb], [g.PSB[ob]], signal=False)
                S.mm(g.PS[sbk][:], g.onesb[:], pt[:], kt == 0, kt == nkt - 1, [ptb, g.CB], [g.PSB[sbk]], signal=True)
                if kt == nkt - 1:
                    rs, rsb = RSQ[qi % 2], RSQB[qi % 2]
                    S.cp("act", rs[:], g.PS[sbk][:], [g.PSB[sbk]], [rsb])
                    S.op("dve", lambda e, rs=rs: e.reciprocal(out=rs[:], in_=rs[:]), reads=[rsb], writes=[rsb])
                    S.tt("dve", OT[:, h, qsl], g.PS[ob][:], rs[:], ALU.mult, [g.PSB[ob], rsb], g.HTB[qi * 4:(qi + 1) * 4])
        S.barrier()
        S.emit()
        S.release(local)

    with contextlib.ExitStack() as ph:
        WO = g.sb(ph, "WO", [128, 8, D], BF16); WOB = Buf("WO")
        load_ln(g, ph)
        S.dma("pool", WO[:], I["mla_w_o"].rearrange("(k p) n -> p k n", p=128), wbuf=WOB)
        for k in range(8):
            S.tt("dve", WO[:, k, :], WO[:, k, :], g.MOD[:, 2, :], ALU.mult, [WOB, g.MODB], [WOB])
        for t in range(NT):
            pbs = (2 * (t % 2), 2 * (t % 2) + 1)
            for hh in range(2):
                for k in range(8):
                    S.mm(g.PS[pbs[hh]][:], g.HT[:, k, t * 128:(t + 1) * 128], WO[:, k, hh * 512:(hh + 1) * 512], k == 0, k == 7,
                         [g.HTB[t], WOB], [g.PSB[pbs[hh]]])
            mixer_epilogue_tile(g, t, pbs)
        ln_flush(g)
        S.barrier()
        S.emit()
        S.release([WOB, g.LNGB, g.LNBB])


def s5_phase(g):
    S, I = g.S, g.I
    NSTG = 11
    with contextlib.ExitStack() as s5:
        UT = g.sb(s5, "UT", [128, 8, L], BF16)
        UTB = [Buf("UT%d" % n) for n in range(NB)]
        DCOL = g.sb(s5, "DCOL", [128, 8], F32); BGLU = g.sb(s5, "BGLU", [128, 8], F32); SMB = Buf("s5small")
        DREP = g.sb(s5, "DREP", [128, 32], F32)
        ab = s5.enter_context(contextlib.ExitStack())
        BBR = g.sb(ab, "BBR", [128, 32, 32], F32); BBI = g.sb(ab, "BBI", [128, 32, 32], F32); BBB = Buf("BB")
        CRE = g.sb(ab, "CRE", [128, 32, 32], BF16); CIM = g.sb(ab, "CIM", [128, 32, 32], BF16); CCB = Buf("CC")
        PR = g.sb(ab, "PR", [128, NSTG, 32], F32); PIm = g.sb(ab, "PIm", [128, NSTG, 32], F32); NPI = g.sb(ab, "NPI", [128, NSTG, 32], F32)
        IAT = g.sb(ab, "IAT", [128, 4, 32], F32)
        ER = g.sb(ab, "ER", [128, 32, 16], F32); EI = g.sb(ab, "EI", [128, 32, 16], F32)
        FR = g.sb(ab, "FR", [128, 32, 16], F32); FI = g.sb(ab, "FI", [128, 32, 16], F32)
        POWB = Buf("POW")

        with contextlib.ExitStack() as ph:
            sb = lambda name, shape, dt: g.sb(ph, name, shape, dt)
            WIN = sb("SWIN", [128, 8, D], BF16); WINB = Buf("SWIN")
            BRE = sb("BRE", [128, 32, 32], F32); BIM = sb("BIM", [128, 32, 32], F32); BRB = Buf("BRE")
            PRM = sb("PRM", [128, 16, 32], F32); PB_ = Buf("PRM")
            TB = sb("TB", [128, 2, 32], F32); TBB = Buf("TB")
            S.dma("pool", WIN[:], I["ssm_w_in"].rearrange("(k p) n -> p k n", p=128), wbuf=WINB)
            S.dma("pool", CRE[:], I["ssm_cre"].rearrange("p (s c) -> p s c", c=32), wbuf=CCB)
            S.dma("pool", CIM[:], I["ssm_cim"].rearrange("p (s c) -> p s c", c=32), wbuf=CCB)
            S.dma("sp", BRE[:], I["ssm_bre"].rearrange("p (s c) -> p s c", c=32), wbuf=BRB)
            S.dma("sp", BIM[:], I["ssm_bim"].rearrange("p (s c) -> p s c", c=32), wbuf=BRB)
            S.dma("sp", PRM[:, 0, :], I["ssm_ldt"], wbuf=PB_)
            S.dma("sp", PRM[:, 1, :], I["ssm_lr"], wbuf=PB_)
            S.dma("sp", PRM[:, 2, :], I["ssm_li"], wbuf=PB_)
            S.dma("sp", DCOL[:], I["ssm_d"], wbuf=SMB)
            S.dma("sp", BGLU[:], I["ssm_bglu"], wbuf=SMB)
            S.dma("sp", DREP[:], I["ssm_drep"], wbuf=SMB)
            P = lambda i: PRM[:, i, :]
            rw = ([PB_], [PB_])
            S.actf(P(3), P(0), AF.Exp, *rw)
            S.tt("dve", P(4), P(1), P(3), ALU.mult, *rw)
            S.tt("dve", P(5), P(2), P(3), ALU.mult, *rw)
            S.actf(P(6), P(4), AF.Exp, *rw)
            def sin_ladder(out, ang, shift):
                S.ts("dve", P(13), ang, shift + 2 * PI, None, ALU.add, None, *rw)
                for mult in (16, 8, 4, 2):
                    S.ts("dve", P(14), P(13), mult * PI, None, ALU.is_ge, None, *rw)
                    S.stt("dve", P(13), P(14), -mult * PI, P(13), ALU.mult, ALU.add, *rw)
                S.actf(out, P(13), AF.Sin, [PB_, g.CB], [PB_], bias=g.cst[:, 1:2], scale=-1.0)
            sin_ladder(P(7), P(5), 0.5 * PI)
            sin_ladder(P(8), P(5), 0.0)
            S.tt("dve", PR[:, 0, :], P(6), P(7), ALU.mult, [PB_], [POWB])
            S.tt("dve", PIm[:, 0, :], P(6), P(8), ALU.mult, [PB_], [POWB])
            S.ts("dve", P(9), PR[:, 0, :], -1.0, None, ALU.add, None, [PB_, POWB], [PB_])
            S.tt("dve", P(13), P(1), P(1), ALU.mult, *rw)
            S.tt("dve", P(14), P(2), P(2), ALU.mult, *rw)
            S.tt("dve", P(10), P(13), P(14), ALU.add, *rw)
            S.op("dve", lambda e: e.reciprocal(out=PRM[:, 10, :], in_=PRM[:, 10, :]), reads=[PB_], writes=[PB_])
            S.tt("dve", P(13), P(9), P(1), ALU.mult, *rw)
            S.tt("dve", P(14), PIm[:, 0, :], P(2), ALU.mult, [PB_, POWB], [PB_])
            S.tt("dve", P(13), P(13), P(14), ALU.add, *rw)
            S.tt("dve", P(11), P(13), P(10), ALU.mult, *rw)
            S.tt("dve", P(13), PIm[:, 0, :], P(1), ALU.mult, [PB_, POWB], [PB_])
            S.tt("dve", P(14), P(9), P(2), ALU.mult, *rw)
            S.tt("dve", P(13), P(13), P(14), ALU.subtract, *rw)
            S.tt("dve", P(12), P(13), P(10), ALU.mult, *rw)
            for k in range(NSTG):
                if k > 0:
                    S.tt("dve", P(13), PR[:, k - 1, :], PR[:, k - 1, :], ALU.mult, [POWB, PB_], [PB_])
                    S.tt("dve", P(14), PIm[:, k - 1, :], PIm[:, k - 1, :], ALU.mult, [POWB, PB_], [PB_])
                    S.tt("dve", PR[:, k, :], P(13), P(14), ALU.subtract, [PB_, POWB], [POWB])
                    S.tt("dve", P(13), PR[:, k - 1, :], PIm[:, k - 1, :], ALU.mult, [POWB, PB_], [PB_])
                    S.ts("dve", PIm[:, k, :], P(13), 2.0, None, ALU.mult, None, [PB_, POWB], [POWB])
                S.ts("dve", NPI[:, k, :], PIm[:, k, :], -1.0, None, ALU.mult, None, [POWB], [POWB])
            for s in range(32):
                cr, ci = PRM[:, 11, s:s + 1], PRM[:, 12, s:s + 1]
                S.ts("dve", TB[:, 0, :], BIM[:, s, :], ci, None, ALU.mult, None, [BRB, PB_, TBB], [TBB])
                S.ts("dve", TB[:, 1, :], BRE[:, s, :], ci, None, ALU.mult, None, [BRB, PB_, TBB], [TBB])
                S.stt("dve", TB[:, 0, :], BRE[:, s, :], cr, TB[:, 0, :], ALU.mult, ALU.subtract, [BRB, PB_, TBB], [TBB])
                S.stt("dve", BBI[:, s, :], BIM[:, s, :], cr, TB[:, 1, :], ALU.mult, ALU.add, [BRB, PB_, TBB], [BBB])
                S.cp("dve", BBR[:, s, :], TB[:, 0, :], [TBB], [BBB])
            S.tt("dve", P(13), PR[:, 4, :], PR[:, 4, :], ALU.mult, [POWB, PB_], [PB_])
            S.tt("dve", P(14), PIm[:, 4, :], PIm[:, 4, :], ALU.mult, [POWB, PB_], [PB_])
            S.tt("dve", P(13), P(13), P(14), ALU.add, *rw)
            S.op("dve", lambda e: e.reciprocal(out=PRM[:, 13, :], in_=PRM[:, 13, :]), reads=[PB_], writes=[PB_])
            S.tt("dve", IAT[:, 0, :], PR[:, 4, :], P(13), ALU.mult, [POWB, PB_], [POWB])
            S.tt("dve", IAT[:, 1, :], NPI[:, 4, :], P(13), ALU.mult, [POWB, PB_], [POWB])
            S.ts("dve", IAT[:, 2, :], IAT[:, 0, :], -1.0, None, ALU.mult, None, [POWB], [POWB])
            S.ts("dve", IAT[:, 3, :], IAT[:, 1, :], -1.0, None, ALU.mult, None, [POWB], [POWB])
            TT = sb("TT", [128, 2, 256], F32); TTB = Buf("TT")

            def cmul_bc(dr, di, sr, si, k, w):
                pr = PR[:, k, :].unsqueeze(2).to_broadcast([128, 32, w])
                pi_ = PIm[:, k, :].unsqueeze(2).to_broadcast([128, 32, w])
                t1 = TT[:, 0, 0:32 * w].rearrange("p (s w) -> p s w", w=w)
                t2 = TT[:, 1, 0:32 * w].rearrange("p (s w) -> p s w", w=w)
                rwt = ([POWB, TTB], [TTB])
                S.tt("dve", t1, sr, pr, ALU.mult, *rwt)
                S.tt("dve", t2, si, pi_, ALU.mult, *rwt)
                S.tt("dve", dr, t1, t2, ALU.subtract, [TTB, POWB], [POWB])
                S.tt("dve", t1, sr, pi_, ALU.mult, *rwt)
                S.tt("dve", t2, si, pr, ALU.mult, *rwt)
                S.tt("dve", di, t1, t2, ALU.add, [TTB, POWB], [POWB])
            S.memset("pool", ER[:, :, 15:16], 1.0, [POWB], [POWB])
            S.memset("pool", EI[:, :, 15:16], 0.0, [POWB], [POWB])
            S.cp("dve", ER[:, :, 14:15], PR[:, 0, :].unsqueeze(2), [POWB], [POWB])
            S.cp("dve", EI[:, :, 14:15], PIm[:, 0, :].unsqueeze(2), [POWB], [POWB])
            cmul_bc(ER[:, :, 12:14], EI[:, :, 12:14], ER[:, :, 14:16], EI[:, :, 14:16], 1, 2)
            cmul_bc(ER[:, :, 8:12], EI[:, :, 8:12], ER[:, :, 12:16], EI[:, :, 12:16], 2, 4)
            cmul_bc(ER[:, :, 0:8], EI[:, :, 0:8], ER[:, :, 8:16], EI[:, :, 8:16], 3, 8)
            S.cp("dve", FR[:, :, 0:1], PR[:, 0, :].unsqueeze(2), [POWB], [POWB])
            S.cp("dve", FI[:, :, 0:1], PIm[:, 0, :].unsqueeze(2), [POWB], [POWB])
            S.cp("dve", FR[:, :, 1:2], PR[:, 1, :].unsqueeze(2), [POWB], [POWB])
            S.cp("dve", FI[:, :, 1:2], PIm[:, 1, :].unsqueeze(2), [POWB], [POWB])
            cmul_bc(FR[:, :, 2:4], FI[:, :, 2:4], FR[:, :, 0:2], FI[:, :, 0:2], 1, 2)
            cmul_bc(FR[:, :, 4:8], FI[:, :, 4:8], FR[:, :, 0:4], FI[:, :, 0:4], 2, 4)
            cmul_bc(FR[:, :, 8:16], FI[:, :, 8:16], FR[:, :, 0:8], FI[:, :, 0:8], 3, 8)
            for n in range(NB):
                hsl = slice(n * 512, (n + 1) * 512)
                for m in range(8):
                    pb = m % 2
                    for k in range(8):
                        S.mm(g.PS[pb][:], WIN[:, k, m * 128:(m + 1) * 128], g.HT[:, k, hsl], k == 0, k == 7,
                             [WINB] + g.HTB[n * 4:(n + 1) * 4], [g.PSB[pb]])
                    S.cp("act", UT[:, m, hsl], g.PS[pb][:], [g.PSB[pb]], [UTB[n]])
            S.barrier()
            S.emit()
            S.release([WINB, BRB, PB_, CCB, SMB])

        with contextlib.ExitStack() as ph:
            sb = lambda name, shape, dt: g.sb(ph, name, shape, dt)
            NCH = L // 16
            SEL = sb("SEL", [128, 16, 128], BF16); SELT = sb("SELT", [128, 16, 128], BF16); SELB = Buf("SEL")
            TRI = sb("TRI", [128, 128], BF16); MSKB = Buf("TRI")
            CPW = [sb("CPW%d" % i, [128, 512], F32) for i in range(2)]; CPWB = Buf("CPW")
            BPb = [sb("BPb%d" % i, [128, 512], BF16) for i in range(2)]; BPbB = Buf("BPb")
            CW2 = [sb("CW2%d" % i, [128, 512], BF16) for i in range(2)]; CW2B = Buf("CW2")
            CPb = [sb("CPb%d" % i, [128, 512], BF16) for i in range(2)]; CPbB = Buf("CPb")
            M2 = sb("M2", [128, 8, 128], BF16); M2B = Buf("M2")
            M1 = sb("M1", [128, 4, 512], BF16); M1B = Buf("M1")
            DG = sb("DG", [128, 128], BF16); DGB = Buf("DG")
            UG = sb("UG", [128, 512], BF16); UGB = Buf("UG")
            GG = sb("GG", [128, 4, 512], BF16); GGB = [Buf("GG%d" % j) for j in range(4)]
            ZP = 64
            SA = [sb("SA%d" % i, [128, ZP + NCH], F32) for i in range(2)]
            SBf = [sb("SBf%d" % i, [128, ZP + NCH], F32) for i in range(2)]
            SAB = [Buf("SA0"), Buf("SA1")]; SBB = [Buf("SB0"), Buf("SB1")]
            XS = [sb("XS%d" % i, [128, NCH], BF16) for i in range(2)]; XSB = Buf("XS")
            GA = sb("GA", [128, 512], F32); GAB = Buf("GA")
            GS = sb("GS", [128, 512], F32); GSB = Buf("GS")
            S.memset("pool", SEL[:], 0.0, [], [SELB])
            S.memset("pool", SELT[:], 0.0, [], [SELB])
            for j in range(4):
                for il in range(4):
                    S.cp("dve", SEL[32 * j:32 * j + 32, j * 4 + il, 32 * il:32 * il + 32], g.ident[32 * j:32 * j + 32, 32 * j:32 * j + 32], [g.CB, SELB], [SELB])
                    S.cp("dve", SELT[32 * il:32 * il + 32, j * 4 + il, 32 * j:32 * j + 32], g.ident[32 * il:32 * il + 32, 32 * il:32 * il + 32], [g.CB, SELB], [SELB])
            S.memset("pool", GA[:, 0:128], 1.0, [GAB], [GAB])
            S.op("pool", lambda e: e.affine_select(out=GA[:, 0:128].rearrange("p (j c) -> p j c", c=32), in_=GA[:, 0:128].rearrange("p (j c) -> p j c", c=32),
                                                   compare_op=ALU.is_ge, fill=0.0, base=31, pattern=[[32, 4], [0, 32]], channel_multiplier=-1),
                 reads=[GAB], writes=[GAB])
            S.cp("pool", TRI[:], GA[:, 0:128], [GAB], [MSKB])
            for i in range(2):
                S.memset("pool", XS[i][:, 0:1], 0.0, [], [XSB])
                S.memset("pool", SA[i][:, 0:ZP], 0.0, [], [SAB[i]])
                S.memset("pool", SBf[i][:, 0:ZP], 0.0, [], [SBB[i]])

            def cmul(dst, src, lo_d, lo_s, width, k, s, reads, writes):
                pr, pi_, npi = PR[:, k, s:s + 1], PIm[:, k, s:s + 1], NPI[:, k, s:s + 1]
                d0, d1 = dst[0][:, lo_d:lo_d + width], dst[1][:, lo_d:lo_d + width]
                s0, s1 = src[0][:, lo_s:lo_s + width], src[1][:, lo_s:lo_s + width]
                S.ts("dve", d0, s0, pr, None, ALU.mult, None, reads, writes)
                S.stt("dve", d0, s1, npi, d0, ALU.mult, ALU.add, reads, writes)
                S.ts("dve", d1, s1, pr, None, ALU.mult, None, reads, writes)
                S.stt("dve", d1, s0, pi_, d1, ALU.mult, ALU.add, reads, writes)

            gT = g.HT
            pending = []
            for ct in range(8):
                for j in range(4):
                    s = 4 * ct + j
                    for il in range(4):
                        S.mm(g.PS[2][:], SEL[:, j * 4 + il, :], UT[:, ct, il::4], il == 0, il == 3, [SELB] + UTB, [g.PSB[2]])
                    S.cp("act", UG[:], g.PS[2][:], [g.PSB[2]], [UGB])
                    v3 = lambda ap: ap.rearrange("p (i c) -> p i c", c=32)
                    bc_i = lambda tab: tab[:, s, :].unsqueeze(2).to_broadcast([128, 16, 32])
                    bc_c = lambda tab: tab[:, s, :].unsqueeze(1).to_broadcast([128, 16, 32])
                    ga3, gs3 = v3(GA[:]), v3(GS[:])
                    rwa = ([POWB, BBB, CCB, GAB], [GAB])
                    rws = ([POWB, BBB, CCB, GSB], [GSB])
                    S.tt("dve", ga3, bc_i(ER), bc_c(BBR), ALU.mult, *rwa)
                    S.tt("dve", gs3, bc_i(EI), bc_c(BBI), ALU.mult, *rws)
                    S.tt("dve", BPb[0][:], GA[:], GS[:], ALU.subtract, [GAB, GSB], [BPbB])
                    S.tt("dve", ga3, bc_i(ER), bc_c(BBI), ALU.mult, *rwa)
                    S.tt("dve", gs3, bc_i(EI), bc_c(BBR), ALU.mult, *rws)
                    S.tt("dve", BPb[1][:], GA[:], GS[:], ALU.add, [GAB, GSB], [BPbB])
                    pst = g.PS[6][:].bitcast(BF16)
                    for r in range(2):
                        for kt in range(4):
                            S.tr(pst[:, (r * 4 + kt) * 128:(r * 4 + kt + 1) * 128], BPb[r][:, kt * 128:(kt + 1) * 128], g.ident[:], [BPbB, g.CB], [g.PSB[6]],
                                 signal=(r == 1 and kt == 3))
                    S.cp("act", M2[:], pst.rearrange("p (k n) -> p k n", k=8), [g.PSB[6]], [M2B])
                    for r in range(2):
                        for kt in range(4):
                            S.mm(g.PS[3][:, r * NCH:(r + 1) * NCH], M2[:, r * 4 + kt, :], UG[:, kt::4], kt == 0, kt == 3, [M2B, UGB], [g.PSB[3]])
                    S.cp("act", SA[0][:, ZP:ZP + NCH], g.PS[3][:, 0:NCH], [g.PSB[3]], [SAB[0]])
                    S.cp("act", SA[1][:, ZP:ZP + NCH], g.PS[3][:, NCH:2 * NCH], [g.PSB[3]], [SAB[1]])
                    S.tt("dve", ga3, bc_i(FR), bc_c(CRE), ALU.mult, *rwa)
                    S.tt("dve", gs3, bc_i(FI), bc_c(CIM), ALU.mult, *rws)
                    S.tt("dve", CPW[0][:], GA[:], GS[:], ALU.subtract, [GAB, GSB], [CPWB])
                    S.tt("dve", ga3, bc_i(FR), bc_c(CIM), ALU.mult, *rwa)
                    S.tt("dve", gs3, bc_i(FI), bc_c(CRE), ALU.mult, *rws)
                    S.tt("dve", CPW[1][:], GA[:], GS[:], ALU.add, [GAB, GSB], [CPWB])
                    S.cp("act", CPb[0][:], CPW[0][:], [CPWB], [CPbB])
                    S.actf(CPb[1][:], CPW[1][:], AF.Copy, [CPWB], [CPbB], scale=-1.0)
                    iar, iai, niar, niai = IAT[:, 0, s:s + 1], IAT[:, 1, s:s + 1], IAT[:, 2, s:s + 1], IAT[:, 3, s:s + 1]
                    rw2 = ([CPWB, POWB, GAB], [GAB])
                    S.ts("dve", GA[:], CPW[1][:], niai, None, ALU.mult, None, *rw2)
                    S.stt("dve", CW2[0][:], CPW[0][:], iar, GA[:], ALU.mult, ALU.add, [CPWB, POWB, GAB], [CW2B])
                    S.ts("dve", GA[:], CPW[1][:], niar, None, ALU.mult, None, *rw2)
                    S.stt("dve", CW2[1][:], CPW[0][:], niai, GA[:], ALU.mult, ALU.add, [CPWB, POWB, GAB], [CW2B])
                    S.ts("dve", DG[:], g.identf[:], DREP[:, s:s + 1], None, ALU.mult, None, [g.CB, SMB], [DGB])
                    if pending:
                        pending.pop()()
                    m1_evac = []
                    for kt in range(4):
                        pb = kt % 2
                        c0 = 128 * kt
                        S.mm(g.PS[pb][:, c0:512], BPb[0][:, c0:c0 + 128], CW2[0][:, c0:512], True, False, [BPbB, CW2B], [g.PSB[pb]], signal=False)
                        S.mm(g.PS[pb][:, c0:512], BPb[1][:, c0:c0 + 128], CW2[1][:, c0:512], False, True, [BPbB, CW2B], [g.PSB[pb]], signal=True)
                        if kt < 2:
                            S.tt("dve", M1[:, kt, c0:c0 + 128], g.PS[pb][:, c0:c0 + 128], TRI[:], ALU.mult, [g.PSB[pb], MSKB], [M1B])
                            S.cp("act", M1[:, kt, c0 + 128:512], g.PS[pb][:, c0 + 128:512], [g.PSB[pb]], [M1B])
                        else:
                            def ev(kt=kt, pb=pb, c0=c0):
                                S.tt("dve", M1[:, kt, c0:c0 + 128], g.PS[pb][:, c0:c0 + 128], TRI[:], ALU.mult, [g.PSB[pb], MSKB], [M1B])
                                if kt < 3:
                                    S.cp("act", M1[:, kt, c0 + 128:512], g.PS[pb][:, c0 + 128:512], [g.PSB[pb]], [M1B])
                            m1_evac.append(ev)
                    src, dst, srcB, dstB = SA, SBf, SAB, SBB
                    for k in range(7):
                        sh = 1 << k
                        pr, pi_, npi = PR[:, k + 4, s:s + 1], PIm[:, k + 4, s:s + 1], NPI[:, k + 4, s:s + 1]
                        lo, hi = ZP - sh, ZP - sh + NCH
                        S.stt("dve", dst[0][:, ZP:ZP + NCH], src[0][:, lo:hi], pr, src[0][:, ZP:ZP + NCH], ALU.mult, ALU.add, [srcB[0], POWB], [dstB[0]])
                        S.stt("dve", dst[1][:, ZP:ZP + NCH], src[1][:, lo:hi], pr, src[1][:, ZP:ZP + NCH], ALU.mult, ALU.add, [srcB[1], POWB], [dstB[1]])
                        S.stt("dve", dst[0][:, ZP:ZP + NCH], src[1][:, lo:hi], npi, dst[0][:, ZP:ZP + NCH], ALU.mult, ALU.add, [srcB[1], dstB[0], POWB], [dstB[0]])
                        S.stt("dve", dst[1][:, ZP:ZP + NCH], src[0][:, lo:hi], pi_, dst[1][:, ZP:ZP + NCH], ALU.mult, ALU.add, [srcB[0], dstB[1], POWB], [dstB[1]])
                        if m1_evac:
                            m1_evac.pop(0)()
                        src, dst, srcB, dstB = dst, src, dstB, srcB
                    S.cp("act", XS[0][:, 1:NCH], src[0][:, ZP:ZP + NCH - 1], [srcB[0]], [XSB])
                    S.cp("act", XS[1][:, 1:NCH], src[1][:, ZP:ZP + NCH - 1], [srcB[1]], [XSB])
                    yb = 4 + (s % 2)
                    for jt in range(4):
                        osl = g.PS[yb][:, jt * NCH:(jt + 1) * NCH]
                        first = True
                        for kt in range(jt + 1):
                            S.mm(osl, M1[:, kt, jt * 128:(jt + 1) * 128], UG[:, kt::4], first, False, [M1B, UGB], [g.PSB[yb]], signal=False)
                            first = False
                        S.mm(osl, DG[:], UG[:, jt::4], False, False, [DGB, UGB], [g.PSB[yb]], signal=False)
                        S.mm(osl, CPb[0][:, jt * 128:(jt + 1) * 128], XS[0][:], False, False, [CPbB, XSB], [g.PSB[yb]], signal=False)
                        S.mm(osl, CPb[1][:, jt * 128:(jt + 1) * 128], XS[1][:], False, True, [CPbB, XSB], [g.PSB[yb]], signal=True)

                    def gelu(yb=yb, j=j):
                        ybuf = g.PSB[yb]
                        yp = g.PS[yb][:]
                        S.actf(GA[:], yp, AF.Square, [ybuf], [GAB])
                        S.ts("dve", GA[:], GA[:], 0.044715, 1.0, ALU.mult, ALU.add, [GAB], [GAB])
                        S.tt("dve", GA[:], GA[:], yp, ALU.mult, [GAB, ybuf], [GAB])
                        S.actf(GS[:], GA[:], AF.Sigmoid, [GAB], [GSB], scale=1.5957691216057308)
                        S.tt("dve", GG[:, j, :].rearrange("p (n jt) -> p jt n", jt=4), GS[:].rearrange("p (jt n) -> p jt n", jt=4),
                             yp.rearrange("p (jt n) -> p jt n", jt=4), ALU.mult, [GSB, ybuf], [GGB[j]])
                    pending.append(gelu)
                if pending:
                    pending.pop()()
                for jl in range(4):
                    pb = jl % 2
                    for j in range(4):
                        S.mm(g.PS[pb][:], SELT[:, j * 4 + jl, :], GG[:, j, :], j == 0, j == 3, [SELB, GGB[j]], [g.PSB[pb]])
                    S.cp("act", gT[:, ct, jl::4], g.PS[pb][:], [g.PSB[pb]], g.HTB)
            S.barrier()
            S.emit()
            S.release([BBB, POWB])
        ab.close()

        with contextlib.ExitStack() as ph:
            sb = lambda name, shape, dt: g.sb(ph, name, shape, dt)
            WG = sb("WG", [128, 8, D], BF16); WGB = Buf("WG")
            WOU = sb("WOU", [128, 8, D], BF16); WOUB = Buf("WOU")
            load_ln(g, ph)
            SG = [sb("SG%d" % i, [128, 512], F32) for i in range(2)]; SGB = [Buf("SG%d" % i) for i in range(2)]
            S.dma("pool", WG[:], I["ssm_w_glu"].rearrange("(k p) n -> p k n", p=128), wbuf=WGB)
            S.dma("pool", WOU[:], I["ssm_w_out"].rearrange("(k p) n -> p k n", p=128), wbuf=WOUB)
            for k in range(8):
                S.tt("dve", WOU[:, k, :], WOU[:, k, :], g.MOD[:, 2, :], ALU.mult, [WOUB, g.MODB], [WOUB])
            gT = g.HT
            ZT = UT
            ZTB = [Buf("ZT%d" % n) for n in range(NB)]
            c_ = 0
            for n in range(NB):
                hsl = slice(n * 512, (n + 1) * 512)
                for m in range(8):
                    pb = c_ % 2
                    sg, sgb = SG[c_ % 2], SGB[c_ % 2]
                    c_ += 1
                    for k in range(8):
                        S.mm(g.PS[pb][:], WG[:, k, m * 128:(m + 1) * 128], gT[:, k, hsl], k == 0, k == 7,
                             [WGB] + g.HTB[n * 4:(n + 1) * 4], [g.PSB[pb]])
                    S.actf(sg[:], g.PS[pb][:], AF.Sigmoid, [g.PSB[pb], SMB], [sgb], bias=BGLU[:, m:m + 1], scale=1.0)
                    S.tt("dve", ZT[:, m, hsl], gT[:, m, hsl], sg[:], ALU.mult, [sgb] + g.HTB[n * 4:(n + 1) * 4] + [UTB[n]], [ZTB[n], UTB[n]])
            for t in range(NT):
                pbs = (2 + 2 * (t % 2), 3 + 2 * (t % 2))
                for hh in range(2):
                    for k in range(8):
                        S.mm(g.PS[pbs[hh]][:], ZT[:, k, t * 128:(t + 1) * 128], WOU[:, k, hh * 512:(hh + 1) * 512], k == 0, k == 7,
                             [ZTB[t // 4], WOUB], [g.PSB[pbs[hh]]])
                mixer_epilogue_tile(g, t, pbs)
            ln_flush(g)
            S.barrier()
            S.emit()
            S.release([WGB, WOUB, g.LNGB, g.LNBB])


def _prep_inputs(inp, b):
    f = np.float32
    m = {}
    m["x"] = np.ascontiguousarray(inp["x"][b], dtype=f)
    m["c"] = np.ascontiguousarray(inp["c"][b].reshape(8, 128).T, dtype=f)
    m["pos"] = np.ascontiguousarray(np.broadcast_to(inp["positions"][b][None, :], (64, L)), dtype=np.int32)
    p = np.arange(64) % 32
    m["invf"] = (10000.0 ** (-(2.0 * p) / 64.0)).astype(f).reshape(64, 1)
    w_in = inp["mla_w_in"][0]
    kpe = w_in[:, 384:448]
    m["mla_w_in"] = np.ascontiguousarray(np.concatenate([w_in, kpe[:, 32:], kpe[:, :32]], axis=1), dtype=f)
    m["mla_qn"] = np.ascontiguousarray(inp["mla_q_norm"][0].reshape(2, 128).T, dtype=f)
    wq = inp["mla_w_qb"][0].reshape(256, 8, 192)
    wq_ext = np.zeros((256, 8, 320), f)
    wq_ext[:, :, 0:192] = wq
    wq_ext[:, :, 192:224] = wq[:, :, 160:192]
    wq_ext[:, :, 224:256] = wq[:, :, 128:160]
    m["mla_w_qb"] = wq_ext.reshape(256, 2560)
    m["mla_kvn"] = np.ascontiguousarray(inp["mla_kv_norm"][0].reshape(128, 1), dtype=f)
    wkv = inp["mla_w_kvb"][0].reshape(128, 8, 256)
    m["mla_w_kvb"] = np.ascontiguousarray(np.concatenate([wkv[:, :, :128].reshape(128, 1024), wkv[:, :, 128:].reshape(128, 1024)], axis=1), dtype=f)
    m["mla_w_o"] = np.ascontiguousarray(inp["mla_w_o"][0], dtype=f)
    m["ssm_w_in"] = np.ascontiguousarray(inp["ssm_w_in"][0], dtype=f)

    def sp(a):
        return np.ascontiguousarray(a.reshape(32, 2, 64).transpose(1, 2, 0).reshape(128, 32), dtype=f)
    m["ssm_ldt"] = sp(np.repeat(inp["ssm_log_dt"][0][:, None], 64, axis=1))
    m["ssm_lr"] = sp(inp["ssm_a_re"][0])
    m["ssm_li"] = sp(inp["ssm_a_im"][0])

    def bd(a):
        o = np.zeros((2, 64, 32, 2, 16), f)
        a4 = a.reshape(32, 2, 64, 16)
        for gl in range(2):
            o[gl, :, :, gl, :] = a4[:, gl].transpose(1, 0, 2)
        return o.reshape(128, 32 * 32)
    m["ssm_bre"] = bd(inp["ssm_b_re"][0])
    m["ssm_bim"] = bd(inp["ssm_b_im"][0])
    m["ssm_cre"] = bd(inp["ssm_c_re"][0].transpose(0, 2, 1))
    m["ssm_cim"] = bd(inp["ssm_c_im"][0].transpose(0, 2, 1))
    m["ssm_d"] = np.ascontiguousarray(inp["ssm_d"][0].reshape(8, 128).T, dtype=f)
    m["ssm_drep"] = np.ascontiguousarray(np.tile(inp["ssm_d"][0].reshape(32, 32).T, (4, 1)), dtype=f)
    m["ssm_w_glu"] = np.ascontiguousarray(inp["ssm_w_glu"][0], dtype=f)
    m["ssm_bglu"] = np.ascontiguousarray(inp["ssm_b_glu"][0].reshape(8, 128).T, dtype=f)
    m["ssm_w_out"] = np.ascontiguousarray(inp["ssm_w_out"][0], dtype=f)
    m["mlp_w1"] = np.ascontiguousarray(inp["mlp_w1"], dtype=f)
    m["mlp_b1"] = np.ascontiguousarray(inp["mlp_b1"].reshape(2, 32, 128).transpose(0, 2, 1), dtype=f)
    m["mlp_w2"] = np.ascontiguousarray(inp["mlp_w2"], dtype=f)
    m["mlp_b2"] = np.ascontiguousarray(np.broadcast_to(inp["mlp_b2"][:, None, :], (2, 128, D)), dtype=f)
    mw = np.stack([inp["mod_mix_w"][0], inp["mod_ffn_w"][0], inp["mod_mix_w"][1], inp["mod_ffn_w"][1]])
    mb = np.stack([inp["mod_mix_b"][0], inp["mod_ffn_b"][0], inp["mod_mix_b"][1], inp["mod_ffn_b"][1]])
    m["mod_w"] = np.ascontiguousarray(mw, dtype=f)
    m["mod_b"] = np.ascontiguousarray(np.broadcast_to(mb[:, None, :], (4, 128, 3 * D)), dtype=f)
    lg = np.stack([inp["ln_mix_g"][0], inp["ln_ffn_g"][0], inp["ln_mix_g"][1], inp["ln_ffn_g"][1]])
    lb = np.stack([inp["ln_mix_b"][0], inp["ln_ffn_b"][0], inp["ln_mix_b"][1], inp["ln_ffn_b"][1]])
    m["ln_g"] = np.ascontiguousarray(np.broadcast_to(lg[:, None, :], (4, 128, D)), dtype=f)
    m["ln_b"] = np.ascontiguousarray(np.broadcast_to(lb[:, None, :], (4, 128, D)), dtype=f)
    return m


_NC_CACHE = {}


def run(inp, sublayers=(0, 1, 2, 3), trace=False, same_engine_sync=True):
    key = (tuple(sublayers), same_engine_sync)
    if key not in _NC_CACHE:
        _NC_CACHE[key] = build_nc(sublayers, same_engine_sync)
    nc = _NC_CACHE[key]
    shared = None
    in_maps = []
    for b in range(8):
        m = _prep_inputs(inp, b)
        if shared is None:
            shared = m
        else:
            for k in m:
                if k not in ("x", "c", "pos"):
                    m[k] = shared[k]
        in_maps.append(m)
    res = run_bass_kernel_spmd(nc, in_maps, core_ids=list(range(8)), trace=trace)
    outp = np.stack([np.asarray(r["out"], dtype=np.float32) for r in res.results], axis=0)
    return outp, res


def kernel(**inputs):
    inp = {k: np.asarray(v) for k, v in inputs.items()}
    outp, _ = run(inp)
    return outp
```
